# Optimizing a Trainium2 kernel written in Bass

```python
import math
import jax
import jax.numpy as jnp
from jax import lax
import numpy as np

D_MODEL = 4096
BATCH = 4
SEQ = 4096
DEPTH = 4

GRID_W = 64
CTX_LEN = 256
HEAD_DIM = 128
EPS = 1e-6
HY_W = D_MODEL // 2
HY_CONV = 3
HY_EMB = 33
HY_FF = 64
HY_TARGET = 1e-2
HY_FAST_PCT = 0.3
HY_SLOW_PCT = 1.5
ATT_HEADS = D_MODEL // 2 // HEAD_DIM
ATT_KV_HEADS = ATT_HEADS // 4
ATT_W = ATT_HEADS * HEAD_DIM
ATT_KV_W = ATT_KV_HEADS * HEAD_DIM
Q_BLOCK = 128
ROPE_THETA = 10000.0
LRU_W = D_MODEL // 2
LRU_BLOCKS = 16
LRU_BS = LRU_W // LRU_BLOCKS
LRU_CONV = 4
LRU_C = 8.0
NA_HEADS = D_MODEL // 2 // HEAD_DIM
NA_W = NA_HEADS * HEAD_DIM
NA_WIN_H = 8
NA_WIN_W = 16

EVEN_SPLITS = (3 * HY_W, HY_W, ATT_W, ATT_KV_W, ATT_KV_W, ATT_W)
ODD_SPLITS = (LRU_W, LRU_W, NA_W, NA_W, NA_W, NA_W)
EVEN_IN = sum(EVEN_SPLITS)
ODD_IN = sum(ODD_SPLITS)
EVEN_OUT = HY_W + ATT_W
ODD_OUT = LRU_W + NA_W

kernel_name = "hybrid_hyena_gqa_rglru_natten_flow_block"


def rmsnorm(x, g):
    x32 = x.astype(jnp.float32)
    y = x32 * lax.rsqrt(jnp.mean(x32 * x32, axis=-1, keepdims=True) + EPS)
    return (y * g.astype(jnp.float32)).astype(x.dtype)


def modulate(x, g, shift, scale):
    return rmsnorm(x, g) * (1 + scale) + shift


def split_cols(z, sizes):
    idx = [int(v) for v in np.cumsum(sizes)[:-1]]
    return jnp.split(z, idx, axis=-1)


def split_heads(t, n_heads):
    return t.reshape(t.shape[0], t.shape[1], n_heads, HEAD_DIM)


def dwconv(x, w, b, pad_lo, pad_hi):
    y = lax.conv_general_dilated(x, w[:, None, :].astype(x.dtype), window_strides=(1,),
                                 padding=[(pad_lo, pad_hi)],
                                 dimension_numbers=('NWC', 'WIO', 'NWC'),
                                 feature_group_count=x.shape[-1])
    return y + b


def rope_tables(L):
    pos = jnp.arange(L, dtype=jnp.int32)
    row = (pos // GRID_W).astype(jnp.float32)
    col = (pos % GRID_W).astype(jnp.float32)
    n = HEAD_DIM // 4
    inv = ROPE_THETA ** (-jnp.arange(n, dtype=jnp.float32) / n)
    ang = jnp.concatenate([row[:, None] * inv, col[:, None] * inv], axis=-1)
    return jnp.cos(ang), jnp.sin(ang)


def apply_rope(x, cos, sin):
    B, L, H, Dh = x.shape
    n = Dh // 4
    xr = x.astype(jnp.float32).reshape(B, L, H, 2, 2, n)
    c = cos.reshape(L, 1, 2, n)
    s = sin.reshape(L, 1, 2, n)
    x1 = xr[..., 0, :]
    x2 = xr[..., 1, :]
    out = jnp.stack([x1 * c - x2 * s, x1 * s + x2 * c], axis=-2)
    return out.reshape(B, L, H, Dh).astype(x.dtype)


def gqa_attend(q, k, v):
    B, Lq, H, Dh = q.shape
    kvh = k.shape[2]
    qg = q.reshape(B, Lq, kvh, H // kvh, Dh)
    s = jnp.einsum('bqkgd,btkd->bkgqt', qg, k, preferred_element_type=jnp.float32) * (Dh ** -0.5)
    p = jax.nn.softmax(s, axis=-1).astype(v.dtype)
    o = jnp.einsum('bkgqt,btkd->bqkgd', p, v)
    return o.reshape(B, Lq, H * Dh)


def blocked_attention(q, k, v):
    B, S, H, Dh = q.shape
    nblk = S // Q_BLOCK
    qb = q.reshape(B, nblk, Q_BLOCK, H, Dh).swapaxes(0, 1)
    o = lax.map(lambda qi: gqa_attend(qi, k, v), qb)
    return o.swapaxes(0, 1).reshape(B, S, H * Dh)


def hyena_filter(L, w1, b1, freq, w2, b2, w3):
    f32 = jnp.float32
    t = jnp.linspace(0.0, 1.0, L, dtype=f32)[:, None]
    bands = (HY_EMB - 1) // 2
    w = 2.0 * math.pi * jnp.arange(L, dtype=f32)[:, None] / L
    fr = jnp.linspace(1e-4, bands - 1, bands, dtype=f32)[None, :]
    feats = jnp.concatenate([t, jnp.cos(fr * w), -jnp.sin(fr * w)], axis=-1)
    om = freq.astype(f32)
    hdn = jnp.sin(om * (feats @ w1.astype(f32) + b1.astype(f32)))
    hdn = jnp.sin(om * (hdn @ w2.astype(f32) + b2.astype(f32)))
    h = hdn @ w3.astype(f32)
    deltas = jnp.linspace(abs(math.log(HY_TARGET) / HY_SLOW_PCT),
                          abs(math.log(HY_TARGET) / HY_FAST_PCT), HY_W, dtype=f32)
    decay = jnp.exp(-t * deltas[None, :])
    h_fwd = h[:, :HY_W] * decay
    h_bwd = h[:, HY_W:] * decay
    kern = jnp.concatenate([h_fwd, jnp.zeros((1, HY_W), f32), jnp.flip(h_bwd[1:], axis=0)], axis=0)
    return kern / jnp.sum(jnp.abs(kern), axis=0, keepdims=True)


def hyena_branch(u, conv_w, conv_b, w1, b1, freq, w2, b2, w3, skip):
    L = u.shape[1]
    u = dwconv(u, conv_w, conv_b, HY_CONV // 2, HY_CONV // 2)
    x0, x1, v = jnp.split(u, 3, axis=-1)
    kern = hyena_filter(L, w1, b1, freq, w2, b2, w3)
    s = (x1 * v).astype(jnp.float32)
    y = jnp.fft.irfft(jnp.fft.rfft(s, n=2 * L, axis=1) * jnp.fft.rfft(kern, axis=0)[None],
                      n=2 * L, axis=1)[:, :L]
    y = y + s * skip.astype(jnp.float32)
    return (x0.astype(jnp.float32) * y).astype(u.dtype)


def blockdiag(x, w, b):
    B, L, C = x.shape
    y = jnp.einsum('blnd,nde->blne', x.reshape(B, L, LRU_BLOCKS, LRU_BS), w).reshape(B, L, C)
    return y + b


def lru_coeffs(xr, wa, ba, wx, bx, lam):
    r = jax.nn.sigmoid(blockdiag(xr, wa, ba).astype(jnp.float32))
    i = jax.nn.sigmoid(blockdiag(xr, wx, bx).astype(jnp.float32))
    log_a = -LRU_C * r * jax.nn.softplus(-lam.astype(jnp.float32))
    a = jnp.exp(log_a)
    b = jnp.sqrt(-jnp.expm1(2.0 * log_a)) * (i * xr.astype(jnp.float32))
    return a, b


def linear_scan(a, b, h0):
    b = b.at[:, 0].add(a[:, 0] * h0)

    def comb(c1, c2):
        return c1[0] * c2[0], c2[0] * c1[1] + c2[1]

    _, h = lax.associative_scan(comb, (a, b), axis=1)
    return h


def rglru_branch(u_lat, u_ctx, conv_w, conv_b, wa, ba, wx, bx, lam, need_ctx):
    xl = dwconv(u_lat, conv_w, conv_b, 2, 1)
    xc = dwconv(u_ctx, conv_w, conv_b, 2, 1)
    y_lat = None
    y_ctx = None
    for d in range(2):
        al, bl = lru_coeffs(xl, wa[d], ba[d], wx[d], bx[d], lam[d])
        ac, bc = lru_coeffs(xc, wa[d], ba[d], wx[d], bx[d], lam[d])
        if d == 1:
            al, bl, ac, bc = (jnp.flip(t, axis=1) for t in (al, bl, ac, bc))
        hc = linear_scan(ac, bc, jnp.zeros_like(bc[:, 0]))
        hl = linear_scan(al, bl, hc[:, -1])
        if d == 1:
            hl = jnp.flip(hl, axis=1)
            hc = jnp.flip(hc, axis=1)
        y_lat = hl if y_lat is None else y_lat + hl
        y_ctx = hc if y_ctx is None else y_ctx + hc
    y_lat = y_lat.astype(u_lat.dtype)
    y_ctx = y_ctx.astype(u_ctx.dtype) if need_ctx else None
    return y_lat, y_ctx


def na_branch(q, k, v, kc, vc, rpb):
    B, S, H, Dh = q.shape
    rows = S // GRID_W
    wh = min(NA_WIN_H, rows)
    r = jnp.arange(rows, dtype=jnp.int32)
    cc = jnp.arange(GRID_W, dtype=jnp.int32)
    r0 = jnp.clip(r - wh // 2, 0, rows - wh)
    row_idx = r0[:, None] + jnp.arange(wh, dtype=jnp.int32)[None, :]
    row_off = row_idx - r[:, None] + (NA_WIN_H - 1)
    c0 = jnp.clip(cc - NA_WIN_W // 2, 0, GRID_W - NA_WIN_W)
    col_idx = c0[:, None] + jnp.arange(NA_WIN_W, dtype=jnp.int32)[None, :]
    col_off = col_idx - cc[:, None] + (NA_WIN_W - 1)
    rpb_cols = rpb[:, :, col_off]
    kg = k.reshape(B, rows, GRID_W, H, Dh)
    vg = v.reshape(B, rows, GRID_W, H, Dh)
    qg = q.reshape(B, rows, GRID_W, H, Dh).swapaxes(0, 1)
    scale = Dh ** -0.5
    n_win = wh * NA_WIN_W

    def row_block(args):
        q_r, ridx, roff = args
        k_win = jnp.take(jnp.take(kg, ridx, axis=1), col_idx, axis=2)
        v_win = jnp.take(jnp.take(vg, ridx, axis=1), col_idx, axis=2)
        bias = rpb_cols[:, roff].swapaxes(1, 2).astype(jnp.float32)
        s_win = jnp.einsum('bqhd,bwqvhd->bhqwv', q_r, k_win,
                           preferred_element_type=jnp.float32) * scale + bias[None]
        s_ctx = jnp.einsum('bqhd,bchd->bhqc', q_r, kc, preferred_element_type=jnp.float32) * scale
        s = jnp.concatenate([s_win.reshape(B, H, GRID_W, n_win), s_ctx], axis=-1)
        p = jax.nn.softmax(s, axis=-1)
        p_win = p[..., :n_win].reshape(B, H, GRID_W, wh, NA_WIN_W).astype(v.dtype)
        p_ctx = p[..., n_win:].astype(v.dtype)
        return (jnp.einsum('bhqwv,bwqvhd->bqhd', p_win, v_win)
                + jnp.einsum('bhqc,bchd->bqhd', p_ctx, vc))

    o = lax.map(row_block, (qg, row_idx, row_off))
    return o.swapaxes(0, 1).reshape(B, S, H * Dh)


def even_layer(h, hc, w_in, w_out, conv_w, conv_b, w1, b1, freq, w2, b2, w3, skip, q_g, k_g, need_ctx):
    hy_u, hy_g, q, k, v, att_g = split_cols(h @ w_in, EVEN_SPLITS)
    hy_uc, hy_gc, qc, kc, vc, att_gc = split_cols(hc @ w_in, EVEN_SPLITS)
    S = h.shape[1]
    y_hy = hyena_branch(hy_u, conv_w, conv_b, w1, b1, freq, w2, b2, w3, skip)
    cos, sin = rope_tables(S)
    q = apply_rope(rmsnorm(split_heads(q, ATT_HEADS), q_g), cos, sin)
    k = apply_rope(rmsnorm(split_heads(k, ATT_KV_HEADS), k_g), cos, sin)
    v = split_heads(v, ATT_KV_HEADS)
    kc = rmsnorm(split_heads(kc, ATT_KV_HEADS), k_g)
    vc = split_heads(vc, ATT_KV_HEADS)
    y_att = blocked_attention(q, jnp.concatenate([k, kc], axis=1), jnp.concatenate([v, vc], axis=1))
    out = jnp.concatenate([y_hy * jax.nn.silu(hy_g), y_att * jax.nn.silu(att_g)], axis=-1) @ w_out
    out_c = None
    if need_ctx:
        y_hyc = hyena_branch(hy_uc, conv_w, conv_b, w1, b1, freq, w2, b2, w3, skip)
        qc = rmsnorm(split_heads(qc, ATT_HEADS), q_g)
        y_attc = gqa_attend(qc, kc, vc)
        out_c = jnp.concatenate([y_hyc * jax.nn.silu(hy_gc), y_attc * jax.nn.silu(att_gc)], axis=-1) @ w_out
    return out, out_c


def odd_layer(h, hc, w_in, w_out, conv_w, conv_b, wa, ba, wx, bx, lam, rpb, need_ctx):
    lru_u, lru_g, q, k, v, na_g = split_cols(h @ w_in, ODD_SPLITS)
    lru_uc, lru_gc, qc, kc, vc, na_gc = split_cols(hc @ w_in, ODD_SPLITS)
    y_lru, y_lruc = rglru_branch(lru_u, lru_uc, conv_w, conv_b, wa, ba, wx, bx, lam, need_ctx)
    kc = split_heads(kc, NA_HEADS)
    vc = split_heads(vc, NA_HEADS)
    y_na = na_branch(split_heads(q, NA_HEADS), split_heads(k, NA_HEADS), split_heads(v, NA_HEADS), kc, vc, rpb)
    out = jnp.concatenate([y_lru * jax.nn.silu(lru_g), y_na * jax.nn.silu(na_g)], axis=-1) @ w_out
    out_c = None
    if need_ctx:
        y_nac = gqa_attend(split_heads(qc, NA_HEADS), kc, vc)
        out_c = jnp.concatenate([y_lruc * jax.nn.silu(lru_gc), y_nac * jax.nn.silu(na_gc)], axis=-1) @ w_out
    return out, out_c


def setup_inputs(seed: int = 0) -> dict:
    key = jax.random.key(seed)
    ks = jax.random.split(key, 32)
    f32 = jnp.float32
    n_ev = (DEPTH + 1) // 2
    n_od = DEPTH // 2

    def nrm(k, shape, fan):
        return jax.random.normal(k, shape, f32) * (fan ** -0.5)

    def small(k, shape, s=0.02):
        return jax.random.normal(k, shape, f32) * s

    a_c = jax.random.uniform(ks[30], (n_od, 2, LRU_W), f32, minval=0.9, maxval=0.999)
    a_base = a_c ** (1.0 / LRU_C)
    lru_lambda = jnp.log(a_base) - jnp.log1p(-a_base)
    return {
        "x": jax.random.normal(ks[0], (BATCH, SEQ, D_MODEL), f32),
        "c": jax.random.normal(ks[1], (BATCH, D_MODEL), f32),
        "ctx": jax.random.normal(ks[2], (BATCH, CTX_LEN, D_MODEL), f32),
        "c_ctx": jax.random.normal(ks[3], (D_MODEL,), f32),
        "w_mod": nrm(ks[4], (DEPTH, D_MODEL, 3 * D_MODEL), D_MODEL) * 0.5,
        "b_mod": small(ks[5], (DEPTH, 3 * D_MODEL)),
        "norm_g": 1.0 + small(ks[6], (DEPTH, D_MODEL), 0.05),
        "final_g": 1.0 + small(ks[7], (D_MODEL,), 0.05),
        "ev_w_in": nrm(ks[8], (n_ev, D_MODEL, EVEN_IN), D_MODEL),
        "ev_w_out": nrm(ks[9], (n_ev, EVEN_OUT, D_MODEL), EVEN_OUT),
        "hy_conv_w": nrm(ks[10], (n_ev, HY_CONV, 3 * HY_W), HY_CONV),
        "hy_conv_b": small(ks[11], (n_ev, 3 * HY_W)),
        "hy_w1": nrm(ks[12], (n_ev, HY_EMB, HY_FF), HY_EMB),
        "hy_b1": small(ks[13], (n_ev, HY_FF)),
        "hy_freq": 1.0 + small(ks[14], (n_ev, HY_FF), 0.05),
        "hy_w2": nrm(ks[15], (n_ev, HY_FF, HY_FF), HY_FF),
        "hy_b2": small(ks[16], (n_ev, HY_FF)),
        "hy_w3": nrm(ks[17], (n_ev, HY_FF, 2 * HY_W), HY_FF),
        "hy_skip": jax.random.normal(ks[18], (n_ev, HY_W), f32),
        "att_q_g": 1.0 + small(ks[19], (n_ev, HEAD_DIM), 0.05),
        "att_k_g": 1.0 + small(ks[20], (n_ev, HEAD_DIM), 0.05),
        "od_w_in": nrm(ks[21], (n_od, D_MODEL, ODD_IN), D_MODEL),
        "od_w_out": nrm(ks[22], (n_od, ODD_OUT, D_MODEL), ODD_OUT),
        "lru_conv_w": nrm(ks[23], (n_od, LRU_CONV, LRU_W), LRU_CONV),
        "lru_conv_b": small(ks[24], (n_od, LRU_W)),
        "lru_wa": nrm(ks[25], (n_od, 2, LRU_BLOCKS, LRU_BS, LRU_BS), LRU_BS),
        "lru_ba": small(ks[26], (n_od, 2, LRU_W)),
        "lru_wx": nrm(ks[27], (n_od, 2, LRU_BLOCKS, LRU_BS, LRU_BS), LRU_BS),
        "lru_bx": small(ks[28], (n_od, 2, LRU_W)),
        "lru_lambda": lru_lambda,
        "na_rpb": small(ks[29], (n_od, NA_HEADS, 2 * NA_WIN_H - 1, 2 * NA_WIN_W - 1), 0.1),
    }


def reference(x, c, ctx, c_ctx, w_mod, b_mod, norm_g, final_g,
              ev_w_in, ev_w_out, hy_conv_w, hy_conv_b, hy_w1, hy_b1, hy_freq, hy_w2, hy_b2, hy_w3,
              hy_skip, att_q_g, att_k_g,
              od_w_in, od_w_out, lru_conv_w, lru_conv_b, lru_wa, lru_ba, lru_wx, lru_bx,
              lru_lambda, na_rpb):
    sc = jax.nn.silu(c)
    sc_ctx = jax.nn.silu(c_ctx)
    for i in range(DEPTH):
        need_ctx = i < DEPTH - 1
        mod = sc @ w_mod[i] + b_mod[i]
        shift, scale, gate = jnp.split(mod[:, None, :], 3, axis=-1)
        mod_c = sc_ctx @ w_mod[i] + b_mod[i]
        shift_c, scale_c, gate_c = jnp.split(mod_c[None, None, :], 3, axis=-1)
        h = modulate(x, norm_g[i], shift, scale)
        hc = modulate(ctx, norm_g[i], shift_c, scale_c)
        j = i // 2
        if i % 2 == 0:
            out, out_c = even_layer(h, hc, ev_w_in[j], ev_w_out[j], hy_conv_w[j], hy_conv_b[j],
                                    hy_w1[j], hy_b1[j], hy_freq[j], hy_w2[j], hy_b2[j], hy_w3[j],
                                    hy_skip[j], att_q_g[j], att_k_g[j], need_ctx)
        else:
            out, out_c = odd_layer(h, hc, od_w_in[j], od_w_out[j], lru_conv_w[j], lru_conv_b[j],
                                   lru_wa[j], lru_ba[j], lru_wx[j], lru_bx[j], lru_lambda[j],
                                   na_rpb[j], need_ctx)
        x = x + gate * out
        if need_ctx:
            ctx = ctx + gate_c * out_c
    return rmsnorm(x, final_g)
```

```python
import math
from contextlib import ExitStack
import numpy as np
import ml_dtypes
import concourse.bass as bass
import concourse.mybir as mybir
from concourse.bass_utils import run_bass_kernel_spmd

F32 = mybir.dt.float32
BF16 = mybir.dt.bfloat16
I32 = mybir.dt.int32
ACT = mybir.ActivationFunctionType
ALU = mybir.AluOpType
AX = mybir.AxisListType

D = 4096
L = 4096
LC = 256
NT = L + LC
DB = 32
EPS = 1e-6
NCORES = 4
NMB = 96 // NCORES
TWO_PI = 2.0 * math.pi


class Buf:
    __slots__ = ("name", "lw", "rd", "excl")

    def __init__(self, name="b", excl=False):
        self.name = name
        self.lw = None
        self.rd = {}
        self.excl = excl


def PBuf():
    return Buf("psum", True)


class Sched:
    NDMA = 12

    def __init__(self, nc, stack):
        self.nc = nc
        self.eng = {"pe": nc.tensor, "act": nc.scalar, "dve": nc.vector, "pool": nc.gpsimd, "sp": nc.sync}
        self.sem = {k: stack.enter_context(nc.semaphore("c_" + k)) for k in self.eng}
        self.cnt = {k: 0 for k in self.eng}
        self.dsem = {k: [stack.enter_context(nc.semaphore(f"d_{k}{i}")) for i in range(self.NDMA)]
                     for k in ("sp", "act", "pool")}
        self.dcnt = {k: 0 for k in self.dsem}
        self.ccsem = stack.enter_context(nc.semaphore("ccsem"))
        self.cccnt = 0
        self.waited = {k: {} for k in self.eng}
        self.semobj = {}
        for k in self.eng:
            self.semobj[("c", k)] = self.sem[k]
        for k in self.dsem:
            for i in range(self.NDMA):
                self.semobj[("d", k, i)] = self.dsem[k][i]
        self.semobj[("cc",)] = self.ccsem
        self.nwait = 0
        self.nins = 0

    def _wait(self, e, tok):
        if tok is None:
            return
        key, val = tok
        if key == ("c", "pe") and e == "pe":
            return
        if self.waited[e].get(key, 0) >= val:
            return
        self.eng[e].wait_ge(self.semobj[key], val)
        self.waited[e][key] = val
        self.nwait += 1

    def _deps(self, e, reads, writes):
        for b in reads:
            self._wait(e, b.lw)
        for b in writes:
            self._wait(e, b.lw)
            for t in list(b.rd.items()):
                self._wait(e, t)

    def _commit(self, tok, reads, writes):
        for b in reads:
            if b.rd.get(tok[0], 0) < tok[1]:
                b.rd[tok[0]] = tok[1]
        for b in writes:
            b.lw = tok
            b.rd = {}

    def op(self, e, fn, reads=(), writes=()):
        if any(b.excl for b in reads):
            writes = list(writes) + [b for b in reads if b.excl]
            reads = [b for b in reads if not b.excl]
        self._deps(e, reads, writes)
        self.cnt[e] += 1
        fn(self.eng[e]).then_inc(self.sem[e], 1)
        tok = (("c", e), self.cnt[e])
        self._commit(tok, reads, writes)
        self.nins += 1
        return tok

    def dma(self, e, out, in_, reads=(), writes=(), **kw):
        k = self.dcnt[e]
        slot = k % self.NDMA
        rnd = k // self.NDMA
        key = ("d", e, slot)
        if rnd > 0:
            self._wait(e, (key, 16 * rnd))
        self._deps(e, reads, writes)
        self.dcnt[e] += 1
        self.eng[e].dma_start(out=out, in_=in_, **kw).then_inc(self.dsem[e][slot], 16)
        tok = (key, 16 * (rnd + 1))
        self._commit(tok, reads, writes)
        self.nins += 1
        return tok

    def collective(self, kind, op, groups, in_ap, out_ap, reads=(), writes=()):
        e = "pool"
        self._deps(e, reads, writes)
        self.cccnt += 1
        self.eng[e].collective_compute(kind, op, replica_groups=groups, ins=[in_ap], outs=[out_ap]).then_inc(self.ccsem)
        tok = (("cc",), self.cccnt)
        self._commit(tok, reads, writes)
        return tok

    def all_tokens(self):
        toks = [(("c", k), self.cnt[k]) for k in self.eng if self.cnt[k] > 0]
        for e in self.dsem:
            k = self.dcnt[e]
            for slot in range(self.NDMA):
                n = (k - slot + self.NDMA - 1) // self.NDMA
                if n > 0:
                    toks.append((("d", e, slot), 16 * n))
        if self.cccnt:
            toks.append((("cc",), self.cccnt))
        return toks

    def barrier(self, engines=None):
        toks = self.all_tokens()
        for e in (engines or self.eng):
            for t in toks:
                if t[0] == ("c", "pe") and e == "pe":
                    continue
                self._wait(e, t)


class Phase:
    _uid = [0]

    def __init__(self, nc, S):
        self.nc = nc
        self.S = S
        self.st = ExitStack()

    def _nm(self, name):
        Phase._uid[0] += 1
        return f"{name}_{Phase._uid[0]}"

    def sb(self, name, shape, dt=F32):
        return self.st.enter_context(self.nc.sbuf_tensor(self._nm(name), shape, dt))

    def ps(self, name, shape=(128, 512), dt=F32):
        return self.st.enter_context(self.nc.psum_tensor(self._nm(name), list(shape), dt))

    def close(self):
        self.S.barrier()
        self.st.close()


class Rot:
    def __init__(self, items):
        self.items = items
        self.i = 0

    def next(self):
        it = self.items[self.i % len(self.items)]
        self.i += 1
        return it


def even_fm_cols(h):
    blocks = []
    for base in (0, 2048, 4096, 6144):
        blocks += [base + h * 1024 + i * 128 for i in range(8)]
    blocks += [8192 + h * 1024 + i * 128 for i in range(8)]
    blocks += [10240 + h * 256 + i * 128 for i in range(2)]
    blocks += [11264 + h * 1024 + i * 128 for i in range(8)]
    return blocks


def even_tm_cols(h):
    return 10752 + h * 256, 256


def odd_fm_cols(h):
    blocks = []
    for base in (0, 2048, 4096, 6144, 10240):
        blocks += [base + h * 1024 + i * 128 for i in range(8)]
    return blocks


def odd_tm_cols(h):
    return 8192 + h * 1024, 1024


NCB_EVEN = 50
NCB_ODD = 40


def build(cfg):
    nl = cfg.get("nl", 4)
    stop = cfg.get("stop", None)
    dbg = cfg.get("dbg", ())
    nc = bass.Bass("TRN2", target_bir_lowering=False)

    def din(name, shape, dt=F32):
        return nc.dram_tensor(name, list(shape), dt, kind="ExternalInput").ap()

    def dscr(name, shape, dt=F32):
        return nc.dram_tensor(name, list(shape), dt).ap()

    def dout(name, shape, dt=F32):
        return nc.dram_tensor(name, list(shape), dt, kind="ExternalOutput").ap()

    mixtest = cfg.get("mixtest", None)
    tailtest = cfg.get("tailtest", False)
    gn = din("gn", [128, 4, DB])
    gfin = din("gfin", [128, DB])
    if tailtest:
        xT_in = din("xT", [DB, 128, NT])
        modv_in = din("modv_in", [128, 2, 4, 96])
        wout_t = din("wout0", [DB, 128, 16, 128])
    elif mixtest is None:
        xT_in = din("xT", [DB, 128, NT])
        scin = din("scin", [128, DB, 5])
        onehot = din("onehot", [128, 5])
        wmod = din("wmod", [4, 128, DB, NMB * 128])
        bmod = din("bmod", [128, 4, NMB])
    wfm, wtm, wout = [], [], []
    for l in range(nl if (mixtest is None and not tailtest) else 0):
        ncb = NCB_EVEN if l % 2 == 0 else NCB_ODD
        vc = 256 if l % 2 == 0 else 1024
        wfm.append([din(f"wfm{l}_{hh}", [ncb, 128, DB, 128]) for hh in range(2)])
        wtm.append([din(f"wtm{l}_{hh}", [128, DB, vc]) for hh in range(2)])
        wout.append([din(f"wout{l}_{hh}", [DB, 128, 16, 128]) for hh in range(2)])
    n_ev = (nl + 1) // 2
    n_od = nl // 2
    ev_js = range(n_ev) if mixtest is None else ([mixtest // 2] if mixtest % 2 == 0 else [])
    od_js = range(n_od) if mixtest is None else ([mixtest // 2] if mixtest % 2 == 1 else [])
    if tailtest:
        ev_js, od_js = [], []
        wout = [[wout_t, wout_t]]
    hy = {}
    for j in ev_js:
        w1_ = din(f"hyw1{j}", [33, 64]); b1_ = din(f"hyb1{j}", [64, 1]); fr_ = din(f"hyfr{j}", [64, 1])
        w2_ = din(f"hyw2{j}", [64, 64]); b2_ = din(f"hyb2{j}", [64, 1])
        qg_ = din(f"attqg{j}", [128, 1]); kg_ = din(f"attkg{j}", [128, 1])
        for hh in range(2):
            hy[(j, hh)] = (dict(
                cw=din(f"hycw{j}_{hh}", [128, 3, 8, 3]), cb=din(f"hycb{j}_{hh}", [128, 3, 8]), skip=din(f"hyskip{j}_{hh}", [128, 8]),
                w1=w1_, b1=b1_, fr=fr_, w2=w2_, b2=b2_, w3=din(f"hyw3{j}_{hh}", [64, 2, 1024]), qg=qg_, kg=kg_))
    lru = {}
    for j in od_js:
        for hh in range(2):
            lru[(j, hh)] = (dict(
                cw=din(f"lrucw{j}_{hh}", [128, 8, 4]), cb=din(f"lrucb{j}_{hh}", [128, 8]),
                wa=din(f"lruwa{j}_{hh}", [128, 2, 8, 128]), ba=din(f"lruba{j}_{hh}", [128, 2, 8]),
                wx=din(f"lruwx{j}_{hh}", [128, 2, 8, 128]), bx=din(f"lrubx{j}_{hh}", [128, 2, 8]),
                lam=din(f"lrulam{j}_{hh}", [128, 2, 8]), nab=din(f"nab{j}_{hh}", [8, 20, 128, 512])))
    if not tailtest:
        featsL = din("featsL", [33, L])
        featsC = din("featsC", [33, LC])
        tnormL = din("tnormL", [128, L])
        tnormC = din("tnormC", [128, LC])
        negdelta = [din(f"negdelta_{hh}", [128, 8]) for hh in range(2)]
        CtabL = din("CtabL", [32, 128, L], BF16)
        StabL = din("StabL", [32, 128, L], BF16)
        CtabC = din("CtabC", [2, 128, LC], BF16)
        StabC = din("StabC", [2, 128, LC], BF16)
        phiL = din("phiL", [128, 3, 32])
        phiC = din("phiC", [128, 3, 2])
        ropeC = din("ropeC", [128, L])
        ropeS = din("ropeS", [128, L])
        protT = din("protT", [128, 128])
    else:
        featsL = featsC = tnormL = tnormC = negdelta = CtabL = StabL = CtabC = StabC = phiL = phiC = ropeC = ropeS = protT = None

    outT = dout("outT", [DB, 128, L])
    XT = dscr("XT", [DB, 128, NT])
    HT = dscr("HT", [17, 128, DB, 256], BF16)
    if mixtest is None:
        UF = dscr("UF", [NCB_EVEN, 128, NT])
        VT = dscr("VT", [34, 128, 1024], BF16)
    else:
        UF = din("UF_in", [NCB_EVEN, 128, NT])
        VT = din("VT_in", [34, 128, 1024], BF16)
    YT = din("YT_in", [16, 128, NT], BF16) if tailtest else dscr("YT", [16, 128, NT], BF16)
    PT = dscr("PT", [DB * 128, NT])
    RT = dscr("RT", [DB * 128, NT])
    PT3 = PT.rearrange("(a p) t -> a p t", p=128)
    RT3 = RT.rearrange("(a p) t -> a p t", p=128)
    SD = dscr("SD", [8, 128, NT])
    X0G = dscr("X0G", [8, 128, NT])
    KAL = dscr("KAL", [32, 128, 1024])
    KBL = dscr("KBL", [32, 128, 1024])
    KAC = dscr("KAC", [2, 128, 1024])
    KBC = dscr("KBC", [2, 128, 1024])
    modloc_d = dscr("modloc_d", [128, 4 * NMB * 5])
    modall_d = dscr("modall_d", [NCORES * 128, 4 * NMB * 5])
    dbg_out = {}
    if "HT" in dbg:
        dbg_out["HT"] = dout("dbg_HT", [17, 128, DB, 256], BF16)
    if "UF" in dbg:
        dbg_out["UF"] = dout("dbg_UF", [NCB_EVEN, 128, NT])
    if "VT" in dbg:
        dbg_out["VT"] = dout("dbg_VT", [34, 128, 1024], BF16)
    if "YT" in dbg:
        dbg_out["YT"] = dout("dbg_YT", [16, 128, NT], BF16)
    if "PT" in dbg:
        dbg_out["PT"] = dout("dbg_PT", [DB * 128, NT])
    if "RT" in dbg:
        dbg_out["RT"] = dout("dbg_RT", [DB * 128, NT])
    if "XT" in dbg:
        dbg_out["XT"] = dout("dbg_XT", [DB, 128, NT])
    if "MOD" in dbg:
        dbg_out["MOD"] = dout("dbg_MOD", [128, 2, 4, 96])
    if "KA" in dbg:
        dbg_out["KA"] = dout("dbg_KA", [32, 128, 1024])
        dbg_out["KB"] = dout("dbg_KB", [32, 128, 1024])
        dbg_out["KAC"] = dout("dbg_KAC", [2, 128, 1024])
    if "SD" in dbg:
        dbg_out["SD"] = dout("dbg_SD", [8, 128, NT])

    bXT = [Buf(f"XT{i}") for i in range(17)]
    bHT = [Buf(f"HT{i}") for i in range(17)]
    bUF = [Buf(f"UF{i}") for i in range(NCB_EVEN)]
    bVT = Buf("VT")
    bYT = [Buf(f"YT{i}") for i in range(16)]
    bPT = [Buf(f"PT{i}") for i in range(4)]
    bPTt = [[Buf() for _ in range(9)] for _ in range(DB)]
    bRT = [Buf(f"RT{i}") for i in range(4)]
    bSD = [Buf(f"SD{i}") for i in range(8)]
    bX0G = [Buf(f"X0G{i}") for i in range(8)]
    bK = Buf("K")
    bmodd = Buf("modd")

    stack = ExitStack()
    S = Sched(nc, stack)
    blk = stack.enter_context(nc.Block())

    def body(_e):
        G = Phase(nc, S)
        ones_bf = G.sb("ones_bf", [128, 128], BF16)
        ones_f = G.sb("ones_f", [128, 128])
        ident_f = G.sb("ident_f", [128, 128])
        ident_bf = G.sb("ident_bf", [128, 128], BF16)
        bconst = Buf("const")
        S.op("pool", lambda e: e.memset(ones_f[:], 1.0), writes=[bconst])
        S.op("pool", lambda e: e.memset(ident_f[:], 0.0), writes=[bconst])
        S.op("pool", lambda e: e.affine_select(out=ident_f[:], in_=ident_f[:], pattern=[[-1, 128]],
                                               compare_op=ALU.not_equal, fill=1.0, base=0, channel_multiplier=1),
             reads=[bconst], writes=[bconst])
        S.op("dve", lambda e: e.tensor_copy(out=ones_bf[:], in_=ones_f[:]), reads=[bconst], writes=[bconst])
        S.op("dve", lambda e: e.tensor_copy(out=ident_bf[:], in_=ident_f[:]), reads=[bconst], writes=[bconst])
        modv = G.sb("modv", [128, 2, 4, 96])
        gn_sb = G.sb("gn_sb", [128, 4, DB])
        gfin_sb = G.sb("gfin_sb", [128, DB])
        gs_sb = G.sb("gs_sb", [128, 2, DB])
        bmodv = Buf("modv")
        bgs = Buf("gs")
        S.dma("sp", gn_sb[:], gn[:, :, :], writes=[bconst])
        S.dma("sp", gfin_sb[:], gfin[:, :], writes=[bconst])

        def phase_mod():
            P = Phase(nc, S)
            sc = P.sb("sc", [128, DB, 5])
            oh = P.sb("oh", [128, 5])
            bm = P.sb("bm", [128, 4, NMB])
            ml = P.sb("ml", [128, 4, NMB, 5])
            ma = P.sb("ma", [128, NCORES, 4 * NMB * 5])
            tmp = P.sb("tmp", [128, NCORES, 4 * NMB])
            bsc, bml, bma, btmp = Buf(), Buf(), Buf(), Buf()
            wbufs = Rot([(P.sb(f"wm{i}", [128, DB, 384]), Buf()) for i in range(2)])
            pss = Rot([(P.ps(f"pm{i}"), PBuf()) for i in range(2)])
            S.dma("sp", sc[:], scin[:, :, :], writes=[bsc])
            S.dma("sp", oh[:], onehot[:, :], writes=[bsc])
            S.dma("sp", bm[:], bmod[:, :, :], writes=[bsc])
            S.op("act", lambda e: e.activation(out=sc[:], in_=sc[:], func=ACT.Silu), reads=[bsc], writes=[bsc])
            for l in range(4):
                for ch in range(NMB // 3):
                    w, bw = wbufs.next()
                    S.dma("sp", w[:], wmod[l, :, :, ch * 384:(ch + 1) * 384], writes=[bw])
                    for jj in range(3):
                        j = ch * 3 + jj
                        ps, bp = pss.next()
                        for db in range(DB):
                            S.op("pe", lambda e: e.matmul(ps[:, 0:5], lhsT=w[:, db, jj * 128:(jj + 1) * 128], rhs=sc[:, db, :],
                                                          start=(db == 0), stop=(db == DB - 1)),
                                 reads=[bw, bsc], writes=[bp])
                        S.op("act", lambda e: e.activation(out=ml[:, l, j, :], in_=ps[:, 0:5], func=ACT.Identity,
                                                           bias=bm[:, l, j:j + 1], scale=1.0),
                             reads=[bp, bsc], writes=[bml])
            S.dma("sp", modloc_d[:, :], ml[:].rearrange("p a b c -> p (a b c)"), reads=[bml], writes=[bmodd])
            S.collective("AllGather", ALU.bypass, [list(range(NCORES))], modloc_d[:, :], modall_d[:, :],
                         reads=[bmodd], writes=[bmodd])
            S.dma("sp", ma[:], modall_d.rearrange("(r p) f -> p r f", p=128), reads=[bmodd], writes=[bma])
            mav = ma[:].rearrange("p r (lj c) -> p r lj c", c=5)
            S.op("dve", lambda e: e.tensor_scalar(out=tmp[:], in0=mav[:, :, :, 0], scalar1=oh[:, 0:1], scalar2=None, op0=ALU.mult),
                 reads=[bma, bsc], writes=[btmp])
            for c in range(1, 4):
                S.op("dve", lambda e: e.scalar_tensor_tensor(out=tmp[:], in0=mav[:, :, :, c], scalar=oh[:, c:c + 1], in1=tmp[:],
                                                             op0=ALU.mult, op1=ALU.add),
                     reads=[bma, bsc, btmp], writes=[btmp])
            for l in range(4):
                tv = tmp[:].rearrange("p r (l j) -> p r l j", j=NMB)
                S.op("dve", lambda e: e.tensor_copy(out=modv[:, 0, l, :].rearrange("p (r j) -> p r j", j=NMB), in_=tv[:, :, l, :]),
                     reads=[btmp], writes=[bmodv])
                mv4 = ma[:].rearrange("p r (l j c) -> p r l j c", j=NMB, c=5)
                S.op("dve", lambda e: e.tensor_copy(out=modv[:, 1, l, :].rearrange("p (r j) -> p r j", j=NMB), in_=mv4[:, :, l, :, 4]),
                     reads=[bma], writes=[bmodv])
            P.close()

        if tailtest:
            S.dma("sp", modv[:], modv_in, writes=[bmodv])
            S.dma("sp", modloc_d[:, :], gn_sb[:, 0:2, :].rearrange("p a b -> p (a b)")[:, 0:64], reads=[bconst], writes=[bmodd]) if False else None
            S.collective("AllGather", ALU.bypass, [list(range(NCORES))], modloc_d[:, :], modall_d[:, :], reads=[bmodd], writes=[bmodd])
        elif mixtest is None:
            phase_mod()

        def phase_A(l):
            P = Phase(nc, S)
            xts = Rot([(P.sb(f"xt{i}", [128, DB, 256]), Buf()) for i in range(2)])
            rts = Rot([(P.sb(f"rt{i}", [128, DB, 256]), Buf()) for i in range(1)])
            sqs = Rot([(P.sb(f"sq{i}", [128, DB, 256], BF16), Buf()) for i in range(1)])
            hos = Rot([(P.sb(f"ho{i}", [128, DB, 256], BF16), Buf()) for i in range(2)])
            rss = Rot([(P.sb(f"rs{i}", [128, 256]), Buf()) for i in range(2)])
            pss = Rot([(P.ps(f"pa{i}"), PBuf()) for i in range(2)])
            for w in range(2):
                S.op("dve", lambda e: e.scalar_tensor_tensor(out=gs_sb[:, w, :], in0=modv[:, w, l, 32:64], scalar=1.0, in1=gn_sb[:, l, :],
                                                             op0=ALU.add, op1=ALU.mult),
                     reads=[bmodv, bconst], writes=[bgs])
            for tt in range(17):
                w = 1 if tt == 16 else 0
                xt, bx = xts.next()
                src = xT_in if l <= 1 else XT
                S.dma("sp", xt[:], src[:, :, tt * 256:(tt + 1) * 256].rearrange("a p t -> p a t"),
                      reads=[] if l <= 1 else [bXT[tt]], writes=[bx])
                if l > 0:
                    rt, br = rts.next()
                    S.dma("sp", rt[:], PT3[:, :, tt * 256:(tt + 1) * 256].rearrange("a p t -> p a t"), reads=bPT, writes=[br])
                    for db in range(DB):
                        S.op("dve",
                             lambda e: e.scalar_tensor_tensor(out=xt[:, db, :], in0=rt[:, db, :], scalar=modv[:, w, l - 1, 64 + db:65 + db],
                                                              in1=xt[:, db, :], op0=ALU.mult, op1=ALU.add),
                             reads=[br, bx, bmodv], writes=[bx])
                    S.dma("sp", XT[:, :, tt * 256:(tt + 1) * 256].rearrange("a p t -> p a t"), xt[:], reads=[bx], writes=[bXT[tt]])
                sq, bs = sqs.next()
                S.op("act", lambda e: e.activation(out=sq[:], in_=xt[:], func=ACT.Square), reads=[bx], writes=[bs])
                ps, bp = pss.next()
                for db in range(DB):
                    S.op("pe", lambda e: e.matmul(ps[:, 0:256], lhsT=ones_bf[:], rhs=sq[:, db, :], start=(db == 0), stop=(db == DB - 1)),
                         reads=[bs, bconst], writes=[bp])
                rs, brs = rss.next()
                S.op("act", lambda e: e.activation(out=rs[:], in_=ps[:, 0:256], func=ACT.Sqrt, scale=1.0 / D, bias=EPS), reads=[bp], writes=[brs])
                S.op("dve", lambda e: e.reciprocal(out=rs[:], in_=rs[:]), reads=[brs], writes=[brs])
                ho, bh = hos.next()
                for db in range(DB):
                    S.op("dve", lambda e: e.scalar_tensor_tensor(out=xt[:, db, :], in0=xt[:, db, :], scalar=gs_sb[:, w, db:db + 1], in1=rs[:],
                                                                 op0=ALU.mult, op1=ALU.mult),
                         reads=[bx, brs, bgs], writes=[bx])
                    S.op("act", lambda e: e.activation(out=ho[:, db, :], in_=xt[:, db, :], func=ACT.Identity,
                                                       bias=modv[:, w, l, db:db + 1], scale=1.0),
                         reads=[bx, bmodv], writes=[bh])
                S.dma("sp", HT[tt, :, :, :], ho[:], reads=[bh], writes=[bHT[tt]])
            P.close()

        def phase_B(l, hh):
            even = (l % 2 == 0)
            ncb = NCB_EVEN if even else NCB_ODD
            vcols = 256 if even else 1024
            P = Phase(nc, S)
            ht = P.sb("ht", [128, 4, DB, 256], BF16)
            bht = Buf()
            ws = Rot([(P.sb(f"w{i}", [128, DB, 128], BF16), Buf()) for i in range(3)])
            wv = P.sb("wv", [128, DB, 512], BF16)
            bwv = Buf()
            stg = Rot([(P.sb(f"stg{i}", [128, 512]), Buf()) for i in range(3)])
            stgb = Rot([(P.sb(f"stgb{i}", [128, 512], BF16), Buf()) for i in range(2)])
            pss = Rot([(P.ps(f"pb{i}"), PBuf()) for i in range(4)])
            ev = [0]
            for st in range(5):
                ntile = 4 if st < 4 else 1
                for k in range(ntile):
                    S.dma("sp", ht[:, k, :, :], HT[st * 4 + k, :, :, :], reads=[bHT[st * 4 + k]], writes=[bht])
                nhalf = 2 if st < 4 else 1
                for cb in range(ncb):
                    w, bw = ws.next()
                    S.dma("pool", w[:], wfm[l][hh][cb, :, :, :], writes=[bw])
                    for hf in range(nhalf):
                        ps, bp = pss.next()
                        if st < 4:
                            n = 512
                            rhs_of = lambda db: ht[:, 2 * hf:2 * hf + 2, db, :]
                        else:
                            n = 256
                            rhs_of = lambda db: ht[:, 0, db, :]
                        for db in range(DB):
                            S.op("pe", lambda e: e.matmul(ps[:, 0:n], lhsT=w[:, db, :], rhs=rhs_of(db), start=(db == 0), stop=(db == DB - 1)),
                                 reads=[bw, bht], writes=[bp])
                        sg, bsg = stg.next()
                        ev[0] += 1
                        if ev[0] % 2 == 0:
                            S.op("act", lambda e: e.copy(out=sg[:, 0:n], in_=ps[:, 0:n]), reads=[bp], writes=[bsg])
                        else:
                            S.op("dve", lambda e: e.tensor_copy(out=sg[:, 0:n], in_=ps[:, 0:n]), reads=[bp], writes=[bsg])
                        t0 = st * 1024 + hf * 512
                        S.dma("sp", UF[cb, :, t0:t0 + n], sg[:, 0:n], reads=[bsg], writes=[bUF[cb]])
                for vch in range(vcols // min(vcols, 512)):
                    vw = min(vcols, 512)
                    S.dma("pool", wv[:, :, 0:vw], wtm[l][hh][:, :, vch * vw:(vch + 1) * vw], writes=[bwv])
                    for k in range(ntile):
                        for sub in range(2):
                            tb = (st * 4 + k) * 2 + sub
                            ps, bp = pss.next()
                            for db in range(DB):
                                S.op("pe", lambda e: e.matmul(ps[:, 0:vw], lhsT=ht[:, k, db, sub * 128:(sub + 1) * 128], rhs=wv[:, db, 0:vw],
                                                              start=(db == 0), stop=(db == DB - 1)),
                                     reads=[bwv, bht], writes=[bp])
                            sg, bsg = stgb.next()
                            ev[0] += 1
                            if ev[0] % 2 == 0:
                                S.op("act", lambda e: e.copy(out=sg[:, 0:vw], in_=ps[:, 0:vw]), reads=[bp], writes=[bsg])
                            else:
                                S.op("dve", lambda e: e.tensor_copy(out=sg[:, 0:vw], in_=ps[:, 0:vw]), reads=[bp], writes=[bsg])
                            S.dma("sp", VT[tb, :, vch * vw:(vch + 1) * vw], sg[:, 0:vw], reads=[bsg], writes=[bVT])
            P.close()

        def phase_D(l, hh):
            need_ctx = l < 3
            P = Phase(nc, S)
            yt = P.sb("yt", [128, 16, NT], BF16)
            byt = Buf()
            ws = Rot([(P.sb(f"wo{i}", [128, 16, 128], BF16), Buf()) for i in range(3)])
            stg = Rot([(P.sb(f"stg{i}", [128, 512]), Buf()) for i in range(4)])
            pvs = Rot([(P.sb(f"pv{i}", [128, 512]), Buf()) for i in range(3)])
            pss = Rot([(P.ps(f"pd{i}"), PBuf()) for i in range(4)])
            for c in range(16):
                S.dma("sp", yt[:, c, :], YT[c, :, :], reads=[bYT[c]], writes=[byt])
            ev = 0
            for nb in range(DB):
                w, bw = ws.next()
                S.dma("pool", w[:], wout[l][hh][nb, :, :, :], writes=[bw])
                for tt in range(9 if need_ctx else 8):
                    n = 512 if tt < 8 else 256
                    ps, bp = pss.next()
                    for c in range(16):
                        S.op("pe", lambda e: e.matmul(ps[:, 0:n], lhsT=w[:, c, :], rhs=yt[:, c, tt * 512:tt * 512 + n], start=(c == 0), stop=(c == 15)),
                             reads=[bw, byt], writes=[bp])
                    sg, bsg = stg.next()
                    ev += 1
                    if hh == 1:
                        pv, bpv = pvs.next()
                        S.dma("sp", pv[:, 0:n], PT3[nb, :, tt * 512:tt * 512 + n], reads=[bPTt[nb][tt]], writes=[bpv])
                        S.op("dve", lambda e: e.tensor_tensor(out=sg[:, 0:n], in0=ps[:, 0:n], in1=pv[:, 0:n], op=ALU.add), reads=[bp, bpv], writes=[bsg])
                    elif ev % 2 == 0:
                        S.op("act", lambda e: e.copy(out=sg[:, 0:n], in_=ps[:, 0:n]), reads=[bp], writes=[bsg])
                    else:
                        S.op("dve", lambda e: e.tensor_copy(out=sg[:, 0:n], in_=ps[:, 0:n]), reads=[bp], writes=[bsg])
                    S.dma("sp", PT3[nb, :, tt * 512:tt * 512 + n], sg[:, 0:n], reads=[bsg], writes=[bPTt[nb][tt], bPT[nb // 8]])
            P.close()

        def phase_final():
            l = 4
            P = Phase(nc, S)
            xts = Rot([(P.sb(f"xt{i}", [128, DB, 256]), Buf()) for i in range(2)])
            rts = Rot([(P.sb(f"rt{i}", [128, DB, 256]), Buf()) for i in range(2)])
            sqs = Rot([(P.sb(f"sq{i}", [128, DB, 256], BF16), Buf()) for i in range(1)])
            rss = Rot([(P.sb(f"rs{i}", [128, 256]), Buf()) for i in range(2)])
            pss = Rot([(P.ps(f"pa{i}"), PBuf()) for i in range(2)])
            for tt in range(16):
                xt, bx = xts.next()
                S.dma("sp", xt[:], XT[:, :, tt * 256:(tt + 1) * 256].rearrange("a p t -> p a t"), reads=[bXT[tt]], writes=[bx])
                rt, br = rts.next()
                S.dma("sp", rt[:], PT3[:, :, tt * 256:(tt + 1) * 256].rearrange("a p t -> p a t"), reads=bPT, writes=[br])
                for db in range(DB):
                    S.op("dve",
                         lambda e: e.scalar_tensor_tensor(out=xt[:, db, :], in0=rt[:, db, :], scalar=modv[:, 0, l - 1, 64 + db:65 + db],
                                                          in1=xt[:, db, :], op0=ALU.mult, op1=ALU.add),
                         reads=[br, bx, bmodv], writes=[bx])
                sq, bs = sqs.next()
                S.op("act", lambda e: e.activation(out=sq[:], in_=xt[:], func=ACT.Square), reads=[bx], writes=[bs])
                ps, bp = pss.next()
                for db in range(DB):
                    S.op("pe", lambda e: e.matmul(ps[:, 0:256], lhsT=ones_bf[:], rhs=sq[:, db, :], start=(db == 0), stop=(db == DB - 1)),
                         reads=[bs, bconst], writes=[bp])
                rs, brs = rss.next()
                S.op("act", lambda e: e.activation(out=rs[:], in_=ps[:, 0:256], func=ACT.Sqrt, scale=1.0 / D, bias=EPS), reads=[bp], writes=[brs])
                S.op("dve", lambda e: e.reciprocal(out=rs[:], in_=rs[:]), reads=[brs], writes=[brs])
                for db in range(DB):
                    S.op("dve", lambda e: e.scalar_tensor_tensor(out=xt[:, db, :], in0=xt[:, db, :], scalar=gfin_sb[:, db:db + 1], in1=rs[:],
                                                                 op0=ALU.mult, op1=ALU.mult),
                         reads=[bx, brs, bconst], writes=[bx])
                S.dma("sp", outT[:, :, tt * 256:(tt + 1) * 256].rearrange("a p t -> p a t"), xt[:], reads=[bx], writes=[Buf()])
            P.close()

        import types
        E = types.SimpleNamespace(
            nc=nc, S=S, cfg=cfg, ones_bf=ones_bf, ones_f=ones_f, ident_f=ident_f, ident_bf=ident_bf, bconst=bconst,
            modv=modv, bmodv=bmodv, UF=UF, bUF=bUF, VT=VT, bVT=bVT, YT=YT, bYT=bYT, SD=SD, bSD=bSD, X0G=X0G, bX0G=bX0G,
            KAL=KAL, KBL=KBL, KAC=KAC, KBC=KBC, bK=bK, hy=hy, lru=lru, featsL=featsL, featsC=featsC, tnormL=tnormL,
            tnormC=tnormC, negdelta=negdelta, CtabL=CtabL, StabL=StabL, CtabC=CtabC, StabC=StabC, phiL=phiL, phiC=phiC,
            ropeC=ropeC, ropeS=ropeS, protT=protT)
        MIX = build_mixers(E)

        def dump(name, src, buf_reads):
            if name in dbg_out:
                S.dma("sp", dbg_out[name], src, reads=buf_reads)

        done = False
        if tailtest:
            phase_D(0, 0)
            phase_D(0, 1)
            phase_A(1)
            phase_final()
            done = True
            nl_loop = 0
        if mixtest is not None:
            (MIX["even"] if mixtest % 2 == 0 else MIX["odd"])(mixtest, 0)
            done = True
        for l in range(nl if (mixtest is None and not tailtest) else 0):
            phase_A(l)
            for hh in range(2):
                phase_B(l, hh)
                (MIX["even"] if l % 2 == 0 else MIX["odd"])(l, hh)
                phase_D(l, hh)
            if done:
                break
        if not done:
            phase_final()
        S.barrier()
        if "MOD" in dbg_out:
            S.dma("sp", dbg_out["MOD"], modv[:], reads=[bmodv])
        dump("HT", HT, bHT)
        dump("UF", UF, bUF)
        dump("VT", VT, [bVT])
        dump("YT", YT, bYT)
        dump("PT", PT, bPT)
        dump("RT", RT, bRT)
        dump("XT", XT, bXT)
        dump("SD", SD, bSD)
        if "KA" in dbg_out:
            S.dma("sp", dbg_out["KA"], KAL, reads=[bK])
            S.dma("sp", dbg_out["KB"], KBL, reads=[bK])
            S.dma("sp", dbg_out["KAC"], KAC, reads=[bK])
        S.barrier()
        G.st.close()

    blk.sync(body)
    stack.close()
    return nc, S


def _evac(S, k, out, in_, reads, writes):
    if k % 2 == 0:
        S.op("act", lambda e: e.copy(out=out, in_=in_), reads=reads, writes=writes)
    else:
        S.op("dve", lambda e: e.tensor_copy(out=out, in_=in_), reads=reads, writes=writes)


def build_mixers(E):
    nc, S = E.nc, E.S
    ISQ = 1.0 / math.sqrt(128.0)

    def hyena(l, hh):
        j = l // 2
        need_ctx = l < 3
        hp = E.hy[(j, hh)]
        PH = Phase(nc, S)
        cw = PH.sb("cw", [128, 3, 8, 3]); cb = PH.sb("cb", [128, 3, 8]); skip = PH.sb("skip", [128, 8]); negd = PH.sb("negd", [128, 8])
        w1 = PH.sb("w1", [33, 64]); b1 = PH.sb("b1", [64, 1]); fr = PH.sb("fr", [64, 1]); w2 = PH.sb("w2", [64, 64]); b2 = PH.sb("b2", [64, 1])
        w3 = PH.sb("w3", [64, 2, 1024]); om2 = PH.sb("om2", [64, 1])
        bpar = Buf()
        for dst, src in ((cw, hp["cw"]), (cb, hp["cb"]), (skip, hp["skip"]), (negd, E.negdelta[hh]), (w1, hp["w1"]), (b1, hp["b1"]),
                         (fr, hp["fr"]), (w2, hp["w2"]), (b2, hp["b2"]), (w3, hp["w3"])):
            S.dma("sp", dst[:], src, writes=[bpar])
        S.op("dve", lambda e: e.tensor_scalar(out=om2[:], in0=fr[:], scalar1=1.0 / TWO_PI, scalar2=None, op0=ALU.mult), reads=[bpar], writes=[bpar])

        def transpose_rows(src_bf, bsrc, dstT, bdst, ntb, col0, pst):
            tb = 0
            while tb < ntb:
                n = min(4, ntb - tb)
                pT, bpT = pst.next()
                for q in range(n):
                    S.op("pe", lambda e: e.transpose(out=pT[:, q * 128:(q + 1) * 128], in_=src_bf[:, (tb + q) * 128:(tb + q + 1) * 128], identity=E.ident_f[:]),
                         reads=[bsrc, E.bconst], writes=[bpT])
                _evac(S, tb // 4, dstT[:, tb:tb + n, col0:col0 + 128], pT[:, 0:n * 128].rearrange("p (a c) -> p a c", c=128), [bpT], [bdst])
                tb += n

        def filt(ctx):
            Lq = LC if ctx else L
            CH = min(512, Lq)
            nchunk = Lq // CH
            nb = Lq // 128
            feats, tnorm = (E.featsC, E.tnormC) if ctx else (E.featsL, E.tnormL)
            Ctab, Stab, phi = (E.CtabC, E.StabC, E.phiC) if ctx else (E.CtabL, E.StabL, E.phiL)
            KA, KB = (E.KAC, E.KBC) if ctx else (E.KAL, E.KBL)
            FW = 256
            PF = Phase(nc, S)
            ft = PF.sb("ft", [33, Lq]); tn = PF.sb("tn", [128, Lq]); hdn2 = PF.sb("hdn2", [64, Lq]); phis = PF.sb("phis", [128, 3, nb])
            bft = Buf(); bh2 = Buf()
            S.dma("sp", ft[:], feats, writes=[bft]); S.dma("sp", tn[:], tnorm, writes=[bft]); S.dma("sp", phis[:], phi, writes=[bft])
            P1 = Phase(nc, S)
            hdn1 = P1.sb("hdn1", [64, CH]); u = P1.sb("u", [64, CH]); ki = P1.sb("ki", [64, CH], I32); kf = P1.sb("kf", [64, CH])
            bh1, bu, bki = Buf(), Buf(), Buf()
            pss = Rot([(P1.ps("pf"), PBuf()) for _ in range(2)])

            def sin_layer(ps, bp, bvec, out_ap, bout):
                S.op("dve", lambda e: e.tensor_scalar(out=u[:], in0=ps[0:64, 0:CH], scalar1=bvec[:, 0:1], scalar2=om2[:, 0:1], op0=ALU.add, op1=ALU.mult),
                     reads=[bp, bpar], writes=[bu])
                S.op("dve", lambda e: e.tensor_copy(out=ki[:], in_=u[:]), reads=[bu], writes=[bki])
                S.op("dve", lambda e: e.tensor_copy(out=kf[:], in_=ki[:]), reads=[bki], writes=[bki])
                S.op("dve", lambda e: e.tensor_tensor(out=u[:], in0=u[:], in1=kf[:], op=ALU.subtract), reads=[bu, bki], writes=[bu])
                S.op("act", lambda e: e.activation(out=out_ap, in_=u[:], func=ACT.Sin, scale=TWO_PI), reads=[bu], writes=[bout])

            for ch in range(nchunk):
                ps, bp = pss.next()
                S.op("pe", lambda e: e.matmul(ps[0:64, 0:CH], lhsT=w1[:, :], rhs=ft[:, ch * CH:(ch + 1) * CH], start=True, stop=True), reads=[bpar, bft], writes=[bp])
                sin_layer(ps, bp, b1, hdn1[:], bh1)
                ps, bp = pss.next()
                S.op("pe", lambda e: e.matmul(ps[0:64, 0:CH], lhsT=w2[:, :], rhs=hdn1[:], start=True, stop=True), reads=[bpar, bh1], writes=[bp])
                sin_layer(ps, bp, b2, hdn2[:, ch * CH:(ch + 1) * CH], bh2)
            P1.close()
            fstop = E.cfg.get("fstop", 99)
            for G in range(2 if fstop > 1 else 0):
                PG = Phase(nc, S)
                ksT = PG.sb("ksT", [128, nb, 512], BF16); kdT = PG.sb("kdT", [128, nb, 512], BF16)
                bks, bkd = Buf(), Buf()
                P2 = Phase(nc, S)
                hf = P2.sb("hf", [128, Lq]); hb = P2.sb("hb", [128, Lq]); dec = P2.sb("dec", [128, CH]); ksb = P2.sb("ksb", [128, Lq])
                kdb = P2.sb("kdb", [128, Lq]); nrm = P2.sb("nrm", [128, 4])
                bhf, bhb, bdec, bksb, bkdb, bnrm = Buf(), Buf(), Buf(), Buf(), Buf(), Buf()
                pss = Rot([(P2.ps("pg"), PBuf()) for _ in range(2)])
                pst = Rot([(P2.ps("pt", (128, 512), F32), PBuf()) for _ in range(2)])
                for cq in range(4):
                    cbk = G * 4 + cq
                    for ch in range(nchunk):
                        sl = slice(ch * CH, (ch + 1) * CH)
                        S.op("act", lambda e: e.activation(out=dec[:], in_=tn[:, sl], func=ACT.Exp, scale=negd[:, cbk:cbk + 1]), reads=[bft, bpar], writes=[bdec])
                        for fb, (dst, bd) in enumerate(((hf, bhf), (hb, bhb))):
                            ps, bp = pss.next()
                            S.op("pe", lambda e: e.matmul(ps[:, 0:CH], lhsT=w3[:, fb, cbk * 128:(cbk + 1) * 128], rhs=hdn2[:, sl], start=True, stop=True),
                                 reads=[bpar, bh2], writes=[bp])
                            S.op("dve", lambda e: e.tensor_tensor(out=dst[:, sl], in0=ps[:, 0:CH], in1=dec[:], op=ALU.mult), reads=[bp, bdec], writes=[bd])
                    p2stop = E.cfg.get("p2stop", 99)
                    if p2stop <= 1:
                        continue
                    S.op("dve", lambda e: e.tensor_reduce(out=nrm[:, 0:1], in_=hf[:, :], axis=AX.X, op=ALU.add, apply_absolute_value=True), reads=[bhf], writes=[bnrm])
                    S.op("dve", lambda e: e.tensor_reduce(out=nrm[:, 1:2], in_=hb[:, 1:Lq], axis=AX.X, op=ALU.add, apply_absolute_value=True), reads=[bhb], writes=[bnrm])
                    S.op("dve", lambda e: e.tensor_tensor(out=nrm[:, 2:3], in0=nrm[:, 0:1], in1=nrm[:, 1:2], op=ALU.add), reads=[bnrm], writes=[bnrm])
                    S.op("dve", lambda e: e.reciprocal(out=nrm[:, 3:4], in_=nrm[:, 2:3]), reads=[bnrm], writes=[bnrm])
                    S.op("act", lambda e: e.activation(out=hf[:], in_=hf[:], func=ACT.Identity, scale=nrm[:, 3:4]), reads=[bhf, bnrm], writes=[bhf])
                    S.op("act", lambda e: e.activation(out=hb[:], in_=hb[:], func=ACT.Identity, scale=nrm[:, 3:4]), reads=[bhb, bnrm], writes=[bhb])
                    if p2stop <= 2:
                        continue
                    S.op("dve", lambda e: e.tensor_tensor(out=ksb[:, 0:Lq - 1], in0=hf[:, 0:Lq - 1], in1=hb[:, 1:Lq], op=ALU.add), reads=[bhf, bhb], writes=[bksb])
                    S.op("dve", lambda e: e.tensor_copy(out=ksb[:, Lq - 1:Lq], in_=hf[:, Lq - 1:Lq]), reads=[bhf], writes=[bksb])
                    S.op("pool", lambda e: e.tensor_tensor(out=kdb[:, 0:Lq - 1], in0=hf[:, 0:Lq - 1], in1=hb[:, 1:Lq], op=ALU.subtract), reads=[bhf, bhb], writes=[bkdb])
                    S.op("pool", lambda e: e.tensor_copy(out=kdb[:, Lq - 1:Lq], in_=hf[:, Lq - 1:Lq]), reads=[bhf], writes=[bkdb])
                    if p2stop <= 3:
                        continue
                    transpose_rows(ksb, bksb, ksT, bks, nb, cq * 128, pst)
                    transpose_rows(kdb, bkdb, kdT, bkd, nb, cq * 128, pst)
                P2.close()
                if fstop <= 2:
                    PG.close()
                    continue
                P3 = Phase(nc, S)
                Cs = Rot([(P3.sb("Cc", [128, nb, FW], BF16), Buf()) for _ in range(2)])
                Ss = Rot([(P3.sb("Sc", [128, nb, FW], BF16), Buf()) for _ in range(2)])
                t1 = P3.sb("t1", [128, 512]); t2 = P3.sb("t2", [128, 512]); bt1, bt2 = Buf(), Buf()
                oAs = Rot([(P3.sb("oA", [128, 512]), Buf()) for _ in range(2)])
                oBs = Rot([(P3.sb("oB", [128, 512]), Buf()) for _ in range(2)])
                pA = Rot([(P3.ps("pA"), PBuf()) for _ in range(2)])
                pB = Rot([(P3.ps("pB"), PBuf()) for _ in range(2)])
                for fg in range(Lq // FW):
                    Cc, bC = Cs.next(); Sc, bS = Ss.next()
                    S.dma("sp", Cc[:], Ctab[:, :, fg * FW:(fg + 1) * FW].rearrange("a p f -> p a f"), writes=[bC])
                    S.dma("sp", Sc[:], Stab[:, :, fg * FW:(fg + 1) * FW].rearrange("a p f -> p a f"), writes=[bS])
                    for fq in range(FW // 128):
                        fb = fg * (FW // 128) + fq
                        psA, bpA = pA.next(); psB, bpB = pB.next()
                        for tb in range(nb):
                            S.op("pe", lambda e: e.matmul(psA[:, :], lhsT=Cc[:, tb, fq * 128:(fq + 1) * 128], rhs=ksT[:, tb, :], start=(tb == 0), stop=(tb == nb - 1)),
                                 reads=[bC, bks], writes=[bpA])
                        for tb in range(nb):
                            S.op("pe", lambda e: e.matmul(psB[:, :], lhsT=Sc[:, tb, fq * 128:(fq + 1) * 128], rhs=kdT[:, tb, :], start=(tb == 0), stop=(tb == nb - 1)),
                                 reads=[bS, bkd], writes=[bpB])
                        oA, boA = oAs.next(); oB, boB = oBs.next()
                        S.op("act", lambda e: e.activation(out=t1[:], in_=psB[:, :], func=ACT.Identity, scale=phis[:, 1, fb:fb + 1]), reads=[bpB, bft], writes=[bt1])
                        S.op("dve", lambda e: e.scalar_tensor_tensor(out=oA[:], in0=psA[:, :], scalar=phis[:, 0, fb:fb + 1], in1=t1[:], op0=ALU.mult, op1=ALU.add),
                             reads=[bpA, bt1, bft], writes=[boA])
                        S.op("act", lambda e: e.activation(out=t2[:], in_=psA[:, :], func=ACT.Identity, scale=phis[:, 2, fb:fb + 1]), reads=[bpA, bft], writes=[bt2])
                        S.op("dve", lambda e: e.scalar_tensor_tensor(out=oB[:], in0=psB[:, :], scalar=phis[:, 0, fb:fb + 1], in1=t2[:], op0=ALU.mult, op1=ALU.add),
                             reads=[bpB, bt2, bft], writes=[boB])
                        S.dma("sp", KA[fb, :, G * 512:(G + 1) * 512], oA[:], reads=[boA], writes=[E.bK])
                        S.dma("sp", KB[fb, :, G * 512:(G + 1) * 512], oB[:], reads=[boB], writes=[E.bK])
                P3.close()
                PG.close()
            PF.close()

        filt(False)
        if need_ctx and E.cfg.get("fstop", 99) > 3:
            filt(True)

        def data(G):
            PB = Phase(nc, S)
            Ya = PB.sb("Ya", [128, 32, 512], BF16); Yb = PB.sb("Yb", [128, 32, 512], BF16)
            Yac = PB.sb("Yac", [128, 2, 512], BF16); Ybc = PB.sb("Ybc", [128, 2, 512], BF16)
            bYa, bYb, bYac, bYbc = Buf(), Buf(), Buf(), Buf()
            PD = Phase(nc, S)
            sT = PD.sb("sT", [128, 34, 512], BF16); bsT = Buf()
            PA = Phase(nc, S)
            us = Rot([(PA.sb("u", [128, NT]), Buf()) for _ in range(2)])
            accs = Rot([(PA.sb("acc", [128, NT]), Buf()) for _ in range(2)])
            pst = Rot([(PA.ps("pt", (128, 512), F32), PBuf()) for _ in range(2)])

            def conv(q, cbk):
                u, bu = us.next()
                S.dma("sp", u[:], E.UF[q * 8 + cbk, :, :], reads=[E.bUF[q * 8 + cbk]], writes=[bu])
                a, ba = accs.next()
                S.op("act", lambda e: e.activation(out=a[:], in_=u[:], func=ACT.Identity, scale=cw[:, q, cbk, 1:2], bias=cb[:, q, cbk:cbk + 1]),
                     reads=[bu, bpar], writes=[ba])
                for lo, hi in ((0, L), (L, NT)):
                    S.op("dve", lambda e: e.scalar_tensor_tensor(out=a[:, lo + 1:hi], in0=u[:, lo:hi - 1], scalar=cw[:, q, cbk, 0:1], in1=a[:, lo + 1:hi],
                                                                 op0=ALU.mult, op1=ALU.add), reads=[bu, ba, bpar], writes=[ba])
                    S.op("dve", lambda e: e.scalar_tensor_tensor(out=a[:, lo:hi - 1], in0=u[:, lo + 1:hi], scalar=cw[:, q, cbk, 2:3], in1=a[:, lo:hi - 1],
                                                                 op0=ALU.mult, op1=ALU.add), reads=[bu, ba, bpar], writes=[ba])
                return a, ba

            for cq in range(4):
                cbk = G * 4 + cq
                x1c, bx1 = conv(1, cbk)
                vc, bv = conv(2, cbk)
                S.op("pool", lambda e: e.tensor_tensor(out=x1c[:], in0=x1c[:], in1=vc[:], op=ALU.mult), reads=[bx1, bv], writes=[bx1])
                S.dma("sp", E.SD[cbk, :, :], x1c[:], reads=[bx1], writes=[E.bSD[cbk]])
                transpose_rows(x1c, bx1, sT, bsT, 34, cq * 128, pst)
                x0c, bx0 = conv(0, cbk)
                u, bu = us.next()
                S.dma("sp", u[:], E.UF[24 + cbk, :, :], reads=[E.bUF[24 + cbk]], writes=[bu])
                S.op("act", lambda e: e.activation(out=u[:], in_=u[:], func=ACT.Silu), reads=[bu], writes=[bu])
                S.op("dve", lambda e: e.tensor_tensor(out=x0c[:], in0=x0c[:], in1=u[:], op=ALU.mult), reads=[bx0, bu], writes=[bx0])
                S.dma("sp", E.X0G[cbk, :, :], x0c[:], reads=[bx0], writes=[E.bX0G[cbk]])
            PA.close()
            P1 = Phase(nc, S)
            FW = 256
            Cs = Rot([(P1.sb("Cc", [128, 32, FW], BF16), Buf()) for _ in range(2)])
            Ss = Rot([(P1.sb("Sc", [128, 32, FW], BF16), Buf()) for _ in range(2)])
            kas = Rot([(P1.sb("ka", [128, 512]), Buf()) for _ in range(2)])
            kbs = Rot([(P1.sb("kb", [128, 512]), Buf()) for _ in range(2)])
            tt = [(P1.sb(f"t{i}", [128, 512]), Buf()) for i in range(4)]
            pA = Rot([(P1.ps("pA"), PBuf()) for _ in range(2)])
            pB = Rot([(P1.ps("pB"), PBuf()) for _ in range(2)])

            def spectral(nb, tb0, Ctab, Stab, KA, KB, Yra, bYra, Yrb, bYrb, FWc):
                for fg in range((nb * 128) // FWc):
                    Cc, bC = Cs.next(); Sc, bS = Ss.next()
                    S.dma("sp", Cc[:, 0:nb, 0:FWc], Ctab[:, :, fg * FWc:(fg + 1) * FWc].rearrange("a p f -> p a f"), writes=[bC])
                    S.dma("sp", Sc[:, 0:nb, 0:FWc], Stab[:, :, fg * FWc:(fg + 1) * FWc].rearrange("a p f -> p a f"), writes=[bS])
                    for fq in range(FWc // 128):
                        fb = fg * (FWc // 128) + fq
                        psA, bpA = pA.next(); psB, bpB = pB.next()
                        for tb in range(nb):
                            S.op("pe", lambda e: e.matmul(psA[:, :], lhsT=Cc[:, tb, fq * 128:(fq + 1) * 128], rhs=sT[:, tb0 + tb, :], start=(tb == 0), stop=(tb == nb - 1)),
                                 reads=[bC, bsT], writes=[bpA])
                        for tb in range(nb):
                            S.op("pe", lambda e: e.matmul(psB[:, :], lhsT=Sc[:, tb, fq * 128:(fq + 1) * 128], rhs=sT[:, tb0 + tb, :], start=(tb == 0), stop=(tb == nb - 1)),
                                 reads=[bS, bsT], writes=[bpB])
                        ka, bka = kas.next(); kb, bkb = kbs.next()
                        S.dma("sp", ka[:], KA[fb, :, G * 512:(G + 1) * 512], reads=[E.bK], writes=[bka])
                        S.dma("sp", kb[:], KB[fb, :, G * 512:(G + 1) * 512], reads=[E.bK], writes=[bkb])
                        (t1, b1_), (t2, b2_), (t3, b3_), (t4, b4_) = tt
                        S.op("dve", lambda e: e.tensor_tensor(out=t1[:], in0=psA[:, :], in1=ka[:], op=ALU.mult), reads=[bpA, bka], writes=[b1_])
                        S.op("dve", lambda e: e.tensor_tensor(out=t2[:], in0=psB[:, :], in1=kb[:], op=ALU.mult), reads=[bpB, bkb], writes=[b2_])
                        S.op("pool", lambda e: e.tensor_tensor(out=Yra[:, fb, :], in0=t1[:], in1=t2[:], op=ALU.subtract), reads=[b1_, b2_], writes=[bYra])
                        S.op("dve", lambda e: e.tensor_tensor(out=t3[:], in0=psA[:, :], in1=kb[:], op=ALU.mult), reads=[bpA, bkb], writes=[b3_])
                        S.op("dve", lambda e: e.tensor_tensor(out=t4[:], in0=psB[:, :], in1=ka[:], op=ALU.mult), reads=[bpB, bka], writes=[b4_])
                        S.op("pool", lambda e: e.tensor_tensor(out=Yrb[:, fb, :], in0=t3[:], in1=t4[:], op=ALU.add), reads=[b3_, b4_], writes=[bYrb])

            spectral(32, 0, E.CtabL, E.StabL, E.KAL, E.KBL, Ya, bYa, Yb, bYb, FW)
            if need_ctx:
                spectral(2, 32, E.CtabC, E.StabC, E.KAC, E.KBC, Yac, bYac, Ybc, bYbc, 256)
            P1.close()
            PD.close()
            PC = Phase(nc, S)
            Cs = Rot([(PC.sb("Cr", [128, 16, 512], BF16), Buf()) for _ in range(2)])
            Ss = Rot([(PC.sb("Sr", [128, 16, 512], BF16), Buf()) for _ in range(2)])
            sts = Rot([(PC.sb("s_t", [128, 512]), Buf()) for _ in range(2)])
            xts = Rot([(PC.sb("x_t", [128, 512]), Buf()) for _ in range(2)])
            tms = Rot([(PC.sb("tmp", [128, 512]), Buf()) for _ in range(2)])
            yos = Rot([(PC.sb("yo", [128, 512], BF16), Buf()) for _ in range(2)])
            pys = Rot([(PC.ps("py"), PBuf()) for _ in range(8)])

            def epilogue(cq, ps, bp, t0, n):
                cbk = G * 4 + cq
                s_t, bs_ = sts.next(); x_t, bx_ = xts.next(); tmp, btm = tms.next(); yo, byo = yos.next()
                S.dma("sp", s_t[:, 0:n], E.SD[cbk, :, t0:t0 + n], reads=[E.bSD[cbk]], writes=[bs_])
                S.dma("sp", x_t[:, 0:n], E.X0G[cbk, :, t0:t0 + n], reads=[E.bX0G[cbk]], writes=[bx_])
                S.op("dve", lambda e: e.scalar_tensor_tensor(out=tmp[:, 0:n], in0=s_t[:, 0:n], scalar=skip[:, cbk:cbk + 1], in1=ps[:, 0:n], op0=ALU.mult, op1=ALU.add),
                     reads=[bs_, bp, bpar], writes=[btm])
                S.op("pool", lambda e: e.tensor_tensor(out=yo[:, 0:n], in0=tmp[:, 0:n], in1=x_t[:, 0:n], op=ALU.mult), reads=[btm, bx_], writes=[byo])
                S.dma("sp", E.YT[cbk, :, t0:t0 + n], yo[:, 0:n], reads=[byo], writes=[E.bYT[cbk]])

            for tc in range(8):
                pl = [pys.next() for _ in range(4)]
                for half in range(2):
                    Cr, bC = Cs.next(); Sr, bS = Ss.next()
                    S.dma("sp", Cr[:], E.CtabL[half * 16:(half + 1) * 16, :, tc * 512:(tc + 1) * 512].rearrange("a p t -> p a t"), writes=[bC])
                    S.dma("sp", Sr[:], E.StabL[half * 16:(half + 1) * 16, :, tc * 512:(tc + 1) * 512].rearrange("a p t -> p a t"), writes=[bS])
                    for cq in range(4):
                        ps, bp = pl[cq]
                        for fbl in range(16):
                            fb = half * 16 + fbl
                            S.op("pe", lambda e: e.matmul(ps[:, :], lhsT=Ya[:, fb, cq * 128:(cq + 1) * 128], rhs=Cr[:, fbl, :], start=(fb == 0), stop=False),
                                 reads=[bYa, bC], writes=[bp])
                            S.op("pe", lambda e: e.matmul(ps[:, :], lhsT=Yb[:, fb, cq * 128:(cq + 1) * 128], rhs=Sr[:, fbl, :], start=False, stop=(fb == 31)),
                                 reads=[bYb, bS], writes=[bp])
                for cq in range(4):
                    epilogue(cq, pl[cq][0], pl[cq][1], tc * 512, 512)
            if need_ctx:
                Cr, bC = Cs.next(); Sr, bS = Ss.next()
                S.dma("sp", Cr[:, 0:2, 0:256], E.CtabC.rearrange("a p t -> p a t"), writes=[bC])
                S.dma("sp", Sr[:, 0:2, 0:256], E.StabC.rearrange("a p t -> p a t"), writes=[bS])
                for cq in range(4):
                    ps, bp = pys.next()
                    for fb in range(2):
                        S.op("pe", lambda e: e.matmul(ps[:, 0:256], lhsT=Yac[:, fb, cq * 128:(cq + 1) * 128], rhs=Cr[:, fb, 0:256], start=(fb == 0), stop=False),
                             reads=[bYac, bC], writes=[bp])
                        S.op("pe", lambda e: e.matmul(ps[:, 0:256], lhsT=Ybc[:, fb, cq * 128:(cq + 1) * 128], rhs=Sr[:, fb, 0:256], start=False, stop=(fb == 1)),
                             reads=[bYbc, bS], writes=[bp])
                    epilogue(cq, ps, bp, L, 256)
            PC.close()
            PB.close()

        if not E.cfg.get("hy_filter_only", False):
            data(0)
            data(1)
        PH.close()

    def attend(qT, bq, t0, n, kblocks, lhsK, bk, lhsV, bv, gate, bg, ycb, pS, pO, pZ, pTs, wk, emul=None):
        psO, bpO = pO.next(); psZ, bpZ = pZ.next()
        nk = len(kblocks)
        for i, kb in enumerate(kblocks):
            psS, bpS = pS.next()
            S.op("pe", lambda e: e.matmul(psS[:, 0:n], lhsT=lhsK(kb), rhs=qT[:, t0:t0 + n], start=True, stop=True), reads=[bk, bq], writes=[bpS])
            pT, bpT = pTs.next()
            em = emul(kb) if emul is not None else None
            if em is None:
                S.op("act", lambda e: e.activation(out=pT[:, 0:n], in_=psS[:, 0:n], func=ACT.Exp, scale=ISQ), reads=[bpS], writes=[bpT])
            else:
                pf, bpf = wk["pf"].next()
                S.op("act", lambda e: e.activation(out=pf[:, 0:n], in_=psS[:, 0:n], func=ACT.Exp, scale=ISQ), reads=[bpS], writes=[bpf])
                S.op("dve" if i % 2 == 0 else "pool", lambda e: e.tensor_tensor(out=pT[:, 0:n], in0=pf[:, 0:n], in1=em[0], op=ALU.mult), reads=[bpf, em[1]], writes=[bpT])
            S.op("pe", lambda e: e.matmul(psO[:, 0:n], lhsT=lhsV(kb), rhs=pT[:, 0:n], start=(i == 0), stop=(i == nk - 1)), reads=[bv, bpT], writes=[bpO])
            S.op("pe", lambda e: e.matmul(psZ[:, 0:n], lhsT=E.ones_bf[:], rhs=pT[:, 0:n], start=(i == 0), stop=(i == nk - 1)), reads=[E.bconst, bpT], writes=[bpZ])
        rz, brz = wk["rz"].next(); o1, bo1 = wk["o1"].next(); yo, byo = wk["yo"].next()
        S.op("dve", lambda e: e.reciprocal(out=rz[:, 0:n], in_=psZ[:, 0:n]), reads=[bpZ], writes=[brz])
        S.op("dve", lambda e: e.tensor_tensor(out=o1[:, 0:n], in0=psO[:, 0:n], in1=rz[:, 0:n], op=ALU.mult), reads=[bpO, brz], writes=[bo1])
        S.op("pool", lambda e: e.tensor_tensor(out=yo[:, 0:n], in0=o1[:, 0:n], in1=gate[:, t0:t0 + n], op=ALU.mult), reads=[bo1, bg], writes=[byo])
        S.dma("sp", E.YT[ycb, :, t0:t0 + n], yo[:, 0:n], reads=[byo], writes=[E.bYT[ycb]])

    def attn_work(P):
        return dict(rz=Rot([(P.sb("rz", [128, 512]), Buf()) for _ in range(2)]),
                    o1=Rot([(P.sb("o1", [128, 512]), Buf()) for _ in range(2)]),
                    yo=Rot([(P.sb("yo", [128, 512], BF16), Buf()) for _ in range(2)]),
                    pf=Rot([(P.sb("pf", [128, 512]), Buf()) for _ in range(2)]))

    def gqa(l, hh):
        j = l // 2
        need_ctx = l < 3
        hp = E.hy[(j, hh)]
        P = Phase(nc, S)
        kT = P.sb("kT", [128, 2, NT], BF16); vtok = P.sb("vtok", [128, 34, 256], BF16)
        cosT = P.sb("cosT", [128, L]); sinT = P.sb("sinT", [128, L]); prot = P.sb("prot", [128, 128]); qg = P.sb("qg", [128, 1]); kg = P.sb("kg", [128, 1])
        bkT, bvt, bcst = Buf(), Buf(), Buf()
        S.dma("sp", cosT[:], E.ropeC, writes=[bcst]); S.dma("sp", sinT[:], E.ropeS, writes=[bcst]); S.dma("sp", prot[:], E.protT, writes=[bcst])
        S.dma("sp", qg[:], hp["qg"], writes=[bcst]); S.dma("sp", kg[:], hp["kg"], writes=[bcst])
        S.dma("sp", vtok[:], E.VT[:, :, 0:256].rearrange("a p c -> p a c"), reads=[E.bVT], writes=[bvt])
        raws = Rot([(P.sb("raw", [128, NT]), Buf()) for _ in range(2)])
        gates = Rot([(P.sb("gate", [128, NT]), Buf()) for _ in range(2)])
        qTs = Rot([(P.sb("qT", [128, NT], BF16), Buf()) for _ in range(2)])
        sq = P.sb("sq", [128, 512], BF16); rs = P.sb("rs", [128, 512]); qn = P.sb("qn", [128, 512]); t1 = P.sb("t1", [128, 512]); t2 = P.sb("t2", [128, 512])
        bsq, brs, bqn, bt1, bt2 = Buf(), Buf(), Buf(), Buf(), Buf()
        pN = Rot([(P.ps("pN"), PBuf()) for _ in range(1)])
        pS = Rot([(P.ps("pS"), PBuf()) for _ in range(3)])
        pO = Rot([(P.ps("pO"), PBuf()) for _ in range(2)])
        pZ = Rot([(P.ps("pZ"), PBuf()) for _ in range(2)])
        pTs = Rot([(P.sb("pT", [128, 512], BF16), Buf()) for _ in range(3)])
        wk = attn_work(P)

        def normrope(src, bsrc, g, dst_ap_of, bdst):
            for ch in range(9):
                n = 512 if ch < 8 else 256
                t0 = ch * 512
                S.op("act", lambda e: e.activation(out=sq[:, 0:n], in_=src[:, t0:t0 + n], func=ACT.Square), reads=[bsrc], writes=[bsq])
                ps, bp = pN.next()
                S.op("pe", lambda e: e.matmul(ps[:, 0:n], lhsT=E.ones_bf[:], rhs=sq[:, 0:n], start=True, stop=True), reads=[bsq, E.bconst], writes=[bp])
                S.op("act", lambda e: e.activation(out=rs[:, 0:n], in_=ps[:, 0:n], func=ACT.Sqrt, scale=1.0 / 128.0, bias=EPS), reads=[bp], writes=[brs])
                S.op("dve", lambda e: e.reciprocal(out=rs[:, 0:n], in_=rs[:, 0:n]), reads=[brs], writes=[brs])
                S.op("dve", lambda e: e.scalar_tensor_tensor(out=qn[:, 0:n], in0=src[:, t0:t0 + n], scalar=g[:, 0:1], in1=rs[:, 0:n], op0=ALU.mult, op1=ALU.mult),
                     reads=[bsrc, brs, bcst], writes=[bqn])
                if ch < 8:
                    ps, bp = pN.next()
                    S.op("pe", lambda e: e.matmul(ps[:, 0:n], lhsT=prot[:], rhs=qn[:, 0:n], start=True, stop=True), reads=[bqn, bcst], writes=[bp])
                    S.op("dve", lambda e: e.tensor_tensor(out=t1[:, 0:n], in0=qn[:, 0:n], in1=cosT[:, t0:t0 + n], op=ALU.mult), reads=[bqn, bcst], writes=[bt1])
                    S.op("dve", lambda e: e.tensor_tensor(out=t2[:, 0:n], in0=ps[:, 0:n], in1=sinT[:, t0:t0 + n], op=ALU.mult), reads=[bp, bcst], writes=[bt2])
                    S.op("pool", lambda e: e.tensor_tensor(out=dst_ap_of(t0, n), in0=t1[:, 0:n], in1=t2[:, 0:n], op=ALU.add), reads=[bt1, bt2], writes=[bdst])
                else:
                    S.op("act", lambda e: e.copy(out=dst_ap_of(t0, n), in_=qn[:, 0:n]), reads=[bqn], writes=[bdst])

        for kv in range(2):
            raw, braw = raws.next()
            S.dma("sp", raw[:], E.UF[40 + kv, :, :], reads=[E.bUF[40 + kv]], writes=[braw])
            normrope(raw, braw, kg, lambda t0, n: kT[:, kv, t0:t0 + n], bkT)
        for hd in range(8):
            kv = hd // 4
            raw, braw = raws.next()
            S.dma("sp", raw[:], E.UF[32 + hd, :, :], reads=[E.bUF[32 + hd]], writes=[braw])
            gate, bg = gates.next()
            S.dma("sp", gate[:], E.UF[42 + hd, :, :], reads=[E.bUF[42 + hd]], writes=[bg])
            S.op("act", lambda e: e.activation(out=gate[:], in_=gate[:], func=ACT.Silu), reads=[bg], writes=[bg])
            qT, bq = qTs.next()
            normrope(raw, braw, qg, lambda t0, n: qT[:, t0:t0 + n], bq)
            for ch in range(9 if need_ctx else 8):
                n = 512 if ch < 8 else 256
                kbl = list(range(34)) if ch < 8 else [32, 33]
                attend(qT, bq, ch * 512, n, kbl, lambda kb: kT[:, kv, kb * 128:(kb + 1) * 128], bkT,
                       lambda kb: vtok[:, kb, kv * 128:(kv + 1) * 128], bvt, gate, bg, 8 + hd, pS, pO, pZ, pTs, wk)
        P.close()

    def lru(l, hh):
        j = l // 2
        need_ctx = l < 3
        lp = E.lru[(j, hh)]
        P = Phase(nc, S)
        cw = P.sb("cw", [128, 8, 4]); cb = P.sb("cb", [128, 8]); wa = P.sb("wa", [128, 2, 8, 128]); wx = P.sb("wx", [128, 2, 8, 128])
        ba = P.sb("ba", [128, 2, 8]); bx = P.sb("bx", [128, 2, 8]); lam = P.sb("lam", [128, 2, 8]); cl = P.sb("cl", [128, 2, 8]); cl2 = P.sb("cl2", [128, 2, 8])
        bpar = Buf()
        for dst, src in ((cw, lp["cw"]), (cb, lp["cb"]), (wa, lp["wa"]), (wx, lp["wx"]), (ba, lp["ba"]), (bx, lp["bx"]), (lam, lp["lam"])):
            S.dma("sp", dst[:], src, writes=[bpar])
        S.op("act", lambda e: e.activation(out=lam[:], in_=lam[:], func=ACT.Exp, scale=-1.0), reads=[bpar], writes=[bpar])
        S.op("act", lambda e: e.activation(out=lam[:], in_=lam[:], func=ACT.Ln, bias=1.0, scale=1.0), reads=[bpar], writes=[bpar])
        S.op("dve", lambda e: e.tensor_scalar(out=cl[:], in0=lam[:], scalar1=-8.0, scalar2=None, op0=ALU.mult), reads=[bpar], writes=[bpar])
        S.op("dve", lambda e: e.tensor_scalar(out=cl2[:], in0=lam[:], scalar1=-16.0, scalar2=None, op0=ALU.mult), reads=[bpar], writes=[bpar])
        us = Rot([(P.sb("u", [128, NT]), Buf()) for _ in range(2)])
        xr = P.sb("xr", [128, NT]); r = P.sb("r", [128, NT]); ig = P.sb("ig", [128, NT]); a = P.sb("a", [128, NT]); bb = P.sb("bb", [128, NT])
        hA = P.sb("hA", [128, NT]); hB = P.sb("hB", [128, NT]); yo = P.sb("yo", [128, NT], BF16)
        bxr, br, bi, ba_, bbb, bhA, bhB, byo = Buf(), Buf(), Buf(), Buf(), Buf(), Buf(), Buf(), Buf()
        pss = Rot([(P.ps("pl"), PBuf()) for _ in range(4)])
        for cbk in range(8):
            u, bu = us.next()
            S.dma("sp", u[:], E.UF[cbk, :, :], reads=[E.bUF[cbk]], writes=[bu])
            S.op("act", lambda e: e.activation(out=xr[:], in_=u[:], func=ACT.Identity, scale=cw[:, cbk, 2:3], bias=cb[:, cbk:cbk + 1]), reads=[bu, bpar], writes=[bxr])
            for lo, hi in ((0, L), (L, NT)):
                S.op("dve", lambda e: e.scalar_tensor_tensor(out=xr[:, lo + 1:hi], in0=u[:, lo:hi - 1], scalar=cw[:, cbk, 1:2], in1=xr[:, lo + 1:hi], op0=ALU.mult, op1=ALU.add),
                     reads=[bu, bxr, bpar], writes=[bxr])
                S.op("dve", lambda e: e.scalar_tensor_tensor(out=xr[:, lo + 2:hi], in0=u[:, lo:hi - 2], scalar=cw[:, cbk, 0:1], in1=xr[:, lo + 2:hi], op0=ALU.mult, op1=ALU.add),
                     reads=[bu, bxr, bpar], writes=[bxr])
                S.op("dve", lambda e: e.scalar_tensor_tensor(out=xr[:, lo:hi - 1], in0=u[:, lo + 1:hi], scalar=cw[:, cbk, 3:4], in1=xr[:, lo:hi - 1], op0=ALU.mult, op1=ALU.add),
                     reads=[bu, bxr, bpar], writes=[bxr])
            for d in range(2):
                for ch in range(9):
                    n = 512 if ch < 8 else 256
                    t0 = ch * 512
                    ps, bp = pss.next()
                    S.op("pe", lambda e: e.matmul(ps[:, 0:n], lhsT=wa[:, d, cbk, :], rhs=xr[:, t0:t0 + n], start=True, stop=True), reads=[bpar, bxr], writes=[bp])
                    S.op("act", lambda e: e.activation(out=r[:, t0:t0 + n], in_=ps[:, 0:n], func=ACT.Sigmoid, bias=ba[:, d, cbk:cbk + 1], scale=1.0), reads=[bp, bpar], writes=[br])
                    ps, bp = pss.next()
                    S.op("pe", lambda e: e.matmul(ps[:, 0:n], lhsT=wx[:, d, cbk, :], rhs=xr[:, t0:t0 + n], start=True, stop=True), reads=[bpar, bxr], writes=[bp])
                    S.op("act", lambda e: e.activation(out=ig[:, t0:t0 + n], in_=ps[:, 0:n], func=ACT.Sigmoid, bias=bx[:, d, cbk:cbk + 1], scale=1.0), reads=[bp, bpar], writes=[bi])
                S.op("act", lambda e: e.activation(out=a[:], in_=r[:], func=ACT.Exp, scale=cl[:, d, cbk:cbk + 1]), reads=[br, bpar], writes=[ba_])
                S.op("act", lambda e: e.activation(out=r[:], in_=r[:], func=ACT.Exp, scale=cl2[:, d, cbk:cbk + 1]), reads=[br, bpar], writes=[br])
                S.op("act", lambda e: e.activation(out=r[:], in_=r[:], func=ACT.Sqrt, scale=-1.0, bias=1.0), reads=[br], writes=[br])
                S.op("pool", lambda e: e.tensor_tensor(out=ig[:], in0=ig[:], in1=xr[:], op=ALU.mult), reads=[bi, bxr], writes=[bi])
                S.op("pool", lambda e: e.tensor_tensor(out=bb[:], in0=ig[:], in1=r[:], op=ALU.mult), reads=[bi, br], writes=[bbb])
                if d == 0:
                    S.op("dve", lambda e: e.tensor_tensor_scan(out=hA[:, L:NT], data0=a[:, L:NT], data1=bb[:, L:NT], initial=0.0, op0=ALU.mult, op1=ALU.add),
                         reads=[ba_, bbb], writes=[bhA])
                    S.op("dve", lambda e: e.tensor_tensor_scan(out=hA[:, 0:L], data0=a[:, 0:L], data1=bb[:, 0:L], initial=hA[:, NT - 1:NT], op0=ALU.mult, op1=ALU.add),
                         reads=[ba_, bbb, bhA], writes=[bhA])
                else:
                    S.op("dve", lambda e: e.tensor_tensor_scan(out=hB[:, L:NT][:, ::-1], data0=a[:, L:NT][:, ::-1], data1=bb[:, L:NT][:, ::-1], initial=0.0,
                                                               op0=ALU.mult, op1=ALU.add), reads=[ba_, bbb], writes=[bhB])
                    S.op("dve", lambda e: e.tensor_tensor_scan(out=hB[:, 0:L][:, ::-1], data0=a[:, 0:L][:, ::-1], data1=bb[:, 0:L][:, ::-1], initial=hB[:, L:L + 1],
                                                               op0=ALU.mult, op1=ALU.add), reads=[ba_, bbb, bhB], writes=[bhB])
            g, bg = us.next()
            S.dma("sp", g[:], E.UF[8 + cbk, :, :], reads=[E.bUF[8 + cbk]], writes=[bg])
            S.op("act", lambda e: e.activation(out=g[:], in_=g[:], func=ACT.Silu), reads=[bg], writes=[bg])
            S.op("pool", lambda e: e.tensor_tensor(out=hA[:], in0=hA[:], in1=hB[:], op=ALU.add), reads=[bhA, bhB], writes=[bhA])
            S.op("dve", lambda e: e.tensor_tensor(out=yo[:], in0=hA[:], in1=g[:], op=ALU.mult), reads=[bhA, bg], writes=[byo])
            S.dma("sp", E.YT[cbk, :, :], yo[:], reads=[byo], writes=[E.bYT[cbk]])
        P.close()

    def na(l, hh):
        j = l // 2
        need_ctx = l < 3
        lp = E.lru[(j, hh)]
        P = Phase(nc, S)
        raws = Rot([(P.sb("raw", [128, NT]), Buf()) for _ in range(2)])
        gates = Rot([(P.sb("gate", [128, NT]), Buf()) for _ in range(2)])
        qTs = Rot([(P.sb("qT", [128, NT], BF16), Buf()) for _ in range(2)])
        kTs = Rot([(P.sb("kT", [128, NT], BF16), Buf()) for _ in range(2)])
        vts = Rot([(P.sb("vt", [128, 34, 128], BF16), Buf()) for _ in range(2)])
        Ebs = Rot([(P.sb("Eb", [128, 20, 512], BF16), Buf()) for _ in range(2)])
        stg = Rot([(P.sb("bst", [128, 512]), Buf()) for _ in range(2)])
        pS = Rot([(P.ps("pS"), PBuf()) for _ in range(3)])
        pO = Rot([(P.ps("pO"), PBuf()) for _ in range(2)])
        pZ = Rot([(P.ps("pZ"), PBuf()) for _ in range(2)])
        pTs = Rot([(P.sb("pT", [128, 512], BF16), Buf()) for _ in range(3)])
        wk = attn_work(P)
        for hd in range(8):
            raw, braw = raws.next()
            S.dma("sp", raw[:], E.UF[16 + hd, :, :], reads=[E.bUF[16 + hd]], writes=[braw])
            qT, bq = qTs.next()
            S.op("act", lambda e: e.copy(out=qT[:], in_=raw[:]), reads=[braw], writes=[bq])
            raw, braw = raws.next()
            S.dma("sp", raw[:], E.UF[24 + hd, :, :], reads=[E.bUF[24 + hd]], writes=[braw])
            kT, bk = kTs.next()
            S.op("dve", lambda e: e.tensor_copy(out=kT[:], in_=raw[:]), reads=[braw], writes=[bk])
            gate, bg = gates.next()
            S.dma("sp", gate[:], E.UF[32 + hd, :, :], reads=[E.bUF[32 + hd]], writes=[bg])
            S.op("act", lambda e: e.activation(out=gate[:], in_=gate[:], func=ACT.Silu), reads=[bg], writes=[bg])
            vt, bv = vts.next()
            S.dma("sp", vt[:], E.VT[:, :, hd * 128:(hd + 1) * 128].rearrange("a p c -> p a c"), reads=[E.bVT], writes=[bv])
            Eb, bE = Ebs.next()
            for ti in range(20):
                st_, bst = stg.next()
                S.dma("sp", st_[:], lp["nab"][hd, ti, :, :], writes=[bst])
                S.op("act", lambda e: e.activation(out=Eb[:, ti, :], in_=st_[:], func=ACT.Exp), reads=[bst], writes=[bE])
            for i in range(8):
                if i == 0:
                    wkb = [(kb, kb) for kb in range(6)]
                elif i == 7:
                    wkb = [(26 + jj, 14 + jj) for jj in range(6)]
                else:
                    wkb = [(4 * i - 2 + jj, 6 + jj) for jj in range(8)]
                tmap = dict(wkb)
                kbl = [kb for kb, _ in wkb] + [32, 33]
                attend(qT, bq, i * 512, 512, kbl, lambda kb: kT[:, kb * 128:(kb + 1) * 128], bk, lambda kb: vt[:, kb, :], bv, gate, bg, 8 + hd,
                       pS, pO, pZ, pTs, wk, emul=lambda kb: ((Eb[:, tmap[kb], :], bE) if kb in tmap else None))
            if need_ctx:
                attend(qT, bq, L, 256, [32, 33], lambda kb: kT[:, kb * 128:(kb + 1) * 128], bk, lambda kb: vt[:, kb, :], bv, gate, bg, 8 + hd,
                       pS, pO, pZ, pTs, wk)
        P.close()

    def even(l, hh):
        if "hy" in E.cfg.get("mix", ("hy", "gqa")):
            hyena(l, hh)
        if "gqa" in E.cfg.get("mix", ("hy", "gqa")):
            gqa(l, hh)

    def odd(l, hh):
        if "lru" in E.cfg.get("mix", ("lru", "na")):
            lru(l, hh)
        if "na" in E.cfg.get("mix", ("lru", "na")):
            na(l, hh)

    return {"even": even, "odd": odd}


def _bf16(a):
    return np.ascontiguousarray(a.astype(ml_dtypes.bfloat16))


_CONST = {}


def host_constants():
    if _CONST:
        return _CONST
    f32 = np.float32
    c = {}
    for name, Lq in (("L", L), ("C", LC)):
        t = np.linspace(0.0, 1.0, Lq, dtype=f32)[:, None]
        bands = 16
        w = (2.0 * math.pi * np.arange(Lq, dtype=f32)[:, None] / Lq).astype(f32)
        fr = np.linspace(1e-4, bands - 1, bands, dtype=f32)[None, :]
        feats = np.concatenate([t, np.cos(fr * w), -np.sin(fr * w)], axis=-1).astype(f32)
        c["feats" + name] = np.ascontiguousarray(feats.T)
        c["tnorm" + name] = np.ascontiguousarray(np.broadcast_to(t[:, 0][None, :], (128, Lq))).astype(f32)
        N = 2 * Lq
        idx = (2 * np.arange(Lq, dtype=np.int64) + 1)
        m = (idx[:, None] * idx[None, :]) % (4 * N)
        ang = (2.0 * math.pi / (4 * N)) * m.astype(np.float64)
        nb = Lq // 128
        c["Ctab" + name] = _bf16(np.cos(ang).astype(f32)).reshape(nb, 128, Lq)
        c["Stab" + name] = _bf16(np.sin(ang).astype(f32)).reshape(nb, 128, Lq)
        phi = math.pi * (np.arange(Lq, dtype=np.float64) + 0.5) / N
        cp = (2.0 / N) * np.cos(phi)
        sp_ = (2.0 / N) * np.sin(phi)
        ph = np.stack([cp, sp_, -sp_], axis=0).astype(f32)
        c["phi" + name] = np.ascontiguousarray(ph.reshape(3, nb, 128).transpose(2, 0, 1))
    pos = np.arange(L)
    row = (pos // 64).astype(f32)
    col = (pos % 64).astype(f32)
    n = 32
    inv = (10000.0 ** (-np.arange(n, dtype=f32) / n)).astype(f32)
    ang = np.concatenate([row[:, None] * inv, col[:, None] * inv], axis=-1).astype(f32)
    cosT = np.zeros((128, L), f32)
    sinT = np.zeros((128, L), f32)
    prot = np.zeros((128, 128), f32)
    for a in range(2):
        for pr in range(2):
            for i in range(n):
                dh = a * 64 + pr * 32 + i
                cosT[dh] = np.cos(ang[:, a * 32 + i])
                sinT[dh] = np.sin(ang[:, a * 32 + i])
        for i in range(n):
            prot[a * 64 + i, a * 64 + 32 + i] = -1.0
            prot[a * 64 + 32 + i, a * 64 + i] = 1.0
    c["ropeC"] = cosT
    c["ropeS"] = sinT
    c["protT"] = np.ascontiguousarray(prot.T)
    deltas = np.linspace(abs(math.log(1e-2) / 1.5), abs(math.log(1e-2) / 0.3), 2048, dtype=f32)
    c["deltas"] = deltas
    _CONST.update(c)
    return c


def na_bias_tiles(rpb_heads):
    key_p = np.arange(128)
    krl = key_p // 64
    kc = key_p % 64
    qf = np.arange(512)
    qrl = qf // 64
    qc = qf % 64
    out = np.full((8, 20, 128, 512), -30000.0, np.float32)
    tiles = [(0, j, j) for j in range(6)] + [(3, j, 4 * 3 - 2 + j) for j in range(8)] + [(7, j, 26 + j) for j in range(6)]
    for ti, (i, j, kb) in enumerate(tiles):
        kr = 2 * kb + krl
        qr = 8 * i + qrl
        r0 = np.clip(qr - 4, 0, 56)
        c0 = np.clip(qc - 8, 0, 48)
        valid = ((kr[:, None] >= r0[None, :]) & (kr[:, None] < r0[None, :] + 8) &
                 (kc[:, None] >= c0[None, :]) & (kc[:, None] < c0[None, :] + 16))
        dr = np.clip(kr[:, None] - qr[None, :] + 7, 0, 14)
        dc = np.clip(kc[:, None] - qc[None, :] + 15, 0, 30)
        g = rpb_heads[:, dr, dc]
        out[:, ti] = np.where(valid[None], g, np.float32(-30000.0))
    return out


def prep_params(inp, core, nl=4):
    C = host_constants()
    f32 = np.float32
    m = {}
    m["gn"] = np.ascontiguousarray(inp["norm_g"].reshape(4, DB, 128).transpose(2, 0, 1))
    m["gfin"] = np.ascontiguousarray(inp["final_g"].reshape(DB, 128).T)
    for j in range((nl + 1) // 2):
        cw = inp["hy_conv_w"][j]
        cb = inp["hy_conv_b"][j]
        m[f"hyw1{j}"] = np.ascontiguousarray(inp["hy_w1"][j])
        m[f"hyb1{j}"] = np.ascontiguousarray(inp["hy_b1"][j][:, None])
        m[f"hyfr{j}"] = np.ascontiguousarray(inp["hy_freq"][j][:, None])
        m[f"hyw2{j}"] = np.ascontiguousarray(inp["hy_w2"][j])
        m[f"hyb2{j}"] = np.ascontiguousarray(inp["hy_b2"][j][:, None])
        m[f"attqg{j}"] = np.ascontiguousarray(inp["att_q_g"][j][:, None])
        m[f"attkg{j}"] = np.ascontiguousarray(inp["att_k_g"][j][:, None])
        w3 = inp["hy_w3"][j]
        for h in range(2):
            cwc = np.zeros((128, 3, 8, 3), f32)
            cbc = np.zeros((128, 3, 8), f32)
            for q in range(3):
                s0 = q * 2048 + h * 1024
                cwc[:, q] = cw[:, s0:s0 + 1024].reshape(3, 8, 128).transpose(2, 1, 0)
                cbc[:, q] = cb[s0:s0 + 1024].reshape(8, 128).T
            m[f"hycw{j}_{h}"] = cwc
            m[f"hycb{j}_{h}"] = cbc
            m[f"hyskip{j}_{h}"] = np.ascontiguousarray(inp["hy_skip"][j][h * 1024:(h + 1) * 1024].reshape(8, 128).T)
            m[f"hyw3{j}_{h}"] = np.ascontiguousarray(np.stack([w3[:, h * 1024:(h + 1) * 1024], w3[:, 2048 + h * 1024:2048 + (h + 1) * 1024]], axis=1))
    for j in range(nl // 2):
        for h in range(2):
            sl = slice(h * 1024, (h + 1) * 1024)
            m[f"lrucw{j}_{h}"] = np.ascontiguousarray(inp["lru_conv_w"][j][:, sl].reshape(4, 8, 128).transpose(2, 1, 0))
            m[f"lrucb{j}_{h}"] = np.ascontiguousarray(inp["lru_conv_b"][j][sl].reshape(8, 128).T)
            for nm, key in (("wa", "lru_wa"), ("wx", "lru_wx")):
                wgt = inp[key][j][:, h * 8:(h + 1) * 8]
                m[f"lru{nm}{j}_{h}"] = np.ascontiguousarray(wgt.transpose(2, 0, 1, 3))
            for nm, key in (("ba", "lru_ba"), ("bx", "lru_bx"), ("lam", "lru_lambda")):
                v = inp[key][j][:, sl]
                m[f"lru{nm}{j}_{h}"] = np.ascontiguousarray(v.reshape(2, 8, 128).transpose(2, 0, 1))
            m[f"nab{j}_{h}"] = na_bias_tiles(inp["na_rpb"][j][h * 8:(h + 1) * 8])
    for k in ("featsL", "featsC", "tnormL", "tnormC", "CtabL", "StabL", "CtabC", "StabC", "phiL", "phiC", "ropeC", "ropeS", "protT"):
        m[k] = C[k]
    for h in range(2):
        m[f"negdelta_{h}"] = np.ascontiguousarray(-C["deltas"][h * 1024:(h + 1) * 1024].reshape(8, 128).T)
    return m


_SHARED = {}


def prep_core(inp, core, nl=4):
    b = core
    f32 = np.float32
    m = {}
    xt = np.concatenate([inp["x"][b].T, inp["ctx"][b].T], axis=1)
    m["xT"] = np.ascontiguousarray(xt.reshape(DB, 128, NT))
    sc = np.concatenate([inp["c"], inp["c_ctx"][None, :]], axis=0)
    m["scin"] = np.ascontiguousarray(sc.reshape(5, DB, 128).transpose(2, 1, 0))
    oh = np.zeros((128, 5), f32)
    oh[:, b] = 1.0
    m["onehot"] = oh
    ncol = NMB * 128
    cols = slice(core * ncol, (core + 1) * ncol)
    m["wmod"] = np.ascontiguousarray(np.stack([inp["w_mod"][l][:, cols].reshape(DB, 128, ncol).transpose(1, 0, 2) for l in range(4)]))
    m["bmod"] = np.ascontiguousarray(np.stack([inp["b_mod"][l][cols].reshape(NMB, 128).T for l in range(4)], axis=1))
    if "w" not in _SHARED:
        w = {}
        for l in range(nl):
            j = l // 2
            for h in range(2):
                if l % 2 == 0:
                    w_in, w_out = inp["ev_w_in"][j], inp["ev_w_out"][j]
                    blocks = even_fm_cols(h)
                    v0, vn = even_tm_cols(h)
                else:
                    w_in, w_out = inp["od_w_in"][j], inp["od_w_out"][j]
                    blocks = odd_fm_cols(h)
                    v0, vn = odd_tm_cols(h)
                ncb = len(blocks)
                colidx = np.concatenate([np.arange(s0, s0 + 128) for s0 in blocks])
                wsel = w_in[:, colidx]
                w[f"wfm{l}_{h}"] = np.ascontiguousarray(wsel.reshape(DB, 128, ncb, 128).transpose(2, 1, 0, 3))
                w[f"wtm{l}_{h}"] = np.ascontiguousarray(w_in[:, v0:v0 + vn].reshape(DB, 128, vn).transpose(1, 0, 2))
                rows = np.concatenate([np.arange(h * 1024, h * 1024 + 1024), np.arange(2048 + h * 1024, 2048 + h * 1024 + 1024)])
                wo = w_out[rows]
                w[f"wout{l}_{h}"] = np.ascontiguousarray(wo.reshape(16, 128, DB, 128).transpose(2, 1, 0, 3))
        w.update(prep_params(inp, core, nl))
        _SHARED["w"] = w
    m.update(_SHARED["w"])
    return m


def kernel(**inputs):
    inp = {k: np.asarray(v) for k, v in inputs.items()}
    nc, S = build({})
    in_maps = [prep_core(inp, c) for c in range(NCORES)]
    res = run_bass_kernel_spmd(nc, in_maps, core_ids=list(range(NCORES)))
    out = np.empty((4, L, D), np.float32)
    for b in range(4):
        o = res.results[b]["outT"]
        out[b] = o.reshape(D, L).T
    _SHARED.clear()
    return out
```

```python
import math
from contextlib import ExitStack
import numpy as np
import ml_dtypes
import concourse.bass as bass
import concourse.mybir as mybir
from concourse.bass_utils import run_bass_kernel_spmd

F32 = mybir.dt.float32
BF16 = mybir.dt.bfloat16
I32 = mybir.dt.int32
ACT = mybir.ActivationFunctionType
ALU = mybir.AluOpType
AX = mybir.AxisListType

D = 4096
L = 4096
LC = 256
NT = L + LC
DB = 32
EPS = 1e-6
NCORES = 8
NRANK = 2
NMB = 96 // NRANK
NH = 1
PAIRS = [[0, 1], [2, 3], [4, 5], [6, 7]]
TWO_PI = 2.0 * math.pi


class Buf:
    __slots__ = ("name", "lw", "rd", "excl")

    def __init__(self, name="b", excl=False):
        self.name = name
        self.lw = None
        self.rd = {}
        self.excl = excl


def PBuf():
    return Buf("psum", True)


class Sched:
    NDMA = 12

    def __init__(self, nc, stack):
        self.nc = nc
        self.eng = {"pe": nc.tensor, "act": nc.scalar, "dve": nc.vector, "pool": nc.gpsimd, "sp": nc.sync}
        self.sem = {k: stack.enter_context(nc.semaphore("c_" + k)) for k in self.eng}
        self.cnt = {k: 0 for k in self.eng}
        self.dsem = {k: [stack.enter_context(nc.semaphore(f"d_{k}{i}")) for i in range(self.NDMA)]
                     for k in ("sp", "act", "pool")}
        self.dcnt = {k: 0 for k in self.dsem}
        self.ccsem = stack.enter_context(nc.semaphore("ccsem"))
        self.cccnt = 0
        self.waited = {k: {} for k in self.eng}
        self.semobj = {}
        for k in self.eng:
            self.semobj[("c", k)] = self.sem[k]
        for k in self.dsem:
            for i in range(self.NDMA):
                self.semobj[("d", k, i)] = self.dsem[k][i]
        self.semobj[("cc",)] = self.ccsem
        self.nwait = 0
        self.nins = 0

    def _wait(self, e, tok):
        if tok is None:
            return
        key, val = tok
        if key == ("c", "pe") and e == "pe":
            return
        if self.waited[e].get(key, 0) >= val:
            return
        self.eng[e].wait_ge(self.semobj[key], val)
        self.waited[e][key] = val
        self.nwait += 1

    def _deps(self, e, reads, writes):
        for b in reads:
            self._wait(e, b.lw)
        for b in writes:
            self._wait(e, b.lw)
            for t in list(b.rd.items()):
                self._wait(e, t)

    def _commit(self, tok, reads, writes):
        for b in reads:
            if b.rd.get(tok[0], 0) < tok[1]:
                b.rd[tok[0]] = tok[1]
        for b in writes:
            b.lw = tok
            b.rd = {}

    def op(self, e, fn, reads=(), writes=()):
        if any(b.excl for b in reads):
            writes = list(writes) + [b for b in reads if b.excl]
            reads = [b for b in reads if not b.excl]
        self._deps(e, reads, writes)
        self.cnt[e] += 1
        fn(self.eng[e]).then_inc(self.sem[e], 1)
        tok = (("c", e), self.cnt[e])
        self._commit(tok, reads, writes)
        self.nins += 1
        return tok

    def dma(self, e, out, in_, reads=(), writes=(), **kw):
        k = self.dcnt[e]
        slot = k % self.NDMA
        rnd = k // self.NDMA
        key = ("d", e, slot)
        if rnd > 0:
            self._wait(e, (key, 16 * rnd))
        self._deps(e, reads, writes)
        self.dcnt[e] += 1
        self.eng[e].dma_start(out=out, in_=in_, **kw).then_inc(self.dsem[e][slot], 16)
        tok = (key, 16 * (rnd + 1))
        self._commit(tok, reads, writes)
        self.nins += 1
        return tok

    def collective(self, kind, op, groups, in_ap, out_ap, reads=(), writes=()):
        e = "pool"
        self._deps(e, reads, writes)
        self.cccnt += 1
        self.eng[e].collective_compute(kind, op, replica_groups=groups, ins=[in_ap], outs=[out_ap]).then_inc(self.ccsem)
        tok = (("cc",), self.cccnt)
        self._commit(tok, reads, writes)
        return tok

    def all_tokens(self):
        toks = [(("c", k), self.cnt[k]) for k in self.eng if self.cnt[k] > 0]
        for e in self.dsem:
            k = self.dcnt[e]
            for slot in range(self.NDMA):
                n = (k - slot + self.NDMA - 1) // self.NDMA
                if n > 0:
                    toks.append((("d", e, slot), 16 * n))
        if self.cccnt:
            toks.append((("cc",), self.cccnt))
        return toks

    def barrier(self, engines=None):
        toks = self.all_tokens()
        for e in (engines or self.eng):
            for t in toks:
                if t[0] == ("c", "pe") and e == "pe":
                    continue
                self._wait(e, t)


class Phase:
    _uid = [0]

    def __init__(self, nc, S):
        self.nc = nc
        self.S = S
        self.st = ExitStack()

    def _nm(self, name):
        Phase._uid[0] += 1
        return f"{name}_{Phase._uid[0]}"

    def sb(self, name, shape, dt=F32):
        return self.st.enter_context(self.nc.sbuf_tensor(self._nm(name), shape, dt))

    def ps(self, name, shape=(128, 512), dt=F32):
        return self.st.enter_context(self.nc.psum_tensor(self._nm(name), list(shape), dt))

    def close(self):
        self.S.barrier()
        self.st.close()


class Rot:
    def __init__(self, items):
        self.items = items
        self.i = 0

    def next(self):
        it = self.items[self.i % len(self.items)]
        self.i += 1
        return it


def even_fm_cols(h):
    blocks = []
    for base in (0, 2048, 4096, 6144):
        blocks += [base + h * 1024 + i * 128 for i in range(8)]
    blocks += [8192 + h * 1024 + i * 128 for i in range(8)]
    blocks += [10240 + h * 256 + i * 128 for i in range(2)]
    blocks += [11264 + h * 1024 + i * 128 for i in range(8)]
    return blocks


def even_tm_cols(h):
    return 10752 + h * 256, 256


def odd_fm_cols(h):
    blocks = []
    for base in (0, 2048, 4096, 6144, 10240):
        blocks += [base + h * 1024 + i * 128 for i in range(8)]
    return blocks


def odd_tm_cols(h):
    return 8192 + h * 1024, 1024


NCB_EVEN = 50
NCB_ODD = 40


def build(cfg):
    nl = cfg.get("nl", 4)
    stop = cfg.get("stop", None)
    dbg = cfg.get("dbg", ())
    nc = bass.Bass("TRN2", target_bir_lowering=False)

    def din(name, shape, dt=F32):
        return nc.dram_tensor(name, list(shape), dt, kind="ExternalInput").ap()

    def dscr(name, shape, dt=F32):
        return nc.dram_tensor(name, list(shape), dt).ap()

    def dout(name, shape, dt=F32):
        return nc.dram_tensor(name, list(shape), dt, kind="ExternalOutput").ap()

    mixtest = cfg.get("mixtest", None)
    tailtest = cfg.get("tailtest", False)
    gn = din("gn", [128, 4, DB])
    gfin = din("gfin", [128, DB])
    if tailtest:
        xT_in = din("xT", [DB, 128, NT])
        modv_in = din("modv_in", [128, 2, 4, 96])
        wout_t = din("wout0", [DB, 128, 16, 128])
    elif mixtest is None:
        xT_in = din("xT", [DB, 128, NT])
        scin = din("scin", [128, DB, 5])
        onehot = din("onehot", [128, 5])
        wmod = din("wmod", [4, 128, DB, NMB * 128])
        bmod = din("bmod", [128, 4, NMB])
    wfm, wtm, wout = [], [], []
    for l in range(nl if (mixtest is None and not tailtest) else 0):
        ncb = NCB_EVEN if l % 2 == 0 else NCB_ODD
        vc = 256 if l % 2 == 0 else 1024
        wfm.append([din(f"wfm{l}_{hh}", [ncb, 128, DB, 128]) for hh in range(NH)])
        wtm.append([din(f"wtm{l}_{hh}", [128, DB, vc]) for hh in range(NH)])
        wout.append([din(f"wout{l}_{hh}", [DB, 128, 16, 128]) for hh in range(NH)])
    n_ev = (nl + 1) // 2
    n_od = nl // 2
    ev_js = range(n_ev) if mixtest is None else ([mixtest // 2] if mixtest % 2 == 0 else [])
    od_js = range(n_od) if mixtest is None else ([mixtest // 2] if mixtest % 2 == 1 else [])
    if tailtest:
        ev_js, od_js = [], []
        wout = [[wout_t, wout_t]]
    hy = {}
    for j in ev_js:
        w1_ = din(f"hyw1{j}", [33, 64]); b1_ = din(f"hyb1{j}", [64, 1]); fr_ = din(f"hyfr{j}", [64, 1])
        w2_ = din(f"hyw2{j}", [64, 64]); b2_ = din(f"hyb2{j}", [64, 1])
        qg_ = din(f"attqg{j}", [128, 1]); kg_ = din(f"attkg{j}", [128, 1])
        for hh in range(NH):
            hy[(j, hh)] = (dict(
                cw=din(f"hycw{j}_{hh}", [128, 3, 8, 3]), cb=din(f"hycb{j}_{hh}", [128, 3, 8]), skip=din(f"hyskip{j}_{hh}", [128, 8]),
                w1=w1_, b1=b1_, fr=fr_, w2=w2_, b2=b2_, w3=din(f"hyw3{j}_{hh}", [64, 2, 1024]), qg=qg_, kg=kg_))
    lru = {}
    for j in od_js:
        for hh in range(NH):
            lru[(j, hh)] = (dict(
                cw=din(f"lrucw{j}_{hh}", [128, 8, 4]), cb=din(f"lrucb{j}_{hh}", [128, 8]),
                wa=din(f"lruwa{j}_{hh}", [128, 2, 8, 128]), ba=din(f"lruba{j}_{hh}", [128, 2, 8]),
                wx=din(f"lruwx{j}_{hh}", [128, 2, 8, 128]), bx=din(f"lrubx{j}_{hh}", [128, 2, 8]),
                lam=din(f"lrulam{j}_{hh}", [128, 2, 8]), nab=din(f"nab{j}_{hh}", [8, 20, 128, 512])))
    if not tailtest:
        featsL = din("featsL", [33, L])
        featsC = din("featsC", [33, LC])
        tnormL = din("tnormL", [128, L])
        tnormC = din("tnormC", [128, LC])
        negdelta = [din(f"negdelta_{hh}", [128, 8]) for hh in range(NH)]
        CtabL = din("CtabL", [32, 128, L], BF16)
        StabL = din("StabL", [32, 128, L], BF16)
        CtabC = din("CtabC", [2, 128, LC], BF16)
        StabC = din("StabC", [2, 128, LC], BF16)
        phiL = din("phiL", [128, 3, 32])
        phiC = din("phiC", [128, 3, 2])
        ropeC = din("ropeC", [128, L])
        ropeS = din("ropeS", [128, L])
        protT = din("protT", [128, 128])
    else:
        featsL = featsC = tnormL = tnormC = negdelta = CtabL = StabL = CtabC = StabC = phiL = phiC = ropeC = ropeS = protT = None

    outT = dout("outT", [DB, 128, L])
    XT = dscr("XT", [DB, 128, NT])
    HT = dscr("HT", [17, 128, DB, 256], BF16)
    if mixtest is None:
        UF = dscr("UF", [NCB_EVEN, 128, NT])
        VT = dscr("VT", [34, 128, 1024], BF16)
    else:
        UF = din("UF_in", [NCB_EVEN, 128, NT])
        VT = din("VT_in", [34, 128, 1024], BF16)
    YT = din("YT_in", [16, 128, NT], BF16) if tailtest else dscr("YT", [16, 128, NT], BF16)
    PT = dscr("PT", [DB * 128, NT])
    RT = dscr("RT", [DB * 128, NT])
    PT3 = PT.rearrange("(a p) t -> a p t", p=128)
    RT3 = RT.rearrange("(a p) t -> a p t", p=128)
    SD = dscr("SD", [8, 128, NT])
    X0G = dscr("X0G", [8, 128, NT])
    KAL = dscr("KAL", [32, 128, 1024])
    KBL = dscr("KBL", [32, 128, 1024])
    KAC = dscr("KAC", [2, 128, 1024])
    KBC = dscr("KBC", [2, 128, 1024])
    modloc_d = dscr("modloc_d", [128, 4 * NMB * 5])
    modall_d = dscr("modall_d", [NRANK * 128, 4 * NMB * 5])
    dbg_out = {}
    if "HT" in dbg:
        dbg_out["HT"] = dout("dbg_HT", [17, 128, DB, 256], BF16)
    if "UF" in dbg:
        dbg_out["UF"] = dout("dbg_UF", [NCB_EVEN, 128, NT])
    if "VT" in dbg:
        dbg_out["VT"] = dout("dbg_VT", [34, 128, 1024], BF16)
    if "YT" in dbg:
        dbg_out["YT"] = dout("dbg_YT", [16, 128, NT], BF16)
    if "PT" in dbg:
        dbg_out["PT"] = dout("dbg_PT", [DB * 128, NT])
    if "RT" in dbg:
        dbg_out["RT"] = dout("dbg_RT", [DB * 128, NT])
    if "XT" in dbg:
        dbg_out["XT"] = dout("dbg_XT", [DB, 128, NT])
    if "MOD" in dbg:
        dbg_out["MOD"] = dout("dbg_MOD", [128, 2, 4, 96])
    if "KA" in dbg:
        dbg_out["KA"] = dout("dbg_KA", [32, 128, 1024])
        dbg_out["KB"] = dout("dbg_KB", [32, 128, 1024])
        dbg_out["KAC"] = dout("dbg_KAC", [2, 128, 1024])
    if "SD" in dbg:
        dbg_out["SD"] = dout("dbg_SD", [8, 128, NT])

    bXT = [Buf(f"XT{i}") for i in range(17)]
    bHT = [Buf(f"HT{i}") for i in range(17)]
    bUF = [Buf(f"UF{i}") for i in range(NCB_EVEN)]
    bVT = Buf("VT")
    bYT = [Buf(f"YT{i}") for i in range(16)]
    bPT = [Buf(f"PT{i}") for i in range(4)]
    bPTt = [[Buf() for _ in range(9)] for _ in range(DB)]
    bRTn = [Buf() for _ in range(DB)]
    bRT = [Buf(f"RT{i}") for i in range(4)]
    bSD = [Buf(f"SD{i}") for i in range(8)]
    bX0G = [Buf(f"X0G{i}") for i in range(8)]
    bK = Buf("K")
    bmodd = Buf("modd")

    stack = ExitStack()
    S = Sched(nc, stack)
    blk = stack.enter_context(nc.Block())

    def body(_e):
        G = Phase(nc, S)
        ones_bf = G.sb("ones_bf", [128, 128], BF16)
        ones_f = G.sb("ones_f", [128, 128])
        ident_f = G.sb("ident_f", [128, 128])
        ident_bf = G.sb("ident_bf", [128, 128], BF16)
        bconst = Buf("const")
        S.op("pool", lambda e: e.memset(ones_f[:], 1.0), writes=[bconst])
        S.op("pool", lambda e: e.memset(ident_f[:], 0.0), writes=[bconst])
        S.op("pool", lambda e: e.affine_select(out=ident_f[:], in_=ident_f[:], pattern=[[-1, 128]],
                                               compare_op=ALU.not_equal, fill=1.0, base=0, channel_multiplier=1),
             reads=[bconst], writes=[bconst])
        S.op("dve", lambda e: e.tensor_copy(out=ones_bf[:], in_=ones_f[:]), reads=[bconst], writes=[bconst])
        S.op("dve", lambda e: e.tensor_copy(out=ident_bf[:], in_=ident_f[:]), reads=[bconst], writes=[bconst])
        modv = G.sb("modv", [128, 2, 4, 96])
        gn_sb = G.sb("gn_sb", [128, 4, DB])
        gfin_sb = G.sb("gfin_sb", [128, DB])
        gs_sb = G.sb("gs_sb", [128, 2, DB])
        bmodv = Buf("modv")
        bgs = Buf("gs")
        S.dma("sp", gn_sb[:], gn[:, :, :], writes=[bconst])
        S.dma("sp", gfin_sb[:], gfin[:, :], writes=[bconst])

        def phase_mod():
            P = Phase(nc, S)
            sc = P.sb("sc", [128, DB, 5])
            oh = P.sb("oh", [128, 5])
            bm = P.sb("bm", [128, 4, NMB])
            ml = P.sb("ml", [128, 4, NMB, 5])
            ma = P.sb("ma", [128, NRANK, 4 * NMB * 5])
            tmp = P.sb("tmp", [128, NRANK, 4 * NMB])
            bsc, bml, bma, btmp = Buf(), Buf(), Buf(), Buf()
            wbufs = Rot([(P.sb(f"wm{i}", [128, DB, 384]), Buf()) for i in range(2)])
            pss = Rot([(P.ps(f"pm{i}"), PBuf()) for i in range(2)])
            S.dma("sp", sc[:], scin[:, :, :], writes=[bsc])
            S.dma("sp", oh[:], onehot[:, :], writes=[bsc])
            S.dma("sp", bm[:], bmod[:, :, :], writes=[bsc])
            S.op("act", lambda e: e.activation(out=sc[:], in_=sc[:], func=ACT.Silu), reads=[bsc], writes=[bsc])
            for l in range(4):
                for ch in range(NMB // 3):
                    w, bw = wbufs.next()
                    S.dma("sp", w[:], wmod[l, :, :, ch * 384:(ch + 1) * 384], writes=[bw])
                    for jj in range(3):
                        j = ch * 3 + jj
                        ps, bp = pss.next()
                        for db in range(DB):
                            S.op("pe", lambda e: e.matmul(ps[:, 0:5], lhsT=w[:, db, jj * 128:(jj + 1) * 128], rhs=sc[:, db, :],
                                                          start=(db == 0), stop=(db == DB - 1)),
                                 reads=[bw, bsc], writes=[bp])
                        S.op("act", lambda e: e.activation(out=ml[:, l, j, :], in_=ps[:, 0:5], func=ACT.Identity,
                                                           bias=bm[:, l, j:j + 1], scale=1.0),
                             reads=[bp, bsc], writes=[bml])
            S.dma("sp", modloc_d[:, :], ml[:].rearrange("p a b c -> p (a b c)"), reads=[bml], writes=[bmodd])
            S.collective("AllGather", ALU.bypass, PAIRS, modloc_d[:, :], modall_d[:, :],
                         reads=[bmodd], writes=[bmodd])
            S.dma("sp", ma[:], modall_d.rearrange("(r p) f -> p r f", p=128), reads=[bmodd], writes=[bma])
            mav = ma[:].rearrange("p r (lj c) -> p r lj c", c=5)
            S.op("dve", lambda e: e.tensor_scalar(out=tmp[:], in0=mav[:, :, :, 0], scalar1=oh[:, 0:1], scalar2=None, op0=ALU.mult),
                 reads=[bma, bsc], writes=[btmp])
            for c in range(1, 4):
                S.op("dve", lambda e: e.scalar_tensor_tensor(out=tmp[:], in0=mav[:, :, :, c], scalar=oh[:, c:c + 1], in1=tmp[:],
                                                             op0=ALU.mult, op1=ALU.add),
                     reads=[bma, bsc, btmp], writes=[btmp])
            for l in range(4):
                tv = tmp[:].rearrange("p r (l j) -> p r l j", j=NMB)
                S.op("dve", lambda e: e.tensor_copy(out=modv[:, 0, l, :].rearrange("p (r j) -> p r j", j=NMB), in_=tv[:, :, l, :]),
                     reads=[btmp], writes=[bmodv])
                mv4 = ma[:].rearrange("p r (l j c) -> p r l j c", j=NMB, c=5)
                S.op("dve", lambda e: e.tensor_copy(out=modv[:, 1, l, :].rearrange("p (r j) -> p r j", j=NMB), in_=mv4[:, :, l, :, 4]),
                     reads=[bma], writes=[bmodv])
            P.close()

        if tailtest:
            S.dma("sp", modv[:], modv_in, writes=[bmodv])
            S.dma("sp", modloc_d[:, :], gn_sb[:, 0:2, :].rearrange("p a b -> p (a b)")[:, 0:64], reads=[bconst], writes=[bmodd]) if False else None
            S.collective("AllGather", ALU.bypass, PAIRS, modloc_d[:, :], modall_d[:, :], reads=[bmodd], writes=[bmodd])
        elif mixtest is None:
            phase_mod()

        def phase_A(l):
            P = Phase(nc, S)
            xts = Rot([(P.sb(f"xt{i}", [128, DB, 256]), Buf()) for i in range(2)])
            rts = Rot([(P.sb(f"rt{i}", [128, DB, 256]), Buf()) for i in range(1)])
            sqs = Rot([(P.sb(f"sq{i}", [128, DB, 256], BF16), Buf()) for i in range(1)])
            hos = Rot([(P.sb(f"ho{i}", [128, DB, 256], BF16), Buf()) for i in range(2)])
            rss = Rot([(P.sb(f"rs{i}", [128, 256]), Buf()) for i in range(2)])
            pss = Rot([(P.ps(f"pa{i}"), PBuf()) for i in range(2)])
            for w in range(2):
                S.op("dve", lambda e: e.scalar_tensor_tensor(out=gs_sb[:, w, :], in0=modv[:, w, l, 32:64], scalar=1.0, in1=gn_sb[:, l, :],
                                                             op0=ALU.add, op1=ALU.mult),
                     reads=[bmodv, bconst], writes=[bgs])
            for tt in range(17):
                w = 1 if tt == 16 else 0
                xt, bx = xts.next()
                src = xT_in if l <= 1 else XT
                S.dma("sp", xt[:], src[:, :, tt * 256:(tt + 1) * 256].rearrange("a p t -> p a t"),
                      reads=[] if l <= 1 else [bXT[tt]], writes=[bx])
                if l > 0:
                    rt, br = rts.next()
                    S.dma("sp", rt[:], RT3[:, :, tt * 256:(tt + 1) * 256].rearrange("a p t -> p a t"), reads=bRTn, writes=[br])
                    for db in range(DB):
                        S.op("dve",
                             lambda e: e.scalar_tensor_tensor(out=xt[:, db, :], in0=rt[:, db, :], scalar=modv[:, w, l - 1, 64 + db:65 + db],
                                                              in1=xt[:, db, :], op0=ALU.mult, op1=ALU.add),
                             reads=[br, bx, bmodv], writes=[bx])
                    S.dma("sp", XT[:, :, tt * 256:(tt + 1) * 256].rearrange("a p t -> p a t"), xt[:], reads=[bx], writes=[bXT[tt]])
                sq, bs = sqs.next()
                S.op("act", lambda e: e.activation(out=sq[:], in_=xt[:], func=ACT.Square), reads=[bx], writes=[bs])
                ps, bp = pss.next()
                for db in range(DB):
                    S.op("pe", lambda e: e.matmul(ps[:, 0:256], lhsT=ones_bf[:], rhs=sq[:, db, :], start=(db == 0), stop=(db == DB - 1)),
                         reads=[bs, bconst], writes=[bp])
                rs, brs = rss.next()
                S.op("act", lambda e: e.activation(out=rs[:], in_=ps[:, 0:256], func=ACT.Sqrt, scale=1.0 / D, bias=EPS), reads=[bp], writes=[brs])
                S.op("dve", lambda e: e.reciprocal(out=rs[:], in_=rs[:]), reads=[brs], writes=[brs])
                ho, bh = hos.next()
                for db in range(DB):
                    S.op("dve", lambda e: e.scalar_tensor_tensor(out=xt[:, db, :], in0=xt[:, db, :], scalar=gs_sb[:, w, db:db + 1], in1=rs[:],
                                                                 op0=ALU.mult, op1=ALU.mult),
                         reads=[bx, brs, bgs], writes=[bx])
                    S.op("act", lambda e: e.activation(out=ho[:, db, :], in_=xt[:, db, :], func=ACT.Identity,
                                                       bias=modv[:, w, l, db:db + 1], scale=1.0),
                         reads=[bx, bmodv], writes=[bh])
                S.dma("sp", HT[tt, :, :, :], ho[:], reads=[bh], writes=[bHT[tt]])
            P.close()

        def phase_B(l, hh):
            even = (l % 2 == 0)
            ncb = NCB_EVEN if even else NCB_ODD
            vcols = 256 if even else 1024
            P = Phase(nc, S)
            ht = P.sb("ht", [128, 4, DB, 256], BF16)
            bht = Buf()
            ws = Rot([(P.sb(f"w{i}", [128, DB, 128], BF16), Buf()) for i in range(3)])
            wv = P.sb("wv", [128, DB, 512], BF16)
            bwv = Buf()
            stg = Rot([(P.sb(f"stg{i}", [128, 512]), Buf()) for i in range(3)])
            stgb = Rot([(P.sb(f"stgb{i}", [128, 512], BF16), Buf()) for i in range(2)])
            pss = Rot([(P.ps(f"pb{i}"), PBuf()) for i in range(4)])
            ev = [0]
            for st in range(5):
                ntile = 4 if st < 4 else 1
                for k in range(ntile):
                    S.dma("sp", ht[:, k, :, :], HT[st * 4 + k, :, :, :], reads=[bHT[st * 4 + k]], writes=[bht])
                nhalf = 2 if st < 4 else 1
                for cb in range(ncb):
                    w, bw = ws.next()
                    S.dma("pool", w[:], wfm[l][hh][cb, :, :, :], writes=[bw])
                    for hf in range(nhalf):
                        ps, bp = pss.next()
                        if st < 4:
                            n = 512
                            rhs_of = lambda db: ht[:, 2 * hf:2 * hf + 2, db, :]
                        else:
                            n = 256
                            rhs_of = lambda db: ht[:, 0, db, :]
                        for db in range(DB):
                            S.op("pe", lambda e: e.matmul(ps[:, 0:n], lhsT=w[:, db, :], rhs=rhs_of(db), start=(db == 0), stop=(db == DB - 1)),
                                 reads=[bw, bht], writes=[bp])
                        sg, bsg = stg.next()
                        ev[0] += 1
                        if ev[0] % 2 == 0:
                            S.op("act", lambda e: e.copy(out=sg[:, 0:n], in_=ps[:, 0:n]), reads=[bp], writes=[bsg])
                        else:
                            S.op("dve", lambda e: e.tensor_copy(out=sg[:, 0:n], in_=ps[:, 0:n]), reads=[bp], writes=[bsg])
                        t0 = st * 1024 + hf * 512
                        S.dma("sp", UF[cb, :, t0:t0 + n], sg[:, 0:n], reads=[bsg], writes=[bUF[cb]])
                for vch in range(vcols // min(vcols, 512)):
                    vw = min(vcols, 512)
                    S.dma("pool", wv[:, :, 0:vw], wtm[l][hh][:, :, vch * vw:(vch + 1) * vw], writes=[bwv])
                    for k in range(ntile):
                        for sub in range(2):
                            tb = (st * 4 + k) * 2 + sub
                            ps, bp = pss.next()
                            for db in range(DB):
                                S.op("pe", lambda e: e.matmul(ps[:, 0:vw], lhsT=ht[:, k, db, sub * 128:(sub + 1) * 128], rhs=wv[:, db, 0:vw],
                                                              start=(db == 0), stop=(db == DB - 1)),
                                     reads=[bwv, bht], writes=[bp])
                            sg, bsg = stgb.next()
                            ev[0] += 1
                            if ev[0] % 2 == 0:
                                S.op("act", lambda e: e.copy(out=sg[:, 0:vw], in_=ps[:, 0:vw]), reads=[bp], writes=[bsg])
                            else:
                                S.op("dve", lambda e: e.tensor_copy(out=sg[:, 0:vw], in_=ps[:, 0:vw]), reads=[bp], writes=[bsg])
                            S.dma("sp", VT[tb, :, vch * vw:(vch + 1) * vw], sg[:, 0:vw], reads=[bsg], writes=[bVT])
            P.close()

        def phase_D(l, hh):
            need_ctx = l < 3
            P = Phase(nc, S)
            yt = P.sb("yt", [128, 16, NT], BF16)
            byt = Buf()
            ws = Rot([(P.sb(f"wo{i}", [128, 16, 128], BF16), Buf()) for i in range(3)])
            stg = Rot([(P.sb(f"stg{i}", [128, 512]), Buf()) for i in range(4)])
            pvs = Rot([(P.sb(f"pv{i}", [128, 512]), Buf()) for i in range(3)])
            pss = Rot([(P.ps(f"pd{i}"), PBuf()) for i in range(4)])
            for c in range(16):
                S.dma("sp", yt[:, c, :], YT[c, :, :], reads=[bYT[c]], writes=[byt])
            ev = 0
            wq = []
            for nb0 in range(2):
                w, bw = ws.next()
                S.dma("pool", w[:], wout[l][hh][nb0, :, :, :], writes=[bw])
                wq.append((w, bw))
            ntt = 9 if need_ctx else 8
            for nb in range(DB):
                if nb + 2 < DB:
                    w2, bw2 = ws.next()
                    S.dma("pool", w2[:], wout[l][hh][nb + 2, :, :, :], writes=[bw2])
                    wq.append((w2, bw2))
                w, bw = wq[nb]
                for tt in range(ntt):
                    n = 512 if tt < 8 else 256
                    ps, bp = pss.next()
                    for c in range(16):
                        S.op("pe", lambda e: e.matmul(ps[:, 0:n], lhsT=w[:, c, :], rhs=yt[:, c, tt * 512:tt * 512 + n], start=(c == 0), stop=(c == 15)),
                             reads=[bw, byt], writes=[bp])
                    sg, bsg = stg.next()
                    ev += 1
                    if hh == 1:
                        pv, bpv = pvs.next()
                        S.dma("sp", pv[:, 0:n], PT3[nb, :, tt * 512:tt * 512 + n], reads=[bPTt[nb][tt]], writes=[bpv])
                        S.op("dve", lambda e: e.tensor_tensor(out=sg[:, 0:n], in0=ps[:, 0:n], in1=pv[:, 0:n], op=ALU.add), reads=[bp, bpv], writes=[bsg])
                    elif ev % 2 == 0:
                        S.op("act", lambda e: e.copy(out=sg[:, 0:n], in_=ps[:, 0:n]), reads=[bp], writes=[bsg])
                    else:
                        S.op("dve", lambda e: e.tensor_copy(out=sg[:, 0:n], in_=ps[:, 0:n]), reads=[bp], writes=[bsg])
                    S.dma("sp", PT3[nb, :, tt * 512:tt * 512 + n], sg[:, 0:n], reads=[bsg], writes=[bPTt[nb][tt], bPT[nb // 8]])
                if hh == NH - 1 and NRANK > 1:
                    S.collective("AllReduce", ALU.add, PAIRS, PT[nb * 128:(nb + 1) * 128, :], RT[nb * 128:(nb + 1) * 128, :],
                                 reads=[bPTt[nb][tt] for tt in range(ntt)], writes=[bRTn[nb]])
            P.close()

        def phase_final():
            l = 4
            P = Phase(nc, S)
            xts = Rot([(P.sb(f"xt{i}", [128, DB, 256]), Buf()) for i in range(2)])
            rts = Rot([(P.sb(f"rt{i}", [128, DB, 256]), Buf()) for i in range(2)])
            sqs = Rot([(P.sb(f"sq{i}", [128, DB, 256], BF16), Buf()) for i in range(1)])
            rss = Rot([(P.sb(f"rs{i}", [128, 256]), Buf()) for i in range(2)])
            pss = Rot([(P.ps(f"pa{i}"), PBuf()) for i in range(2)])
            for tt in range(16):
                xt, bx = xts.next()
                S.dma("sp", xt[:], XT[:, :, tt * 256:(tt + 1) * 256].rearrange("a p t -> p a t"), reads=[bXT[tt]], writes=[bx])
                rt, br = rts.next()
                S.dma("sp", rt[:], RT3[:, :, tt * 256:(tt + 1) * 256].rearrange("a p t -> p a t"), reads=bRTn, writes=[br])
                for db in range(DB):
                    S.op("dve",
                         lambda e: e.scalar_tensor_tensor(out=xt[:, db, :], in0=rt[:, db, :], scalar=modv[:, 0, l - 1, 64 + db:65 + db],
                                                          in1=xt[:, db, :], op0=ALU.mult, op1=ALU.add),
                         reads=[br, bx, bmodv], writes=[bx])
                sq, bs = sqs.next()
                S.op("act", lambda e: e.activation(out=sq[:], in_=xt[:], func=ACT.Square), reads=[bx], writes=[bs])
                ps, bp = pss.next()
                for db in range(DB):
                    S.op("pe", lambda e: e.matmul(ps[:, 0:256], lhsT=ones_bf[:], rhs=sq[:, db, :], start=(db == 0), stop=(db == DB - 1)),
                         reads=[bs, bconst], writes=[bp])
                rs, brs = rss.next()
                S.op("act", lambda e: e.activation(out=rs[:], in_=ps[:, 0:256], func=ACT.Sqrt, scale=1.0 / D, bias=EPS), reads=[bp], writes=[brs])
                S.op("dve", lambda e: e.reciprocal(out=rs[:], in_=rs[:]), reads=[brs], writes=[brs])
                for db in range(DB):
                    S.op("dve", lambda e: e.scalar_tensor_tensor(out=xt[:, db, :], in0=xt[:, db, :], scalar=gfin_sb[:, db:db + 1], in1=rs[:],
                                                                 op0=ALU.mult, op1=ALU.mult),
                         reads=[bx, brs, bconst], writes=[bx])
                S.dma("sp", outT[:, :, tt * 256:(tt + 1) * 256].rearrange("a p t -> p a t"), xt[:], reads=[bx], writes=[Buf()])
            P.close()

        import types
        E = types.SimpleNamespace(
            nc=nc, S=S, cfg=cfg, ones_bf=ones_bf, ones_f=ones_f, ident_f=ident_f, ident_bf=ident_bf, bconst=bconst,
            modv=modv, bmodv=bmodv, UF=UF, bUF=bUF, VT=VT, bVT=bVT, YT=YT, bYT=bYT, SD=SD, bSD=bSD, X0G=X0G, bX0G=bX0G,
            KAL=KAL, KBL=KBL, KAC=KAC, KBC=KBC, bK=bK, hy=hy, lru=lru, featsL=featsL, featsC=featsC, tnormL=tnormL,
            tnormC=tnormC, negdelta=negdelta, CtabL=CtabL, StabL=StabL, CtabC=CtabC, StabC=StabC, phiL=phiL, phiC=phiC,
            ropeC=ropeC, ropeS=ropeS, protT=protT)
        MIX = build_mixers(E)

        def dump(name, src, buf_reads):
            if name in dbg_out:
                S.dma("sp", dbg_out[name], src, reads=buf_reads)

        done = False
        if tailtest:
            phase_D(0, 0)
            phase_D(0, 1)
            phase_A(1)
            phase_final()
            done = True
            nl_loop = 0
        if mixtest is not None:
            (MIX["even"] if mixtest % 2 == 0 else MIX["odd"])(mixtest, 0)
            done = True
        for l in range(nl if (mixtest is None and not tailtest) else 0):
            phase_A(l)
            for hh in range(NH):
                phase_B(l, hh)
                (MIX["even"] if l % 2 == 0 else MIX["odd"])(l, hh)
                phase_D(l, hh)
            if done:
                break
        if not done:
            phase_final()
        S.barrier()
        if "MOD" in dbg_out:
            S.dma("sp", dbg_out["MOD"], modv[:], reads=[bmodv])
        dump("HT", HT, bHT)
        dump("UF", UF, bUF)
        dump("VT", VT, [bVT])
        dump("YT", YT, bYT)
        dump("PT", PT, bPT)
        dump("RT", RT, bRTn)
        dump("XT", XT, bXT)
        dump("SD", SD, bSD)
        if "KA" in dbg_out:
            S.dma("sp", dbg_out["KA"], KAL, reads=[bK])
            S.dma("sp", dbg_out["KB"], KBL, reads=[bK])
            S.dma("sp", dbg_out["KAC"], KAC, reads=[bK])
        S.barrier()
        G.st.close()

    blk.sync(body)
    stack.close()
    return nc, S


def _evac(S, k, out, in_, reads, writes):
    if k % 2 == 0:
        S.op("act", lambda e: e.copy(out=out, in_=in_), reads=reads, writes=writes)
    else:
        S.op("dve", lambda e: e.tensor_copy(out=out, in_=in_), reads=reads, writes=writes)


def build_mixers(E):
    nc, S = E.nc, E.S
    ISQ = 1.0 / math.sqrt(128.0)

    def hyena(l, hh):
        j = l // 2
        need_ctx = l < 3
        hp = E.hy[(j, hh)]
        PH = Phase(nc, S)
        cw = PH.sb("cw", [128, 3, 8, 3]); cb = PH.sb("cb", [128, 3, 8]); skip = PH.sb("skip", [128, 8]); negd = PH.sb("negd", [128, 8])
        w1 = PH.sb("w1", [33, 64]); b1 = PH.sb("b1", [64, 1]); fr = PH.sb("fr", [64, 1]); w2 = PH.sb("w2", [64, 64]); b2 = PH.sb("b2", [64, 1])
        w3 = PH.sb("w3", [64, 2, 1024]); om2 = PH.sb("om2", [64, 1])
        bpar = Buf()
        for dst, src in ((cw, hp["cw"]), (cb, hp["cb"]), (skip, hp["skip"]), (negd, E.negdelta[hh]), (w1, hp["w1"]), (b1, hp["b1"]),
                         (fr, hp["fr"]), (w2, hp["w2"]), (b2, hp["b2"]), (w3, hp["w3"])):
            S.dma("sp", dst[:], src, writes=[bpar])
        S.op("dve", lambda e: e.tensor_scalar(out=om2[:], in0=fr[:], scalar1=1.0 / TWO_PI, scalar2=None, op0=ALU.mult), reads=[bpar], writes=[bpar])

        def transpose_rows(src_bf, bsrc, dstT, bdst, ntb, col0, pst):
            tb = 0
            while tb < ntb:
                n = min(4, ntb - tb)
                pT, bpT = pst.next()
                for q in range(n):
                    S.op("pe", lambda e: e.transpose(out=pT[:, q * 128:(q + 1) * 128], in_=src_bf[:, (tb + q) * 128:(tb + q + 1) * 128], identity=E.ident_f[:]),
                         reads=[bsrc, E.bconst], writes=[bpT])
                _evac(S, tb // 4, dstT[:, tb:tb + n, col0:col0 + 128], pT[:, 0:n * 128].rearrange("p (a c) -> p a c", c=128), [bpT], [bdst])
                tb += n

        def filt(ctx):
            Lq = LC if ctx else L
            CH = min(512, Lq)
            nchunk = Lq // CH
            nb = Lq // 128
            feats, tnorm = (E.featsC, E.tnormC) if ctx else (E.featsL, E.tnormL)
            Ctab, Stab, phi = (E.CtabC, E.StabC, E.phiC) if ctx else (E.CtabL, E.StabL, E.phiL)
            KA, KB = (E.KAC, E.KBC) if ctx else (E.KAL, E.KBL)
            FW = 256
            PF = Phase(nc, S)
            ft = PF.sb("ft", [33, Lq]); tn = PF.sb("tn", [128, Lq]); hdn2 = PF.sb("hdn2", [64, Lq]); phis = PF.sb("phis", [128, 3, nb])
            bft = Buf(); bh2 = Buf()
            S.dma("sp", ft[:], feats, writes=[bft]); S.dma("sp", tn[:], tnorm, writes=[bft]); S.dma("sp", phis[:], phi, writes=[bft])
            P1 = Phase(nc, S)
            hdn1 = P1.sb("hdn1", [64, CH]); u = P1.sb("u", [64, CH]); ki = P1.sb("ki", [64, CH], I32); kf = P1.sb("kf", [64, CH])
            bh1, bu, bki = Buf(), Buf(), Buf()
            pss = Rot([(P1.ps("pf"), PBuf()) for _ in range(2)])

            def sin_layer(ps, bp, bvec, out_ap, bout):
                S.op("dve", lambda e: e.tensor_scalar(out=u[:], in0=ps[0:64, 0:CH], scalar1=bvec[:, 0:1], scalar2=om2[:, 0:1], op0=ALU.add, op1=ALU.mult),
                     reads=[bp, bpar], writes=[bu])
                S.op("dve", lambda e: e.tensor_copy(out=ki[:], in_=u[:]), reads=[bu], writes=[bki])
                S.op("dve", lambda e: e.tensor_copy(out=kf[:], in_=ki[:]), reads=[bki], writes=[bki])
                S.op("dve", lambda e: e.tensor_tensor(out=u[:], in0=u[:], in1=kf[:], op=ALU.subtract), reads=[bu, bki], writes=[bu])
                S.op("act", lambda e: e.activation(out=out_ap, in_=u[:], func=ACT.Sin, scale=TWO_PI), reads=[bu], writes=[bout])

            for ch in range(nchunk):
                ps, bp = pss.next()
                S.op("pe", lambda e: e.matmul(ps[0:64, 0:CH], lhsT=w1[:, :], rhs=ft[:, ch * CH:(ch + 1) * CH], start=True, stop=True), reads=[bpar, bft], writes=[bp])
                sin_layer(ps, bp, b1, hdn1[:], bh1)
                ps, bp = pss.next()
                S.op("pe", lambda e: e.matmul(ps[0:64, 0:CH], lhsT=w2[:, :], rhs=hdn1[:], start=True, stop=True), reads=[bpar, bh1], writes=[bp])
                sin_layer(ps, bp, b2, hdn2[:, ch * CH:(ch + 1) * CH], bh2)
            P1.close()
            fstop = E.cfg.get("fstop", 99)
            for G in range(2 if fstop > 1 else 0):
                PG = Phase(nc, S)
                ksT = PG.sb("ksT", [128, nb, 512], BF16); kdT = PG.sb("kdT", [128, nb, 512], BF16)
                bks, bkd = Buf(), Buf()
                P2 = Phase(nc, S)
                hf = P2.sb("hf", [128, Lq]); hb = P2.sb("hb", [128, Lq]); dec = P2.sb("dec", [128, CH]); ksb = P2.sb("ksb", [128, Lq])
                kdb = P2.sb("kdb", [128, Lq]); nrm = P2.sb("nrm", [128, 4])
                bhf, bhb, bdec, bksb, bkdb, bnrm = Buf(), Buf(), Buf(), Buf(), Buf(), Buf()
                pss = Rot([(P2.ps("pg"), PBuf()) for _ in range(2)])
                pst = Rot([(P2.ps("pt", (128, 512), F32), PBuf()) for _ in range(2)])
                for cq in range(4):
                    cbk = G * 4 + cq
                    for ch in range(nchunk):
                        sl = slice(ch * CH, (ch + 1) * CH)
                        S.op("act", lambda e: e.activation(out=dec[:], in_=tn[:, sl], func=ACT.Exp, scale=negd[:, cbk:cbk + 1]), reads=[bft, bpar], writes=[bdec])
                        for fb, (dst, bd) in enumerate(((hf, bhf), (hb, bhb))):
                            ps, bp = pss.next()
                            S.op("pe", lambda e: e.matmul(ps[:, 0:CH], lhsT=w3[:, fb, cbk * 128:(cbk + 1) * 128], rhs=hdn2[:, sl], start=True, stop=True),
                                 reads=[bpar, bh2], writes=[bp])
                            S.op("dve", lambda e: e.tensor_tensor(out=dst[:, sl], in0=ps[:, 0:CH], in1=dec[:], op=ALU.mult), reads=[bp, bdec], writes=[bd])
                    p2stop = E.cfg.get("p2stop", 99)
                    if p2stop <= 1:
                        continue
                    S.op("dve", lambda e: e.tensor_reduce(out=nrm[:, 0:1], in_=hf[:, :], axis=AX.X, op=ALU.add, apply_absolute_value=True), reads=[bhf], writes=[bnrm])
                    S.op("dve", lambda e: e.tensor_reduce(out=nrm[:, 1:2], in_=hb[:, 1:Lq], axis=AX.X, op=ALU.add, apply_absolute_value=True), reads=[bhb], writes=[bnrm])
                    S.op("dve", lambda e: e.tensor_tensor(out=nrm[:, 2:3], in0=nrm[:, 0:1], in1=nrm[:, 1:2], op=ALU.add), reads=[bnrm], writes=[bnrm])
                    S.op("dve", lambda e: e.reciprocal(out=nrm[:, 3:4], in_=nrm[:, 2:3]), reads=[bnrm], writes=[bnrm])
                    S.op("act", lambda e: e.activation(out=hf[:], in_=hf[:], func=ACT.Identity, scale=nrm[:, 3:4]), reads=[bhf, bnrm], writes=[bhf])
                    S.op("act", lambda e: e.activation(out=hb[:], in_=hb[:], func=ACT.Identity, scale=nrm[:, 3:4]), reads=[bhb, bnrm], writes=[bhb])
                    if p2stop <= 2:
                        continue
                    S.op("dve", lambda e: e.tensor_tensor(out=ksb[:, 0:Lq - 1], in0=hf[:, 0:Lq - 1], in1=hb[:, 1:Lq], op=ALU.add), reads=[bhf, bhb], writes=[bksb])
                    S.op("dve", lambda e: e.tensor_copy(out=ksb[:, Lq - 1:Lq], in_=hf[:, Lq - 1:Lq]), reads=[bhf], writes=[bksb])
                    S.op("pool", lambda e: e.tensor_tensor(out=kdb[:, 0:Lq - 1], in0=hf[:, 0:Lq - 1], in1=hb[:, 1:Lq], op=ALU.subtract), reads=[bhf, bhb], writes=[bkdb])
                    S.op("pool", lambda e: e.tensor_copy(out=kdb[:, Lq - 1:Lq], in_=hf[:, Lq - 1:Lq]), reads=[bhf], writes=[bkdb])
                    if p2stop <= 3:
                        continue
                    transpose_rows(ksb, bksb, ksT, bks, nb, cq * 128, pst)
                    transpose_rows(kdb, bkdb, kdT, bkd, nb, cq * 128, pst)
                P2.close()
                if fstop <= 2:
                    PG.close()
                    continue
                P3 = Phase(nc, S)
                Cs = Rot([(P3.sb("Cc", [128, nb, FW], BF16), Buf()) for _ in range(2)])
                Ss = Rot([(P3.sb("Sc", [128, nb, FW], BF16), Buf()) for _ in range(2)])
                t1 = P3.sb("t1", [128, 512]); t2 = P3.sb("t2", [128, 512]); bt1, bt2 = Buf(), Buf()
                oAs = Rot([(P3.sb("oA", [128, 512]), Buf()) for _ in range(2)])
                oBs = Rot([(P3.sb("oB", [128, 512]), Buf()) for _ in range(2)])
                pA = Rot([(P3.ps("pA"), PBuf()) for _ in range(2)])
                pB = Rot([(P3.ps("pB"), PBuf()) for _ in range(2)])
                for fg in range(Lq // FW):
                    Cc, bC = Cs.next(); Sc, bS = Ss.next()
                    S.dma("sp", Cc[:], Ctab[:, :, fg * FW:(fg + 1) * FW].rearrange("a p f -> p a f"), writes=[bC])
                    S.dma("sp", Sc[:], Stab[:, :, fg * FW:(fg + 1) * FW].rearrange("a p f -> p a f"), writes=[bS])
                    for fq in range(FW // 128):
                        fb = fg * (FW // 128) + fq
                        psA, bpA = pA.next(); psB, bpB = pB.next()
                        for tb in range(nb):
                            S.op("pe", lambda e: e.matmul(psA[:, :], lhsT=Cc[:, tb, fq * 128:(fq + 1) * 128], rhs=ksT[:, tb, :], start=(tb == 0), stop=(tb == nb - 1)),
                                 reads=[bC, bks], writes=[bpA])
                        for tb in range(nb):
                            S.op("pe", lambda e: e.matmul(psB[:, :], lhsT=Sc[:, tb, fq * 128:(fq + 1) * 128], rhs=kdT[:, tb, :], start=(tb == 0), stop=(tb == nb - 1)),
                                 reads=[bS, bkd], writes=[bpB])
                        oA, boA = oAs.next(); oB, boB = oBs.next()
                        S.op("act", lambda e: e.activation(out=t1[:], in_=psB[:, :], func=ACT.Identity, scale=phis[:, 1, fb:fb + 1]), reads=[bpB, bft], writes=[bt1])
                        S.op("dve", lambda e: e.scalar_tensor_tensor(out=oA[:], in0=psA[:, :], scalar=phis[:, 0, fb:fb + 1], in1=t1[:], op0=ALU.mult, op1=ALU.add),
                             reads=[bpA, bt1, bft], writes=[boA])
                        S.op("act", lambda e: e.activation(out=t2[:], in_=psA[:, :], func=ACT.Identity, scale=phis[:, 2, fb:fb + 1]), reads=[bpA, bft], writes=[bt2])
                        S.op("dve", lambda e: e.scalar_tensor_tensor(out=oB[:], in0=psB[:, :], scalar=phis[:, 0, fb:fb + 1], in1=t2[:], op0=ALU.mult, op1=ALU.add),
                             reads=[bpB, bt2, bft], writes=[boB])
                        S.dma("sp", KA[fb, :, G * 512:(G + 1) * 512], oA[:], reads=[boA], writes=[E.bK])
                        S.dma("sp", KB[fb, :, G * 512:(G + 1) * 512], oB[:], reads=[boB], writes=[E.bK])
                P3.close()
                PG.close()
            PF.close()

        filt(False)
        if need_ctx and E.cfg.get("fstop", 99) > 3:
            filt(True)

        def data(G):
            PB = Phase(nc, S)
            Ya = PB.sb("Ya", [128, 32, 512], BF16); Yb = PB.sb("Yb", [128, 32, 512], BF16)
            Yac = PB.sb("Yac", [128, 2, 512], BF16); Ybc = PB.sb("Ybc", [128, 2, 512], BF16)
            bYa, bYb, bYac, bYbc = Buf(), Buf(), Buf(), Buf()
            PD = Phase(nc, S)
            sT = PD.sb("sT", [128, 34, 512], BF16); bsT = Buf()
            PA = Phase(nc, S)
            us = Rot([(PA.sb("u", [128, NT]), Buf()) for _ in range(2)])
            accs = Rot([(PA.sb("acc", [128, NT]), Buf()) for _ in range(2)])
            pst = Rot([(PA.ps("pt", (128, 512), F32), PBuf()) for _ in range(2)])

            def conv(q, cbk):
                u, bu = us.next()
                S.dma("sp", u[:], E.UF[q * 8 + cbk, :, :], reads=[E.bUF[q * 8 + cbk]], writes=[bu])
                a, ba = accs.next()
                S.op("act", lambda e: e.activation(out=a[:], in_=u[:], func=ACT.Identity, scale=cw[:, q, cbk, 1:2], bias=cb[:, q, cbk:cbk + 1]),
                     reads=[bu, bpar], writes=[ba])
                for lo, hi in ((0, L), (L, NT)):
                    S.op("dve", lambda e: e.scalar_tensor_tensor(out=a[:, lo + 1:hi], in0=u[:, lo:hi - 1], scalar=cw[:, q, cbk, 0:1], in1=a[:, lo + 1:hi],
                                                                 op0=ALU.mult, op1=ALU.add), reads=[bu, ba, bpar], writes=[ba])
                    S.op("dve", lambda e: e.scalar_tensor_tensor(out=a[:, lo:hi - 1], in0=u[:, lo + 1:hi], scalar=cw[:, q, cbk, 2:3], in1=a[:, lo:hi - 1],
                                                                 op0=ALU.mult, op1=ALU.add), reads=[bu, ba, bpar], writes=[ba])
                return a, ba

            for cq in range(4):
                cbk = G * 4 + cq
                x1c, bx1 = conv(1, cbk)
                vc, bv = conv(2, cbk)
                S.op("pool", lambda e: e.tensor_tensor(out=x1c[:], in0=x1c[:], in1=vc[:], op=ALU.mult), reads=[bx1, bv], writes=[bx1])
                S.dma("sp", E.SD[cbk, :, :], x1c[:], reads=[bx1], writes=[E.bSD[cbk]])
                transpose_rows(x1c, bx1, sT, bsT, 34, cq * 128, pst)
                x0c, bx0 = conv(0, cbk)
                u, bu = us.next()
                S.dma("sp", u[:], E.UF[24 + cbk, :, :], reads=[E.bUF[24 + cbk]], writes=[bu])
                S.op("act", lambda e: e.activation(out=u[:], in_=u[:], func=ACT.Silu), reads=[bu], writes=[bu])
                S.op("dve", lambda e: e.tensor_tensor(out=x0c[:], in0=x0c[:], in1=u[:], op=ALU.mult), reads=[bx0, bu], writes=[bx0])
                S.dma("sp", E.X0G[cbk, :, :], x0c[:], reads=[bx0], writes=[E.bX0G[cbk]])
            PA.close()
            P1 = Phase(nc, S)
            FW = 256
            Cs = Rot([(P1.sb("Cc", [128, 32, FW], BF16), Buf()) for _ in range(2)])
            Ss = Rot([(P1.sb("Sc", [128, 32, FW], BF16), Buf()) for _ in range(2)])
            kas = Rot([(P1.sb("ka", [128, 512]), Buf()) for _ in range(2)])
            kbs = Rot([(P1.sb("kb", [128, 512]), Buf()) for _ in range(2)])
            tt = [(P1.sb(f"t{i}", [128, 512]), Buf()) for i in range(4)]
            pA = Rot([(P1.ps("pA"), PBuf()) for _ in range(2)])
            pB = Rot([(P1.ps("pB"), PBuf()) for _ in range(2)])

            def spectral(nb, tb0, Ctab, Stab, KA, KB, Yra, bYra, Yrb, bYrb, FWc):
                for fg in range((nb * 128) // FWc):
                    Cc, bC = Cs.next(); Sc, bS = Ss.next()
                    S.dma("sp", Cc[:, 0:nb, 0:FWc], Ctab[:, :, fg * FWc:(fg + 1) * FWc].rearrange("a p f -> p a f"), writes=[bC])
                    S.dma("sp", Sc[:, 0:nb, 0:FWc], Stab[:, :, fg * FWc:(fg + 1) * FWc].rearrange("a p f -> p a f"), writes=[bS])
                    for fq in range(FWc // 128):
                        fb = fg * (FWc // 128) + fq
                        psA, bpA = pA.next(); psB, bpB = pB.next()
                        for tb in range(nb):
                            S.op("pe", lambda e: e.matmul(psA[:, :], lhsT=Cc[:, tb, fq * 128:(fq + 1) * 128], rhs=sT[:, tb0 + tb, :], start=(tb == 0), stop=(tb == nb - 1)),
                                 reads=[bC, bsT], writes=[bpA])
                        for tb in range(nb):
                            S.op("pe", lambda e: e.matmul(psB[:, :], lhsT=Sc[:, tb, fq * 128:(fq + 1) * 128], rhs=sT[:, tb0 + tb, :], start=(tb == 0), stop=(tb == nb - 1)),
                                 reads=[bS, bsT], writes=[bpB])
                        ka, bka = kas.next(); kb, bkb = kbs.next()
                        S.dma("sp", ka[:], KA[fb, :, G * 512:(G + 1) * 512], reads=[E.bK], writes=[bka])
                        S.dma("sp", kb[:], KB[fb, :, G * 512:(G + 1) * 512], reads=[E.bK], writes=[bkb])
                        (t1, b1_), (t2, b2_), (t3, b3_), (t4, b4_) = tt
                        S.op("dve", lambda e: e.tensor_tensor(out=t1[:], in0=psA[:, :], in1=ka[:], op=ALU.mult), reads=[bpA, bka], writes=[b1_])
                        S.op("dve", lambda e: e.tensor_tensor(out=t2[:], in0=psB[:, :], in1=kb[:], op=ALU.mult), reads=[bpB, bkb], writes=[b2_])
                        S.op("pool", lambda e: e.tensor_tensor(out=Yra[:, fb, :], in0=t1[:], in1=t2[:], op=ALU.subtract), reads=[b1_, b2_], writes=[bYra])
                        S.op("dve", lambda e: e.tensor_tensor(out=t3[:], in0=psA[:, :], in1=kb[:], op=ALU.mult), reads=[bpA, bkb], writes=[b3_])
                        S.op("dve", lambda e: e.tensor_tensor(out=t4[:], in0=psB[:, :], in1=ka[:], op=ALU.mult), reads=[bpB, bka], writes=[b4_])
                        S.op("pool", lambda e: e.tensor_tensor(out=Yrb[:, fb, :], in0=t3[:], in1=t4[:], op=ALU.add), reads=[b3_, b4_], writes=[bYrb])

            spectral(32, 0, E.CtabL, E.StabL, E.KAL, E.KBL, Ya, bYa, Yb, bYb, FW)
            if need_ctx:
                spectral(2, 32, E.CtabC, E.StabC, E.KAC, E.KBC, Yac, bYac, Ybc, bYbc, 256)
            P1.close()
            PD.close()
            PC = Phase(nc, S)
            Cs = Rot([(PC.sb("Cr", [128, 16, 512], BF16), Buf()) for _ in range(2)])
            Ss = Rot([(PC.sb("Sr", [128, 16, 512], BF16), Buf()) for _ in range(2)])
            sts = Rot([(PC.sb("s_t", [128, 512]), Buf()) for _ in range(2)])
            xts = Rot([(PC.sb("x_t", [128, 512]), Buf()) for _ in range(2)])
            tms = Rot([(PC.sb("tmp", [128, 512]), Buf()) for _ in range(2)])
            yos = Rot([(PC.sb("yo", [128, 512], BF16), Buf()) for _ in range(2)])
            pys = Rot([(PC.ps("py"), PBuf()) for _ in range(8)])

            def epilogue(cq, ps, bp, t0, n):
                cbk = G * 4 + cq
                s_t, bs_ = sts.next(); x_t, bx_ = xts.next(); tmp, btm = tms.next(); yo, byo = yos.next()
                S.dma("sp", s_t[:, 0:n], E.SD[cbk, :, t0:t0 + n], reads=[E.bSD[cbk]], writes=[bs_])
                S.dma("sp", x_t[:, 0:n], E.X0G[cbk, :, t0:t0 + n], reads=[E.bX0G[cbk]], writes=[bx_])
                S.op("dve", lambda e: e.scalar_tensor_tensor(out=tmp[:, 0:n], in0=s_t[:, 0:n], scalar=skip[:, cbk:cbk + 1], in1=ps[:, 0:n], op0=ALU.mult, op1=ALU.add),
                     reads=[bs_, bp, bpar], writes=[btm])
                S.op("pool", lambda e: e.tensor_tensor(out=yo[:, 0:n], in0=tmp[:, 0:n], in1=x_t[:, 0:n], op=ALU.mult), reads=[btm, bx_], writes=[byo])
                S.dma("sp", E.YT[cbk, :, t0:t0 + n], yo[:, 0:n], reads=[byo], writes=[E.bYT[cbk]])

            for tc in range(8):
                pl = [pys.next() for _ in range(4)]
                for half in range(2):
                    Cr, bC = Cs.next(); Sr, bS = Ss.next()
                    S.dma("sp", Cr[:], E.CtabL[half * 16:(half + 1) * 16, :, tc * 512:(tc + 1) * 512].rearrange("a p t -> p a t"), writes=[bC])
                    S.dma("sp", Sr[:], E.StabL[half * 16:(half + 1) * 16, :, tc * 512:(tc + 1) * 512].rearrange("a p t -> p a t"), writes=[bS])
                    for cq in range(4):
                        ps, bp = pl[cq]
                        for fbl in range(16):
                            fb = half * 16 + fbl
                            S.op("pe", lambda e: e.matmul(ps[:, :], lhsT=Ya[:, fb, cq * 128:(cq + 1) * 128], rhs=Cr[:, fbl, :], start=(fb == 0), stop=False),
                                 reads=[bYa, bC], writes=[bp])
                            S.op("pe", lambda e: e.matmul(ps[:, :], lhsT=Yb[:, fb, cq * 128:(cq + 1) * 128], rhs=Sr[:, fbl, :], start=False, stop=(fb == 31)),
                                 reads=[bYb, bS], writes=[bp])
                for cq in range(4):
                    epilogue(cq, pl[cq][0], pl[cq][1], tc * 512, 512)
            if need_ctx:
                Cr, bC = Cs.next(); Sr, bS = Ss.next()
                S.dma("sp", Cr[:, 0:2, 0:256], E.CtabC.rearrange("a p t -> p a t"), writes=[bC])
                S.dma("sp", Sr[:, 0:2, 0:256], E.StabC.rearrange("a p t -> p a t"), writes=[bS])
                for cq in range(4):
                    ps, bp = pys.next()
                    for fb in range(2):
                        S.op("pe", lambda e: e.matmul(ps[:, 0:256], lhsT=Yac[:, fb, cq * 128:(cq + 1) * 128], rhs=Cr[:, fb, 0:256], start=(fb == 0), stop=False),
                             reads=[bYac, bC], writes=[bp])
                        S.op("pe", lambda e: e.matmul(ps[:, 0:256], lhsT=Ybc[:, fb, cq * 128:(cq + 1) * 128], rhs=Sr[:, fb, 0:256], start=False, stop=(fb == 1)),
                             reads=[bYbc, bS], writes=[bp])
                    epilogue(cq, ps, bp, L, 256)
            PC.close()
            PB.close()

        if not E.cfg.get("hy_filter_only", False):
            data(0)
            data(1)
        PH.close()

    def attend(qT, bq, t0, n, kblocks, lhsK, bk, lhsV, bv, gate, bg, ycb, pS, pO, pZ, pTs, wk, emul=None):
        psO, bpO = pO.next(); psZ, bpZ = pZ.next()
        nk = len(kblocks)
        for i, kb in enumerate(kblocks):
            psS, bpS = pS.next()
            S.op("pe", lambda e: e.matmul(psS[:, 0:n], lhsT=lhsK(kb), rhs=qT[:, t0:t0 + n], start=True, stop=True), reads=[bk, bq], writes=[bpS])
            pT, bpT = pTs.next()
            em = emul(kb) if emul is not None else None
            if em is None:
                S.op("act", lambda e: e.activation(out=pT[:, 0:n], in_=psS[:, 0:n], func=ACT.Exp, scale=ISQ), reads=[bpS], writes=[bpT])
            else:
                pf, bpf = wk["pf"].next()
                S.op("act", lambda e: e.activation(out=pf[:, 0:n], in_=psS[:, 0:n], func=ACT.Exp, scale=ISQ), reads=[bpS], writes=[bpf])
                S.op("dve" if i % 2 == 0 else "pool", lambda e: e.tensor_tensor(out=pT[:, 0:n], in0=pf[:, 0:n], in1=em[0], op=ALU.mult), reads=[bpf, em[1]], writes=[bpT])
            S.op("pe", lambda e: e.matmul(psO[:, 0:n], lhsT=lhsV(kb), rhs=pT[:, 0:n], start=(i == 0), stop=(i == nk - 1)), reads=[bv, bpT], writes=[bpO])
            S.op("pe", lambda e: e.matmul(psZ[:, 0:n], lhsT=E.ones_bf[:], rhs=pT[:, 0:n], start=(i == 0), stop=(i == nk - 1)), reads=[E.bconst, bpT], writes=[bpZ])
        rz, brz = wk["rz"].next(); o1, bo1 = wk["o1"].next(); yo, byo = wk["yo"].next()
        S.op("dve", lambda e: e.reciprocal(out=rz[:, 0:n], in_=psZ[:, 0:n]), reads=[bpZ], writes=[brz])
        S.op("dve", lambda e: e.tensor_tensor(out=o1[:, 0:n], in0=psO[:, 0:n], in1=rz[:, 0:n], op=ALU.mult), reads=[bpO, brz], writes=[bo1])
        S.op("pool", lambda e: e.tensor_tensor(out=yo[:, 0:n], in0=o1[:, 0:n], in1=gate[:, t0:t0 + n], op=ALU.mult), reads=[bo1, bg], writes=[byo])
        S.dma("sp", E.YT[ycb, :, t0:t0 + n], yo[:, 0:n], reads=[byo], writes=[E.bYT[ycb]])

    def attn_work(P):
        return dict(rz=Rot([(P.sb("rz", [128, 512]), Buf()) for _ in range(2)]),
                    o1=Rot([(P.sb("o1", [128, 512]), Buf()) for _ in range(2)]),
                    yo=Rot([(P.sb("yo", [128, 512], BF16), Buf()) for _ in range(2)]),
                    pf=Rot([(P.sb("pf", [128, 512]), Buf()) for _ in range(2)]))

    def gqa(l, hh):
        j = l // 2
        need_ctx = l < 3
        hp = E.hy[(j, hh)]
        P = Phase(nc, S)
        kT = P.sb("kT", [128, 2, NT], BF16); vtok = P.sb("vtok", [128, 34, 256], BF16)
        cosT = P.sb("cosT", [128, L]); sinT = P.sb("sinT", [128, L]); prot = P.sb("prot", [128, 128]); qg = P.sb("qg", [128, 1]); kg = P.sb("kg", [128, 1])
        bkT, bvt, bcst = Buf(), Buf(), Buf()
        S.dma("sp", cosT[:], E.ropeC, writes=[bcst]); S.dma("sp", sinT[:], E.ropeS, writes=[bcst]); S.dma("sp", prot[:], E.protT, writes=[bcst])
        S.dma("sp", qg[:], hp["qg"], writes=[bcst]); S.dma("sp", kg[:], hp["kg"], writes=[bcst])
        S.dma("sp", vtok[:], E.VT[:, :, 0:256].rearrange("a p c -> p a c"), reads=[E.bVT], writes=[bvt])
        raws = Rot([(P.sb("raw", [128, NT]), Buf()) for _ in range(2)])
        gates = Rot([(P.sb("gate", [128, NT]), Buf()) for _ in range(2)])
        qTs = Rot([(P.sb("qT", [128, NT], BF16), Buf()) for _ in range(2)])
        sq = P.sb("sq", [128, 512], BF16); rs = P.sb("rs", [128, 512]); qn = P.sb("qn", [128, 512]); t1 = P.sb("t1", [128, 512]); t2 = P.sb("t2", [128, 512])
        bsq, brs, bqn, bt1, bt2 = Buf(), Buf(), Buf(), Buf(), Buf()
        pN = Rot([(P.ps("pN"), PBuf()) for _ in range(1)])
        pS = Rot([(P.ps("pS"), PBuf()) for _ in range(3)])
        pO = Rot([(P.ps("pO"), PBuf()) for _ in range(2)])
        pZ = Rot([(P.ps("pZ"), PBuf()) for _ in range(2)])
        pTs = Rot([(P.sb("pT", [128, 512], BF16), Buf()) for _ in range(3)])
        wk = attn_work(P)

        def normrope(src, bsrc, g, dst_ap_of, bdst):
            for ch in range(9):
                n = 512 if ch < 8 else 256
                t0 = ch * 512
                S.op("act", lambda e: e.activation(out=sq[:, 0:n], in_=src[:, t0:t0 + n], func=ACT.Square), reads=[bsrc], writes=[bsq])
                ps, bp = pN.next()
                S.op("pe", lambda e: e.matmul(ps[:, 0:n], lhsT=E.ones_bf[:], rhs=sq[:, 0:n], start=True, stop=True), reads=[bsq, E.bconst], writes=[bp])
                S.op("act", lambda e: e.activation(out=rs[:, 0:n], in_=ps[:, 0:n], func=ACT.Sqrt, scale=1.0 / 128.0, bias=EPS), reads=[bp], writes=[brs])
                S.op("dve", lambda e: e.reciprocal(out=rs[:, 0:n], in_=rs[:, 0:n]), reads=[brs], writes=[brs])
                S.op("dve", lambda e: e.scalar_tensor_tensor(out=qn[:, 0:n], in0=src[:, t0:t0 + n], scalar=g[:, 0:1], in1=rs[:, 0:n], op0=ALU.mult, op1=ALU.mult),
                     reads=[bsrc, brs, bcst], writes=[bqn])
                if ch < 8:
                    ps, bp = pN.next()
                    S.op("pe", lambda e: e.matmul(ps[:, 0:n], lhsT=prot[:], rhs=qn[:, 0:n], start=True, stop=True), reads=[bqn, bcst], writes=[bp])
                    S.op("dve", lambda e: e.tensor_tensor(out=t1[:, 0:n], in0=qn[:, 0:n], in1=cosT[:, t0:t0 + n], op=ALU.mult), reads=[bqn, bcst], writes=[bt1])
                    S.op("dve", lambda e: e.tensor_tensor(out=t2[:, 0:n], in0=ps[:, 0:n], in1=sinT[:, t0:t0 + n], op=ALU.mult), reads=[bp, bcst], writes=[bt2])
                    S.op("pool", lambda e: e.tensor_tensor(out=dst_ap_of(t0, n), in0=t1[:, 0:n], in1=t2[:, 0:n], op=ALU.add), reads=[bt1, bt2], writes=[bdst])
                else:
                    S.op("act", lambda e: e.copy(out=dst_ap_of(t0, n), in_=qn[:, 0:n]), reads=[bqn], writes=[bdst])

        for kv in range(2):
            raw, braw = raws.next()
            S.dma("sp", raw[:], E.UF[40 + kv, :, :], reads=[E.bUF[40 + kv]], writes=[braw])
            normrope(raw, braw, kg, lambda t0, n: kT[:, kv, t0:t0 + n], bkT)
        for hd in range(8):
            kv = hd // 4
            raw, braw = raws.next()
            S.dma("sp", raw[:], E.UF[32 + hd, :, :], reads=[E.bUF[32 + hd]], writes=[braw])
            gate, bg = gates.next()
            S.dma("sp", gate[:], E.UF[42 + hd, :, :], reads=[E.bUF[42 + hd]], writes=[bg])
            S.op("act", lambda e: e.activation(out=gate[:], in_=gate[:], func=ACT.Silu), reads=[bg], writes=[bg])
            qT, bq = qTs.next()
            normrope(raw, braw, qg, lambda t0, n: qT[:, t0:t0 + n], bq)
            for ch in range(9 if need_ctx else 8):
                n = 512 if ch < 8 else 256
                kbl = list(range(34)) if ch < 8 else [32, 33]
                attend(qT, bq, ch * 512, n, kbl, lambda kb: kT[:, kv, kb * 128:(kb + 1) * 128], bkT,
                       lambda kb: vtok[:, kb, kv * 128:(kv + 1) * 128], bvt, gate, bg, 8 + hd, pS, pO, pZ, pTs, wk)
        P.close()

    def lru(l, hh):
        j = l // 2
        need_ctx = l < 3
        lp = E.lru[(j, hh)]
        P = Phase(nc, S)
        cw = P.sb("cw", [128, 8, 4]); cb = P.sb("cb", [128, 8]); wa = P.sb("wa", [128, 2, 8, 128]); wx = P.sb("wx", [128, 2, 8, 128])
        ba = P.sb("ba", [128, 2, 8]); bx = P.sb("bx", [128, 2, 8]); lam = P.sb("lam", [128, 2, 8]); cl = P.sb("cl", [128, 2, 8]); cl2 = P.sb("cl2", [128, 2, 8])
        bpar = Buf()
        for dst, src in ((cw, lp["cw"]), (cb, lp["cb"]), (wa, lp["wa"]), (wx, lp["wx"]), (ba, lp["ba"]), (bx, lp["bx"]), (lam, lp["lam"])):
            S.dma("sp", dst[:], src, writes=[bpar])
        S.op("act", lambda e: e.activation(out=lam[:], in_=lam[:], func=ACT.Exp, scale=-1.0), reads=[bpar], writes=[bpar])
        S.op("act", lambda e: e.activation(out=lam[:], in_=lam[:], func=ACT.Ln, bias=1.0, scale=1.0), reads=[bpar], writes=[bpar])
        S.op("dve", lambda e: e.tensor_scalar(out=cl[:], in0=lam[:], scalar1=-8.0, scalar2=None, op0=ALU.mult), reads=[bpar], writes=[bpar])
        S.op("dve", lambda e: e.tensor_scalar(out=cl2[:], in0=lam[:], scalar1=-16.0, scalar2=None, op0=ALU.mult), reads=[bpar], writes=[bpar])
        us = Rot([(P.sb("u", [128, NT]), Buf()) for _ in range(2)])
        xr = P.sb("xr", [128, NT]); r = P.sb("r", [128, NT]); ig = P.sb("ig", [128, NT]); a = P.sb("a", [128, NT]); bb = P.sb("bb", [128, NT])
        hA = P.sb("hA", [128, NT]); hB = P.sb("hB", [128, NT]); yo = P.sb("yo", [128, NT], BF16)
        bxr, br, bi, ba_, bbb, bhA, bhB, byo = Buf(), Buf(), Buf(), Buf(), Buf(), Buf(), Buf(), Buf()
        pss = Rot([(P.ps("pl"), PBuf()) for _ in range(4)])
        for cbk in range(8):
            u, bu = us.next()
            S.dma("sp", u[:], E.UF[cbk, :, :], reads=[E.bUF[cbk]], writes=[bu])
            S.op("act", lambda e: e.activation(out=xr[:], in_=u[:], func=ACT.Identity, scale=cw[:, cbk, 2:3], bias=cb[:, cbk:cbk + 1]), reads=[bu, bpar], writes=[bxr])
            for lo, hi in ((0, L), (L, NT)):
                S.op("dve", lambda e: e.scalar_tensor_tensor(out=xr[:, lo + 1:hi], in0=u[:, lo:hi - 1], scalar=cw[:, cbk, 1:2], in1=xr[:, lo + 1:hi], op0=ALU.mult, op1=ALU.add),
                     reads=[bu, bxr, bpar], writes=[bxr])
                S.op("dve", lambda e: e.scalar_tensor_tensor(out=xr[:, lo + 2:hi], in0=u[:, lo:hi - 2], scalar=cw[:, cbk, 0:1], in1=xr[:, lo + 2:hi], op0=ALU.mult, op1=ALU.add),
                     reads=[bu, bxr, bpar], writes=[bxr])
                S.op("dve", lambda e: e.scalar_tensor_tensor(out=xr[:, lo:hi - 1], in0=u[:, lo + 1:hi], scalar=cw[:, cbk, 3:4], in1=xr[:, lo:hi - 1], op0=ALU.mult, op1=ALU.add),
                     reads=[bu, bxr, bpar], writes=[bxr])
            for d in range(2):
                for ch in range(9):
                    n = 512 if ch < 8 else 256
                    t0 = ch * 512
                    ps, bp = pss.next()
                    S.op("pe", lambda e: e.matmul(ps[:, 0:n], lhsT=wa[:, d, cbk, :], rhs=xr[:, t0:t0 + n], start=True, stop=True), reads=[bpar, bxr], writes=[bp])
                    S.op("act", lambda e: e.activation(out=r[:, t0:t0 + n], in_=ps[:, 0:n], func=ACT.Sigmoid, bias=ba[:, d, cbk:cbk + 1], scale=1.0), reads=[bp, bpar], writes=[br])
                    ps, bp = pss.next()
                    S.op("pe", lambda e: e.matmul(ps[:, 0:n], lhsT=wx[:, d, cbk, :], rhs=xr[:, t0:t0 + n], start=True, stop=True), reads=[bpar, bxr], writes=[bp])
                    S.op("act", lambda e: e.activation(out=ig[:, t0:t0 + n], in_=ps[:, 0:n], func=ACT.Sigmoid, bias=bx[:, d, cbk:cbk + 1], scale=1.0), reads=[bp, bpar], writes=[bi])
                S.op("act", lambda e: e.activation(out=a[:], in_=r[:], func=ACT.Exp, scale=cl[:, d, cbk:cbk + 1]), reads=[br, bpar], writes=[ba_])
                S.op("act", lambda e: e.activation(out=r[:], in_=r[:], func=ACT.Exp, scale=cl2[:, d, cbk:cbk + 1]), reads=[br, bpar], writes=[br])
                S.op("act", lambda e: e.activation(out=r[:], in_=r[:], func=ACT.Sqrt, scale=-1.0, bias=1.0), reads=[br], writes=[br])
                S.op("pool", lambda e: e.tensor_tensor(out=ig[:], in0=ig[:], in1=xr[:], op=ALU.mult), reads=[bi, bxr], writes=[bi])
                S.op("pool", lambda e: e.tensor_tensor(out=bb[:], in0=ig[:], in1=r[:], op=ALU.mult), reads=[bi, br], writes=[bbb])
                if d == 0:
                    S.op("dve", lambda e: e.tensor_tensor_scan(out=hA[:, L:NT], data0=a[:, L:NT], data1=bb[:, L:NT], initial=0.0, op0=ALU.mult, op1=ALU.add),
                         reads=[ba_, bbb], writes=[bhA])
                    S.op("dve", lambda e: e.tensor_tensor_scan(out=hA[:, 0:L], data0=a[:, 0:L], data1=bb[:, 0:L], initial=hA[:, NT - 1:NT], op0=ALU.mult, op1=ALU.add),
                         reads=[ba_, bbb, bhA], writes=[bhA])
                else:
                    S.op("dve", lambda e: e.tensor_tensor_scan(out=hB[:, L:NT][:, ::-1], data0=a[:, L:NT][:, ::-1], data1=bb[:, L:NT][:, ::-1], initial=0.0,
                                                               op0=ALU.mult, op1=ALU.add), reads=[ba_, bbb], writes=[bhB])
                    S.op("dve", lambda e: e.tensor_tensor_scan(out=hB[:, 0:L][:, ::-1], data0=a[:, 0:L][:, ::-1], data1=bb[:, 0:L][:, ::-1], initial=hB[:, L:L + 1],
                                                               op0=ALU.mult, op1=ALU.add), reads=[ba_, bbb, bhB], writes=[bhB])
            g, bg = us.next()
            S.dma("sp", g[:], E.UF[8 + cbk, :, :], reads=[E.bUF[8 + cbk]], writes=[bg])
            S.op("act", lambda e: e.activation(out=g[:], in_=g[:], func=ACT.Silu), reads=[bg], writes=[bg])
            S.op("pool", lambda e: e.tensor_tensor(out=hA[:], in0=hA[:], in1=hB[:], op=ALU.add), reads=[bhA, bhB], writes=[bhA])
            S.op("dve", lambda e: e.tensor_tensor(out=yo[:], in0=hA[:], in1=g[:], op=ALU.mult), reads=[bhA, bg], writes=[byo])
            S.dma("sp", E.YT[cbk, :, :], yo[:], reads=[byo], writes=[E.bYT[cbk]])
        P.close()

    def na(l, hh):
        j = l // 2
        need_ctx = l < 3
        lp = E.lru[(j, hh)]
        P = Phase(nc, S)
        raws = Rot([(P.sb("raw", [128, NT]), Buf()) for _ in range(2)])
        gates = Rot([(P.sb("gate", [128, NT]), Buf()) for _ in range(2)])
        qTs = Rot([(P.sb("qT", [128, NT], BF16), Buf()) for _ in range(2)])
        kTs = Rot([(P.sb("kT", [128, NT], BF16), Buf()) for _ in range(2)])
        vts = Rot([(P.sb("vt", [128, 34, 128], BF16), Buf()) for _ in range(2)])
        Ebs = Rot([(P.sb("Eb", [128, 20, 512], BF16), Buf()) for _ in range(2)])
        stg = Rot([(P.sb("bst", [128, 512]), Buf()) for _ in range(2)])
        pS = Rot([(P.ps("pS"), PBuf()) for _ in range(3)])
        pO = Rot([(P.ps("pO"), PBuf()) for _ in range(2)])
        pZ = Rot([(P.ps("pZ"), PBuf()) for _ in range(2)])
        pTs = Rot([(P.sb("pT", [128, 512], BF16), Buf()) for _ in range(3)])
        wk = attn_work(P)
        for hd in range(8):
            raw, braw = raws.next()
            S.dma("sp", raw[:], E.UF[16 + hd, :, :], reads=[E.bUF[16 + hd]], writes=[braw])
            qT, bq = qTs.next()
            S.op("act", lambda e: e.copy(out=qT[:], in_=raw[:]), reads=[braw], writes=[bq])
            raw, braw = raws.next()
            S.dma("sp", raw[:], E.UF[24 + hd, :, :], reads=[E.bUF[24 + hd]], writes=[braw])
            kT, bk = kTs.next()
            S.op("dve", lambda e: e.tensor_copy(out=kT[:], in_=raw[:]), reads=[braw], writes=[bk])
            gate, bg = gates.next()
            S.dma("sp", gate[:], E.UF[32 + hd, :, :], reads=[E.bUF[32 + hd]], writes=[bg])
            S.op("act", lambda e: e.activation(out=gate[:], in_=gate[:], func=ACT.Silu), reads=[bg], writes=[bg])
            vt, bv = vts.next()
            S.dma("sp", vt[:], E.VT[:, :, hd * 128:(hd + 1) * 128].rearrange("a p c -> p a c"), reads=[E.bVT], writes=[bv])
            Eb, bE = Ebs.next()
            for ti in range(20):
                st_, bst = stg.next()
                S.dma("sp", st_[:], lp["nab"][hd, ti, :, :], writes=[bst])
                S.op("act", lambda e: e.activation(out=Eb[:, ti, :], in_=st_[:], func=ACT.Exp), reads=[bst], writes=[bE])
            for i in range(8):
                if i == 0:
                    wkb = [(kb, kb) for kb in range(6)]
                elif i == 7:
                    wkb = [(26 + jj, 14 + jj) for jj in range(6)]
                else:
                    wkb = [(4 * i - 2 + jj, 6 + jj) for jj in range(8)]
                tmap = dict(wkb)
                kbl = [kb for kb, _ in wkb] + [32, 33]
                attend(qT, bq, i * 512, 512, kbl, lambda kb: kT[:, kb * 128:(kb + 1) * 128], bk, lambda kb: vt[:, kb, :], bv, gate, bg, 8 + hd,
                       pS, pO, pZ, pTs, wk, emul=lambda kb: ((Eb[:, tmap[kb], :], bE) if kb in tmap else None))
            if need_ctx:
                attend(qT, bq, L, 256, [32, 33], lambda kb: kT[:, kb * 128:(kb + 1) * 128], bk, lambda kb: vt[:, kb, :], bv, gate, bg, 8 + hd,
                       pS, pO, pZ, pTs, wk)
        P.close()

    def even(l, hh):
        if "hy" in E.cfg.get("mix", ("hy", "gqa")):
            hyena(l, hh)
        if "gqa" in E.cfg.get("mix", ("hy", "gqa")):
            gqa(l, hh)

    def odd(l, hh):
        if "lru" in E.cfg.get("mix", ("lru", "na")):
            lru(l, hh)
        if "na" in E.cfg.get("mix", ("lru", "na")):
            na(l, hh)

    return {"even": even, "odd": odd}


def _bf16(a):
    return np.ascontiguousarray(a.astype(ml_dtypes.bfloat16))


_CONST = {}


def host_constants():
    if _CONST:
        return _CONST
    f32 = np.float32
    c = {}
    for name, Lq in (("L", L), ("C", LC)):
        t = np.linspace(0.0, 1.0, Lq, dtype=f32)[:, None]
        bands = 16
        w = (2.0 * math.pi * np.arange(Lq, dtype=f32)[:, None] / Lq).astype(f32)
        fr = np.linspace(1e-4, bands - 1, bands, dtype=f32)[None, :]
        feats = np.concatenate([t, np.cos(fr * w), -np.sin(fr * w)], axis=-1).astype(f32)
        c["feats" + name] = np.ascontiguousarray(feats.T)
        c["tnorm" + name] = np.ascontiguousarray(np.broadcast_to(t[:, 0][None, :], (128, Lq))).astype(f32)
        N = 2 * Lq
        idx = (2 * np.arange(Lq, dtype=np.int64) + 1)
        m = (idx[:, None] * idx[None, :]) % (4 * N)
        ang = (2.0 * math.pi / (4 * N)) * m.astype(np.float64)
        nb = Lq // 128
        c["Ctab" + name] = _bf16(np.cos(ang).astype(f32)).reshape(nb, 128, Lq)
        c["Stab" + name] = _bf16(np.sin(ang).astype(f32)).reshape(nb, 128, Lq)
        phi = math.pi * (np.arange(Lq, dtype=np.float64) + 0.5) / N
        cp = (2.0 / N) * np.cos(phi)
        sp_ = (2.0 / N) * np.sin(phi)
        ph = np.stack([cp, sp_, -sp_], axis=0).astype(f32)
        c["phi" + name] = np.ascontiguousarray(ph.reshape(3, nb, 128).transpose(2, 0, 1))
    pos = np.arange(L)
    row = (pos // 64).astype(f32)
    col = (pos % 64).astype(f32)
    n = 32
    inv = (10000.0 ** (-np.arange(n, dtype=f32) / n)).astype(f32)
    ang = np.concatenate([row[:, None] * inv, col[:, None] * inv], axis=-1).astype(f32)
    cosT = np.zeros((128, L), f32)
    sinT = np.zeros((128, L), f32)
    prot = np.zeros((128, 128), f32)
    for a in range(2):
        for pr in range(2):
            for i in range(n):
                dh = a * 64 + pr * 32 + i
                cosT[dh] = np.cos(ang[:, a * 32 + i])
                sinT[dh] = np.sin(ang[:, a * 32 + i])
        for i in range(n):
            prot[a * 64 + i, a * 64 + 32 + i] = -1.0
            prot[a * 64 + 32 + i, a * 64 + i] = 1.0
    c["ropeC"] = cosT
    c["ropeS"] = sinT
    c["protT"] = np.ascontiguousarray(prot.T)
    deltas = np.linspace(abs(math.log(1e-2) / 1.5), abs(math.log(1e-2) / 0.3), 2048, dtype=f32)
    c["deltas"] = deltas
    _CONST.update(c)
    return c


def na_bias_tiles(rpb_heads):
    key_p = np.arange(128)
    krl = key_p // 64
    kc = key_p % 64
    qf = np.arange(512)
    qrl = qf // 64
    qc = qf % 64
    out = np.full((8, 20, 128, 512), -30000.0, np.float32)
    tiles = [(0, j, j) for j in range(6)] + [(3, j, 4 * 3 - 2 + j) for j in range(8)] + [(7, j, 26 + j) for j in range(6)]
    for ti, (i, j, kb) in enumerate(tiles):
        kr = 2 * kb + krl
        qr = 8 * i + qrl
        r0 = np.clip(qr - 4, 0, 56)
        c0 = np.clip(qc - 8, 0, 48)
        valid = ((kr[:, None] >= r0[None, :]) & (kr[:, None] < r0[None, :] + 8) &
                 (kc[:, None] >= c0[None, :]) & (kc[:, None] < c0[None, :] + 16))
        dr = np.clip(kr[:, None] - qr[None, :] + 7, 0, 14)
        dc = np.clip(kc[:, None] - qc[None, :] + 15, 0, 30)
        g = rpb_heads[:, dr, dc]
        out[:, ti] = np.where(valid[None], g, np.float32(-30000.0))
    return out


def prep_params(inp, core, nl=4):
    C = host_constants()
    f32 = np.float32
    m = {}
    m["gn"] = np.ascontiguousarray(inp["norm_g"].reshape(4, DB, 128).transpose(2, 0, 1))
    m["gfin"] = np.ascontiguousarray(inp["final_g"].reshape(DB, 128).T)
    for j in range((nl + 1) // 2):
        cw = inp["hy_conv_w"][j]
        cb = inp["hy_conv_b"][j]
        m[f"hyw1{j}"] = np.ascontiguousarray(inp["hy_w1"][j])
        m[f"hyb1{j}"] = np.ascontiguousarray(inp["hy_b1"][j][:, None])
        m[f"hyfr{j}"] = np.ascontiguousarray(inp["hy_freq"][j][:, None])
        m[f"hyw2{j}"] = np.ascontiguousarray(inp["hy_w2"][j])
        m[f"hyb2{j}"] = np.ascontiguousarray(inp["hy_b2"][j][:, None])
        m[f"attqg{j}"] = np.ascontiguousarray(inp["att_q_g"][j][:, None])
        m[f"attkg{j}"] = np.ascontiguousarray(inp["att_k_g"][j][:, None])
        w3 = inp["hy_w3"][j]
        for h in range(2):
            cwc = np.zeros((128, 3, 8, 3), f32)
            cbc = np.zeros((128, 3, 8), f32)
            for q in range(3):
                s0 = q * 2048 + h * 1024
                cwc[:, q] = cw[:, s0:s0 + 1024].reshape(3, 8, 128).transpose(2, 1, 0)
                cbc[:, q] = cb[s0:s0 + 1024].reshape(8, 128).T
            m[f"hycw{j}_{h}"] = cwc
            m[f"hycb{j}_{h}"] = cbc
            m[f"hyskip{j}_{h}"] = np.ascontiguousarray(inp["hy_skip"][j][h * 1024:(h + 1) * 1024].reshape(8, 128).T)
            m[f"hyw3{j}_{h}"] = np.ascontiguousarray(np.stack([w3[:, h * 1024:(h + 1) * 1024], w3[:, 2048 + h * 1024:2048 + (h + 1) * 1024]], axis=1))
    for j in range(nl // 2):
        for h in range(2):
            sl = slice(h * 1024, (h + 1) * 1024)
            m[f"lrucw{j}_{h}"] = np.ascontiguousarray(inp["lru_conv_w"][j][:, sl].reshape(4, 8, 128).transpose(2, 1, 0))
            m[f"lrucb{j}_{h}"] = np.ascontiguousarray(inp["lru_conv_b"][j][sl].reshape(8, 128).T)
            for nm, key in (("wa", "lru_wa"), ("wx", "lru_wx")):
                wgt = inp[key][j][:, h * 8:(h + 1) * 8]
                m[f"lru{nm}{j}_{h}"] = np.ascontiguousarray(wgt.transpose(2, 0, 1, 3))
            for nm, key in (("ba", "lru_ba"), ("bx", "lru_bx"), ("lam", "lru_lambda")):
                v = inp[key][j][:, sl]
                m[f"lru{nm}{j}_{h}"] = np.ascontiguousarray(v.reshape(2, 8, 128).transpose(2, 0, 1))
            m[f"nab{j}_{h}"] = na_bias_tiles(inp["na_rpb"][j][h * 8:(h + 1) * 8])
    for k in ("featsL", "featsC", "tnormL", "tnormC", "CtabL", "StabL", "CtabC", "StabC", "phiL", "phiC", "ropeC", "ropeS", "protT"):
        m[k] = C[k]
    for h in range(2):
        m[f"negdelta_{h}"] = np.ascontiguousarray(-C["deltas"][h * 1024:(h + 1) * 1024].reshape(8, 128).T)
    return m


_SHARED = {}


def prep_core(inp, core, nl=4):
    b, hme = core // NRANK, core % NRANK
    f32 = np.float32
    m = {}
    xt = np.concatenate([inp["x"][b].T, inp["ctx"][b].T], axis=1)
    m["xT"] = np.ascontiguousarray(xt.reshape(DB, 128, NT))
    sc = np.concatenate([inp["c"], inp["c_ctx"][None, :]], axis=0)
    m["scin"] = np.ascontiguousarray(sc.reshape(5, DB, 128).transpose(2, 1, 0))
    oh = np.zeros((128, 5), f32)
    oh[:, b] = 1.0
    m["onehot"] = oh
    ncol = NMB * 128
    cols = slice(hme * ncol, (hme + 1) * ncol)
    m["wmod"] = np.ascontiguousarray(np.stack([inp["w_mod"][l][:, cols].reshape(DB, 128, ncol).transpose(1, 0, 2) for l in range(4)]))
    m["bmod"] = np.ascontiguousarray(np.stack([inp["b_mod"][l][cols].reshape(NMB, 128).T for l in range(4)], axis=1))
    if "w" not in _SHARED:
        w = {}
        for l in range(nl):
            j = l // 2
            for h in range(2):
                if l % 2 == 0:
                    w_in, w_out = inp["ev_w_in"][j], inp["ev_w_out"][j]
                    blocks = even_fm_cols(h)
                    v0, vn = even_tm_cols(h)
                else:
                    w_in, w_out = inp["od_w_in"][j], inp["od_w_out"][j]
                    blocks = odd_fm_cols(h)
                    v0, vn = odd_tm_cols(h)
                ncb = len(blocks)
                colidx = np.concatenate([np.arange(s0, s0 + 128) for s0 in blocks])
                wsel = w_in[:, colidx]
                w[f"wfm{l}_{h}"] = np.ascontiguousarray(wsel.reshape(DB, 128, ncb, 128).transpose(2, 1, 0, 3))
                w[f"wtm{l}_{h}"] = np.ascontiguousarray(w_in[:, v0:v0 + vn].reshape(DB, 128, vn).transpose(1, 0, 2))
                rows = np.concatenate([np.arange(h * 1024, h * 1024 + 1024), np.arange(2048 + h * 1024, 2048 + h * 1024 + 1024)])
                wo = w_out[rows]
                w[f"wout{l}_{h}"] = np.ascontiguousarray(wo.reshape(16, 128, DB, 128).transpose(2, 1, 0, 3))
        w.update(prep_params(inp, core, nl))
        _SHARED["w"] = w
    for k, v in _SHARED["w"].items():
        if k.endswith("_0") or k.endswith("_1"):
            if int(k[-1]) == hme:
                m[k[:-2] + "_0"] = v
        else:
            m[k] = v
    return m


def kernel(**inputs):
    inp = {k: np.asarray(v) for k, v in inputs.items()}
    nc, S = build({})
    in_maps = [prep_core(inp, c) for c in range(NCORES)]
    res = run_bass_kernel_spmd(nc, in_maps, core_ids=list(range(NCORES)))
    out = np.empty((4, L, D), np.float32)
    for b in range(4):
        o = res.results[NRANK * b]["outT"]
        out[b] = o.reshape(D, L).T
    _SHARED.clear()
    return out
```

```python
import math
from contextlib import ExitStack
import numpy as np
import ml_dtypes
import concourse.bass as bass
import concourse.mybir as mybir
from concourse.bass_utils import run_bass_kernel_spmd

F32 = mybir.dt.float32
BF16 = mybir.dt.bfloat16
I32 = mybir.dt.int32
ACT = mybir.ActivationFunctionType
ALU = mybir.AluOpType
AX = mybir.AxisListType

D = 4096
L = 4096
LC = 256
NT = L + LC
DB = 32
EPS = 1e-6
NCORES = 8
NRANK = 2
NMB = 96 // NRANK
NH = 1
PAIRS = [[0, 1], [2, 3], [4, 5], [6, 7]]
TWO_PI = 2.0 * math.pi


class Buf:
    __slots__ = ("name", "lw", "rd", "excl")

    def __init__(self, name="b", excl=False):
        self.name = name
        self.lw = None
        self.rd = {}
        self.excl = excl


def PBuf():
    return Buf("psum", True)


class Sched:
    NDMA = 12

    def __init__(self, nc, stack):
        self.nc = nc
        self.eng = {"pe": nc.tensor, "act": nc.scalar, "dve": nc.vector, "pool": nc.gpsimd, "sp": nc.sync}
        self.sem = {k: stack.enter_context(nc.semaphore("c_" + k)) for k in self.eng}
        self.cnt = {k: 0 for k in self.eng}
        self.dsem = {k: [stack.enter_context(nc.semaphore(f"d_{k}{i}")) for i in range(self.NDMA)]
                     for k in ("sp", "act", "pool")}
        self.dcnt = {k: 0 for k in self.dsem}
        self.ccsem = stack.enter_context(nc.semaphore("ccsem"))
        self.cccnt = 0
        self.waited = {k: {} for k in self.eng}
        self.semobj = {}
        for k in self.eng:
            self.semobj[("c", k)] = self.sem[k]
        for k in self.dsem:
            for i in range(self.NDMA):
                self.semobj[("d", k, i)] = self.dsem[k][i]
        self.semobj[("cc",)] = self.ccsem
        self.nwait = 0
        self.nins = 0

    def _wait(self, e, tok):
        if tok is None:
            return
        key, val = tok
        if key == ("c", "pe") and e == "pe":
            return
        if self.waited[e].get(key, 0) >= val:
            return
        self.eng[e].wait_ge(self.semobj[key], val)
        self.waited[e][key] = val
        self.nwait += 1

    def _deps(self, e, reads, writes):
        for b in reads:
            self._wait(e, b.lw)
        for b in writes:
            self._wait(e, b.lw)
            for t in list(b.rd.items()):
                self._wait(e, t)

    def _commit(self, tok, reads, writes):
        for b in reads:
            if b.rd.get(tok[0], 0) < tok[1]:
                b.rd[tok[0]] = tok[1]
        for b in writes:
            b.lw = tok
            b.rd = {}

    def op(self, e, fn, reads=(), writes=()):
        if any(b.excl for b in reads):
            writes = list(writes) + [b for b in reads if b.excl]
            reads = [b for b in reads if not b.excl]
        self._deps(e, reads, writes)
        self.cnt[e] += 1
        fn(self.eng[e]).then_inc(self.sem[e], 1)
        tok = (("c", e), self.cnt[e])
        self._commit(tok, reads, writes)
        self.nins += 1
        return tok

    def dma(self, e, out, in_, reads=(), writes=(), **kw):
        k = self.dcnt[e]
        slot = k % self.NDMA
        rnd = k // self.NDMA
        key = ("d", e, slot)
        if rnd > 0:
            self._wait(e, (key, 16 * rnd))
        self._deps(e, reads, writes)
        self.dcnt[e] += 1
        self.eng[e].dma_start(out=out, in_=in_, **kw).then_inc(self.dsem[e][slot], 16)
        tok = (key, 16 * (rnd + 1))
        self._commit(tok, reads, writes)
        self.nins += 1
        return tok

    def collective(self, kind, op, groups, in_ap, out_ap, reads=(), writes=()):
        e = "pool"
        self._deps(e, reads, writes)
        self.cccnt += 1
        self.eng[e].collective_compute(kind, op, replica_groups=groups, ins=[in_ap], outs=[out_ap]).then_inc(self.ccsem)
        tok = (("cc",), self.cccnt)
        self._commit(tok, reads, writes)
        return tok

    def all_tokens(self):
        toks = [(("c", k), self.cnt[k]) for k in self.eng if self.cnt[k] > 0]
        for e in self.dsem:
            k = self.dcnt[e]
            for slot in range(self.NDMA):
                n = (k - slot + self.NDMA - 1) // self.NDMA
                if n > 0:
                    toks.append((("d", e, slot), 16 * n))
        if self.cccnt:
            toks.append((("cc",), self.cccnt))
        return toks

    def barrier(self, engines=None):
        toks = self.all_tokens()
        for e in (engines or self.eng):
            for t in toks:
                if t[0] == ("c", "pe") and e == "pe":
                    continue
                self._wait(e, t)


class Phase:
    _uid = [0]

    def __init__(self, nc, S):
        self.nc = nc
        self.S = S
        self.st = ExitStack()

    def _nm(self, name):
        Phase._uid[0] += 1
        return f"{name}_{Phase._uid[0]}"

    def sb(self, name, shape, dt=F32):
        return self.st.enter_context(self.nc.sbuf_tensor(self._nm(name), shape, dt))

    def ps(self, name, shape=(128, 512), dt=F32):
        return self.st.enter_context(self.nc.psum_tensor(self._nm(name), list(shape), dt))

    def close(self):
        self.S.barrier()
        self.st.close()


class Rot:
    def __init__(self, items):
        self.items = items
        self.i = 0

    def next(self):
        it = self.items[self.i % len(self.items)]
        self.i += 1
        return it


def even_fm_cols(h):
    blocks = []
    for base in (0, 2048, 4096, 6144):
        blocks += [base + h * 1024 + i * 128 for i in range(8)]
    blocks += [8192 + h * 1024 + i * 128 for i in range(8)]
    blocks += [10240 + h * 256 + i * 128 for i in range(2)]
    blocks += [11264 + h * 1024 + i * 128 for i in range(8)]
    return blocks


def even_tm_cols(h):
    return 10752 + h * 256, 256


def odd_fm_cols(h):
    blocks = []
    for base in (0, 2048, 4096, 6144, 10240):
        blocks += [base + h * 1024 + i * 128 for i in range(8)]
    return blocks


def odd_tm_cols(h):
    return 8192 + h * 1024, 1024


NCB_EVEN = 50
NCB_ODD = 40


def build(cfg):
    nl = cfg.get("nl", 4)
    stop = cfg.get("stop", None)
    dbg = cfg.get("dbg", ())
    nc = bass.Bass("TRN2", target_bir_lowering=False)

    def din(name, shape, dt=F32):
        return nc.dram_tensor(name, list(shape), dt, kind="ExternalInput").ap()

    def dscr(name, shape, dt=F32):
        return nc.dram_tensor(name, list(shape), dt).ap()

    def dout(name, shape, dt=F32):
        return nc.dram_tensor(name, list(shape), dt, kind="ExternalOutput").ap()

    mixtest = cfg.get("mixtest", None)
    tailtest = cfg.get("tailtest", False)
    gn = din("gn", [128, 4, DB])
    gfin = din("gfin", [128, DB])
    if tailtest:
        xT_in = din("xT", [DB, 128, NT])
        modv_in = din("modv_in", [128, 2, 4, 96])
        wout_t = din("wout0", [DB, 128, 16, 128])
    elif mixtest is None:
        xT_in = din("xT", [DB, 128, NT])
        scin = din("scin", [128, DB, 5])
        onehot = din("onehot", [128, 5])
        wmod = din("wmod", [4, 128, DB, NMB * 128])
        bmod = din("bmod", [128, 4, NMB])
    wfm, wtm, wout = [], [], []
    for l in range(nl if (mixtest is None and not tailtest) else 0):
        ncb = NCB_EVEN if l % 2 == 0 else NCB_ODD
        vc = 256 if l % 2 == 0 else 1024
        wfm.append([din(f"wfm{l}_{hh}", [ncb, 128, DB, 128]) for hh in range(NH)])
        wtm.append([din(f"wtm{l}_{hh}", [128, DB, vc]) for hh in range(NH)])
        wout.append([din(f"wout{l}_{hh}", [DB, 128, 16, 128]) for hh in range(NH)])
    n_ev = (nl + 1) // 2
    n_od = nl // 2
    ev_js = range(n_ev) if mixtest is None else ([mixtest // 2] if mixtest % 2 == 0 else [])
    od_js = range(n_od) if mixtest is None else ([mixtest // 2] if mixtest % 2 == 1 else [])
    if tailtest:
        ev_js, od_js = [], []
        wout = [[wout_t, wout_t]]
    hy = {}
    for j in ev_js:
        w1_ = din(f"hyw1{j}", [33, 64]); b1_ = din(f"hyb1{j}", [64, 1]); fr_ = din(f"hyfr{j}", [64, 1])
        w2_ = din(f"hyw2{j}", [64, 64]); b2_ = din(f"hyb2{j}", [64, 1])
        qg_ = din(f"attqg{j}", [128, 1]); kg_ = din(f"attkg{j}", [128, 1])
        for hh in range(NH):
            hy[(j, hh)] = (dict(
                cw=din(f"hycw{j}_{hh}", [128, 3, 8, 3]), cb=din(f"hycb{j}_{hh}", [128, 3, 8]), skip=din(f"hyskip{j}_{hh}", [128, 8]),
                w1=w1_, b1=b1_, fr=fr_, w2=w2_, b2=b2_, w3=din(f"hyw3{j}_{hh}", [64, 2, 1024]), qg=qg_, kg=kg_))
    lru = {}
    for j in od_js:
        for hh in range(NH):
            lru[(j, hh)] = (dict(
                cw=din(f"lrucw{j}_{hh}", [128, 8, 4]), cb=din(f"lrucb{j}_{hh}", [128, 8]),
                wa=din(f"lruwa{j}_{hh}", [128, 2, 8, 128]), ba=din(f"lruba{j}_{hh}", [128, 2, 8]),
                wx=din(f"lruwx{j}_{hh}", [128, 2, 8, 128]), bx=din(f"lrubx{j}_{hh}", [128, 2, 8]),
                lam=din(f"lrulam{j}_{hh}", [128, 2, 8]), nab=din(f"nab{j}_{hh}", [8, 20, 128, 512])))
    if not tailtest:
        featsL = din("featsL", [33, L])
        featsC = din("featsC", [33, LC])
        tnormL = din("tnormL", [128, L])
        tnormC = din("tnormC", [128, LC])
        negdelta = [din(f"negdelta_{hh}", [128, 8]) for hh in range(NH)]
        CtabL = din("CtabL", [32, 128, L], BF16)
        StabL = din("StabL", [32, 128, L], BF16)
        CtabC = din("CtabC", [2, 128, LC], BF16)
        StabC = din("StabC", [2, 128, LC], BF16)
        phiL = din("phiL", [128, 3, 32])
        phiC = din("phiC", [128, 3, 2])
        ropeC = din("ropeC", [128, L])
        ropeS = din("ropeS", [128, L])
        protT = din("protT", [128, 128])
    else:
        featsL = featsC = tnormL = tnormC = negdelta = CtabL = StabL = CtabC = StabC = phiL = phiC = ropeC = ropeS = protT = None

    outT = dout("outT", [DB, 128, L])
    XT = dscr("XT", [DB, 128, NT])
    HT = dscr("HT", [17, 128, DB, 256], BF16)
    if mixtest is None:
        UF = dscr("UF", [NCB_EVEN, 128, NT])
        VT = dscr("VT", [34, 128, 1024], BF16)
    else:
        UF = din("UF_in", [NCB_EVEN, 128, NT])
        VT = din("VT_in", [34, 128, 1024], BF16)
    YT = din("YT_in", [16, 128, NT], BF16) if tailtest else dscr("YT", [16, 128, NT], BF16)
    PT = dscr("PT", [DB * 128, NT])
    RT = dscr("RT", [DB * 128, NT])
    PT3 = PT.rearrange("(a p) t -> a p t", p=128)
    RT3 = RT.rearrange("(a p) t -> a p t", p=128)
    SD = dscr("SD", [8, 128, NT])
    X0G = dscr("X0G", [8, 128, NT])
    KAL = dscr("KAL", [32, 128, 1024])
    KBL = dscr("KBL", [32, 128, 1024])
    KAC = dscr("KAC", [2, 128, 1024])
    KBC = dscr("KBC", [2, 128, 1024])
    modloc_d = dscr("modloc_d", [128, 4 * NMB * 5])
    modall_d = dscr("modall_d", [NRANK * 128, 4 * NMB * 5])
    dbg_out = {}
    if "HT" in dbg:
        dbg_out["HT"] = dout("dbg_HT", [17, 128, DB, 256], BF16)
    if "UF" in dbg:
        dbg_out["UF"] = dout("dbg_UF", [NCB_EVEN, 128, NT])
    if "VT" in dbg:
        dbg_out["VT"] = dout("dbg_VT", [34, 128, 1024], BF16)
    if "YT" in dbg:
        dbg_out["YT"] = dout("dbg_YT", [16, 128, NT], BF16)
    if "PT" in dbg:
        dbg_out["PT"] = dout("dbg_PT", [DB * 128, NT])
    if "RT" in dbg:
        dbg_out["RT"] = dout("dbg_RT", [DB * 128, NT])
    if "XT" in dbg:
        dbg_out["XT"] = dout("dbg_XT", [DB, 128, NT])
    if "MOD" in dbg:
        dbg_out["MOD"] = dout("dbg_MOD", [128, 2, 4, 96])
    if "KA" in dbg:
        dbg_out["KA"] = dout("dbg_KA", [32, 128, 1024])
        dbg_out["KB"] = dout("dbg_KB", [32, 128, 1024])
        dbg_out["KAC"] = dout("dbg_KAC", [2, 128, 1024])
    if "SD" in dbg:
        dbg_out["SD"] = dout("dbg_SD", [8, 128, NT])

    bXT = [Buf(f"XT{i}") for i in range(17)]
    bHT = [Buf(f"HT{i}") for i in range(17)]
    bUF = [Buf(f"UF{i}") for i in range(NCB_EVEN)]
    bVT = Buf("VT")
    bYT = [Buf(f"YT{i}") for i in range(16)]
    bPT = [Buf(f"PT{i}") for i in range(4)]
    bPTt = [[Buf() for _ in range(9)] for _ in range(DB)]
    bRTn = [Buf() for _ in range(DB)]
    bRT = [Buf(f"RT{i}") for i in range(4)]
    bSD = [Buf(f"SD{i}") for i in range(8)]
    bX0G = [Buf(f"X0G{i}") for i in range(8)]
    bK = Buf("K")
    bmodd = Buf("modd")

    stack = ExitStack()
    S = Sched(nc, stack)
    blk = stack.enter_context(nc.Block())

    def body(_e):
        G = Phase(nc, S)
        ones_bf = G.sb("ones_bf", [128, 128], BF16)
        ones_f = G.sb("ones_f", [128, 128])
        ident_f = G.sb("ident_f", [128, 128])
        ident_bf = G.sb("ident_bf", [128, 128], BF16)
        bconst = Buf("const")
        S.op("pool", lambda e: e.memset(ones_f[:], 1.0), writes=[bconst])
        S.op("pool", lambda e: e.memset(ident_f[:], 0.0), writes=[bconst])
        S.op("pool", lambda e: e.affine_select(out=ident_f[:], in_=ident_f[:], pattern=[[-1, 128]],
                                               compare_op=ALU.not_equal, fill=1.0, base=0, channel_multiplier=1),
             reads=[bconst], writes=[bconst])
        S.op("dve", lambda e: e.tensor_copy(out=ones_bf[:], in_=ones_f[:]), reads=[bconst], writes=[bconst])
        S.op("dve", lambda e: e.tensor_copy(out=ident_bf[:], in_=ident_f[:]), reads=[bconst], writes=[bconst])
        modv = G.sb("modv", [128, 2, 4, 96])
        gn_sb = G.sb("gn_sb", [128, 4, DB])
        gfin_sb = G.sb("gfin_sb", [128, DB])
        gs_sb = G.sb("gs_sb", [128, 2, DB])
        bmodv = Buf("modv")
        bgs = Buf("gs")
        S.dma("sp", gn_sb[:], gn[:, :, :], writes=[bconst])
        S.dma("sp", gfin_sb[:], gfin[:, :], writes=[bconst])

        def phase_mod():
            P = Phase(nc, S)
            sc = P.sb("sc", [128, DB, 5])
            oh = P.sb("oh", [128, 5])
            bm = P.sb("bm", [128, 4, NMB])
            ml = P.sb("ml", [128, 4, NMB, 5])
            ma = P.sb("ma", [128, NRANK, 4 * NMB * 5])
            tmp = P.sb("tmp", [128, NRANK, 4 * NMB])
            bsc, bml, bma, btmp = Buf(), Buf(), Buf(), Buf()
            wbufs = Rot([(P.sb(f"wm{i}", [128, DB, 384]), Buf()) for i in range(2)])
            pss = Rot([(P.ps(f"pm{i}"), PBuf()) for i in range(2)])
            S.dma("sp", sc[:], scin[:, :, :], writes=[bsc])
            S.dma("sp", oh[:], onehot[:, :], writes=[bsc])
            S.dma("sp", bm[:], bmod[:, :, :], writes=[bsc])
            S.op("act", lambda e: e.activation(out=sc[:], in_=sc[:], func=ACT.Silu), reads=[bsc], writes=[bsc])
            for l in range(4):
                for ch in range(NMB // 3):
                    w, bw = wbufs.next()
                    S.dma("sp", w[:], wmod[l, :, :, ch * 384:(ch + 1) * 384], writes=[bw])
                    for jj in range(3):
                        j = ch * 3 + jj
                        ps, bp = pss.next()
                        for db in range(DB):
                            S.op("pe", lambda e: e.matmul(ps[:, 0:5], lhsT=w[:, db, jj * 128:(jj + 1) * 128], rhs=sc[:, db, :],
                                                          start=(db == 0), stop=(db == DB - 1)),
                                 reads=[bw, bsc], writes=[bp])
                        S.op("act", lambda e: e.activation(out=ml[:, l, j, :], in_=ps[:, 0:5], func=ACT.Identity,
                                                           bias=bm[:, l, j:j + 1], scale=1.0),
                             reads=[bp, bsc], writes=[bml])
            S.dma("sp", modloc_d[:, :], ml[:].rearrange("p a b c -> p (a b c)"), reads=[bml], writes=[bmodd])
            S.collective("AllGather", ALU.bypass, PAIRS, modloc_d[:, :], modall_d[:, :],
                         reads=[bmodd], writes=[bmodd])
            S.dma("sp", ma[:], modall_d.rearrange("(r p) f -> p r f", p=128), reads=[bmodd], writes=[bma])
            mav = ma[:].rearrange("p r (lj c) -> p r lj c", c=5)
            S.op("dve", lambda e: e.tensor_scalar(out=tmp[:], in0=mav[:, :, :, 0], scalar1=oh[:, 0:1], scalar2=None, op0=ALU.mult),
                 reads=[bma, bsc], writes=[btmp])
            for c in range(1, 4):
                S.op("dve", lambda e: e.scalar_tensor_tensor(out=tmp[:], in0=mav[:, :, :, c], scalar=oh[:, c:c + 1], in1=tmp[:],
                                                             op0=ALU.mult, op1=ALU.add),
                     reads=[bma, bsc, btmp], writes=[btmp])
            for l in range(4):
                tv = tmp[:].rearrange("p r (l j) -> p r l j", j=NMB)
                S.op("dve", lambda e: e.tensor_copy(out=modv[:, 0, l, :].rearrange("p (r j) -> p r j", j=NMB), in_=tv[:, :, l, :]),
                     reads=[btmp], writes=[bmodv])
                mv4 = ma[:].rearrange("p r (l j c) -> p r l j c", j=NMB, c=5)
                S.op("dve", lambda e: e.tensor_copy(out=modv[:, 1, l, :].rearrange("p (r j) -> p r j", j=NMB), in_=mv4[:, :, l, :, 4]),
                     reads=[bma], writes=[bmodv])
            P.close()

        if tailtest:
            S.dma("sp", modv[:], modv_in, writes=[bmodv])
            S.dma("sp", modloc_d[:, :], gn_sb[:, 0:2, :].rearrange("p a b -> p (a b)")[:, 0:64], reads=[bconst], writes=[bmodd]) if False else None
            S.collective("AllGather", ALU.bypass, PAIRS, modloc_d[:, :], modall_d[:, :], reads=[bmodd], writes=[bmodd])
        elif mixtest is None:
            phase_mod()

        def phase_A(l):
            P = Phase(nc, S)
            xts = Rot([(P.sb(f"xt{i}", [128, DB, 256]), Buf()) for i in range(2)])
            rts = Rot([(P.sb(f"rt{i}", [128, DB, 256]), Buf()) for i in range(1)])
            sqs = Rot([(P.sb(f"sq{i}", [128, DB, 256], BF16), Buf()) for i in range(1)])
            hos = Rot([(P.sb(f"ho{i}", [128, DB, 256], BF16), Buf()) for i in range(2)])
            rss = Rot([(P.sb(f"rs{i}", [128, 256]), Buf()) for i in range(2)])
            pss = Rot([(P.ps(f"pa{i}"), PBuf()) for i in range(2)])
            for w in range(2):
                S.op("dve", lambda e: e.scalar_tensor_tensor(out=gs_sb[:, w, :], in0=modv[:, w, l, 32:64], scalar=1.0, in1=gn_sb[:, l, :],
                                                             op0=ALU.add, op1=ALU.mult),
                     reads=[bmodv, bconst], writes=[bgs])
            for tt in range(17):
                w = 1 if tt == 16 else 0
                xt, bx = xts.next()
                src = xT_in if l <= 1 else XT
                S.dma("sp", xt[:], src[:, :, tt * 256:(tt + 1) * 256].rearrange("a p t -> p a t"),
                      reads=[] if l <= 1 else [bXT[tt]], writes=[bx])
                if l > 0:
                    rt, br = rts.next()
                    S.dma("sp", rt[:], RT3[:, :, tt * 256:(tt + 1) * 256].rearrange("a p t -> p a t"), reads=bRTn, writes=[br])
                    for db in range(DB):
                        S.op("dve",
                             lambda e: e.scalar_tensor_tensor(out=xt[:, db, :], in0=rt[:, db, :], scalar=modv[:, w, l - 1, 64 + db:65 + db],
                                                              in1=xt[:, db, :], op0=ALU.mult, op1=ALU.add),
                             reads=[br, bx, bmodv], writes=[bx])
                    S.dma("sp", XT[:, :, tt * 256:(tt + 1) * 256].rearrange("a p t -> p a t"), xt[:], reads=[bx], writes=[bXT[tt]])
                sq, bs = sqs.next()
                S.op("act", lambda e: e.activation(out=sq[:], in_=xt[:], func=ACT.Square), reads=[bx], writes=[bs])
                ps, bp = pss.next()
                for db in range(DB):
                    S.op("pe", lambda e: e.matmul(ps[:, 0:256], lhsT=ones_bf[:], rhs=sq[:, db, :], start=(db == 0), stop=(db == DB - 1)),
                         reads=[bs, bconst], writes=[bp])
                rs, brs = rss.next()
                S.op("act", lambda e: e.activation(out=rs[:], in_=ps[:, 0:256], func=ACT.Sqrt, scale=1.0 / D, bias=EPS), reads=[bp], writes=[brs])
                S.op("dve", lambda e: e.reciprocal(out=rs[:], in_=rs[:]), reads=[brs], writes=[brs])
                ho, bh = hos.next()
                for db in range(DB):
                    S.op("dve", lambda e: e.scalar_tensor_tensor(out=xt[:, db, :], in0=xt[:, db, :], scalar=gs_sb[:, w, db:db + 1], in1=rs[:],
                                                                 op0=ALU.mult, op1=ALU.mult),
                         reads=[bx, brs, bgs], writes=[bx])
                    S.op("act", lambda e: e.activation(out=ho[:, db, :], in_=xt[:, db, :], func=ACT.Identity,
                                                       bias=modv[:, w, l, db:db + 1], scale=1.0),
                         reads=[bx, bmodv], writes=[bh])
                S.dma("sp", HT[tt, :, :, :], ho[:], reads=[bh], writes=[bHT[tt]])
            P.close()

        def phase_B(l, hh):
            even = (l % 2 == 0)
            ncb = NCB_EVEN if even else NCB_ODD
            vcols = 256 if even else 1024
            P = Phase(nc, S)
            ht = P.sb("ht", [128, 4, DB, 256], BF16)
            bht = Buf()
            ws = Rot([(P.sb(f"w{i}", [128, DB, 128], BF16), Buf()) for i in range(3)])
            wv = P.sb("wv", [128, DB, 512], BF16)
            bwv = Buf()
            stg = Rot([(P.sb(f"stg{i}", [128, 512]), Buf()) for i in range(3)])
            stgb = Rot([(P.sb(f"stgb{i}", [128, 512], BF16), Buf()) for i in range(2)])
            pss = Rot([(P.ps(f"pb{i}"), PBuf()) for i in range(4)])
            ev = [0]
            for st in range(5):
                ntile = 4 if st < 4 else 1
                for k in range(ntile):
                    S.dma("sp", ht[:, k, :, :], HT[st * 4 + k, :, :, :], reads=[bHT[st * 4 + k]], writes=[bht])
                nhalf = 2 if st < 4 else 1
                for cb in range(ncb):
                    w, bw = ws.next()
                    S.dma("pool", w[:], wfm[l][hh][cb, :, :, :], writes=[bw])
                    for hf in range(nhalf):
                        ps, bp = pss.next()
                        if st < 4:
                            n = 512
                            rhs_of = lambda db: ht[:, 2 * hf:2 * hf + 2, db, :]
                        else:
                            n = 256
                            rhs_of = lambda db: ht[:, 0, db, :]
                        for db in range(DB):
                            S.op("pe", lambda e: e.matmul(ps[:, 0:n], lhsT=w[:, db, :], rhs=rhs_of(db), start=(db == 0), stop=(db == DB - 1)),
                                 reads=[bw, bht], writes=[bp])
                        sg, bsg = stg.next()
                        ev[0] += 1
                        if ev[0] % 2 == 0:
                            S.op("act", lambda e: e.copy(out=sg[:, 0:n], in_=ps[:, 0:n]), reads=[bp], writes=[bsg])
                        else:
                            S.op("dve", lambda e: e.tensor_copy(out=sg[:, 0:n], in_=ps[:, 0:n]), reads=[bp], writes=[bsg])
                        t0 = st * 1024 + hf * 512
                        S.dma("sp", UF[cb, :, t0:t0 + n], sg[:, 0:n], reads=[bsg], writes=[bUF[cb]])
                for vch in range(vcols // min(vcols, 512)):
                    vw = min(vcols, 512)
                    S.dma("pool", wv[:, :, 0:vw], wtm[l][hh][:, :, vch * vw:(vch + 1) * vw], writes=[bwv])
                    for k in range(ntile):
                        for sub in range(2):
                            tb = (st * 4 + k) * 2 + sub
                            ps, bp = pss.next()
                            for db in range(DB):
                                S.op("pe", lambda e: e.matmul(ps[:, 0:vw], lhsT=ht[:, k, db, sub * 128:(sub + 1) * 128], rhs=wv[:, db, 0:vw],
                                                              start=(db == 0), stop=(db == DB - 1)),
                                     reads=[bwv, bht], writes=[bp])
                            sg, bsg = stgb.next()
                            ev[0] += 1
                            if ev[0] % 2 == 0:
                                S.op("act", lambda e: e.copy(out=sg[:, 0:vw], in_=ps[:, 0:vw]), reads=[bp], writes=[bsg])
                            else:
                                S.op("dve", lambda e: e.tensor_copy(out=sg[:, 0:vw], in_=ps[:, 0:vw]), reads=[bp], writes=[bsg])
                            S.dma("sp", VT[tb, :, vch * vw:(vch + 1) * vw], sg[:, 0:vw], reads=[bsg], writes=[bVT])
            P.close()

        def phase_D(l, hh):
            need_ctx = l < 3
            P = Phase(nc, S)
            yt = P.sb("yt", [128, 16, NT], BF16)
            byt = Buf()
            ws = Rot([(P.sb(f"wo{i}", [128, 16, 128], BF16), Buf()) for i in range(3)])
            stg = Rot([(P.sb(f"stg{i}", [128, 512]), Buf()) for i in range(4)])
            pvs = Rot([(P.sb(f"pv{i}", [128, 512]), Buf()) for i in range(3)])
            pss = Rot([(P.ps(f"pd{i}"), PBuf()) for i in range(4)])
            for c in range(16):
                S.dma("sp", yt[:, c, :], YT[c, :, :], reads=[bYT[c]], writes=[byt])
            ev = 0
            wq = []
            for nb0 in range(2):
                w, bw = ws.next()
                S.dma("pool", w[:], wout[l][hh][nb0, :, :, :], writes=[bw])
                wq.append((w, bw))
            ntt = 9 if need_ctx else 8
            for nb in range(DB):
                if nb + 2 < DB:
                    w2, bw2 = ws.next()
                    S.dma("pool", w2[:], wout[l][hh][nb + 2, :, :, :], writes=[bw2])
                    wq.append((w2, bw2))
                w, bw = wq[nb]
                for tt in range(ntt):
                    n = 512 if tt < 8 else 256
                    ps, bp = pss.next()
                    for c in range(16):
                        S.op("pe", lambda e: e.matmul(ps[:, 0:n], lhsT=w[:, c, :], rhs=yt[:, c, tt * 512:tt * 512 + n], start=(c == 0), stop=(c == 15)),
                             reads=[bw, byt], writes=[bp])
                    sg, bsg = stg.next()
                    ev += 1
                    if hh == 1:
                        pv, bpv = pvs.next()
                        S.dma("sp", pv[:, 0:n], PT3[nb, :, tt * 512:tt * 512 + n], reads=[bPTt[nb][tt]], writes=[bpv])
                        S.op("dve", lambda e: e.tensor_tensor(out=sg[:, 0:n], in0=ps[:, 0:n], in1=pv[:, 0:n], op=ALU.add), reads=[bp, bpv], writes=[bsg])
                    elif ev % 2 == 0:
                        S.op("act", lambda e: e.copy(out=sg[:, 0:n], in_=ps[:, 0:n]), reads=[bp], writes=[bsg])
                    else:
                        S.op("dve", lambda e: e.tensor_copy(out=sg[:, 0:n], in_=ps[:, 0:n]), reads=[bp], writes=[bsg])
                    S.dma("sp", PT3[nb, :, tt * 512:tt * 512 + n], sg[:, 0:n], reads=[bsg], writes=[bPTt[nb][tt], bPT[nb // 8]])
                if hh == NH - 1 and NRANK > 1:
                    S.collective("AllReduce", ALU.add, PAIRS, PT[nb * 128:(nb + 1) * 128, :], RT[nb * 128:(nb + 1) * 128, :],
                                 reads=[bPTt[nb][tt] for tt in range(ntt)], writes=[bRTn[nb]])
            P.close()

        def phase_final():
            l = 4
            P = Phase(nc, S)
            xts = Rot([(P.sb(f"xt{i}", [128, DB, 256]), Buf()) for i in range(2)])
            rts = Rot([(P.sb(f"rt{i}", [128, DB, 256]), Buf()) for i in range(2)])
            sqs = Rot([(P.sb(f"sq{i}", [128, DB, 256], BF16), Buf()) for i in range(1)])
            rss = Rot([(P.sb(f"rs{i}", [128, 256]), Buf()) for i in range(2)])
            pss = Rot([(P.ps(f"pa{i}"), PBuf()) for i in range(2)])
            for tt in range(16):
                xt, bx = xts.next()
                S.dma("sp", xt[:], XT[:, :, tt * 256:(tt + 1) * 256].rearrange("a p t -> p a t"), reads=[bXT[tt]], writes=[bx])
                rt, br = rts.next()
                S.dma("sp", rt[:], RT3[:, :, tt * 256:(tt + 1) * 256].rearrange("a p t -> p a t"), reads=bRTn, writes=[br])
                for db in range(DB):
                    S.op("dve",
                         lambda e: e.scalar_tensor_tensor(out=xt[:, db, :], in0=rt[:, db, :], scalar=modv[:, 0, l - 1, 64 + db:65 + db],
                                                          in1=xt[:, db, :], op0=ALU.mult, op1=ALU.add),
                         reads=[br, bx, bmodv], writes=[bx])
                sq, bs = sqs.next()
                S.op("act", lambda e: e.activation(out=sq[:], in_=xt[:], func=ACT.Square), reads=[bx], writes=[bs])
                ps, bp = pss.next()
                for db in range(DB):
                    S.op("pe", lambda e: e.matmul(ps[:, 0:256], lhsT=ones_bf[:], rhs=sq[:, db, :], start=(db == 0), stop=(db == DB - 1)),
                         reads=[bs, bconst], writes=[bp])
                rs, brs = rss.next()
                S.op("act", lambda e: e.activation(out=rs[:], in_=ps[:, 0:256], func=ACT.Sqrt, scale=1.0 / D, bias=EPS), reads=[bp], writes=[brs])
                S.op("dve", lambda e: e.reciprocal(out=rs[:], in_=rs[:]), reads=[brs], writes=[brs])
                for db in range(DB):
                    S.op("dve", lambda e: e.scalar_tensor_tensor(out=xt[:, db, :], in0=xt[:, db, :], scalar=gfin_sb[:, db:db + 1], in1=rs[:],
                                                                 op0=ALU.mult, op1=ALU.mult),
                         reads=[bx, brs, bconst], writes=[bx])
                S.dma("sp", outT[:, :, tt * 256:(tt + 1) * 256].rearrange("a p t -> p a t"), xt[:], reads=[bx], writes=[Buf()])
            P.close()

        import types
        E = types.SimpleNamespace(
            nc=nc, S=S, cfg=cfg, ones_bf=ones_bf, ones_f=ones_f, ident_f=ident_f, ident_bf=ident_bf, bconst=bconst,
            modv=modv, bmodv=bmodv, UF=UF, bUF=bUF, VT=VT, bVT=bVT, YT=YT, bYT=bYT, SD=SD, bSD=bSD, X0G=X0G, bX0G=bX0G,
            KAL=KAL, KBL=KBL, KAC=KAC, KBC=KBC, bK=bK, hy=hy, lru=lru, featsL=featsL, featsC=featsC, tnormL=tnormL,
            tnormC=tnormC, negdelta=negdelta, CtabL=CtabL, StabL=StabL, CtabC=CtabC, StabC=StabC, phiL=phiL, phiC=phiC,
            ropeC=ropeC, ropeS=ropeS, protT=protT)
        MIX = build_mixers(E)

        def dump(name, src, buf_reads):
            if name in dbg_out:
                S.dma("sp", dbg_out[name], src, reads=buf_reads)

        done = False
        if tailtest:
            phase_D(0, 0)
            phase_D(0, 1)
            phase_A(1)
            phase_final()
            done = True
            nl_loop = 0
        if mixtest is not None:
            (MIX["even"] if mixtest % 2 == 0 else MIX["odd"])(mixtest, 0)
            done = True
        for l in range(nl if (mixtest is None and not tailtest) else 0):
            phase_A(l)
            for hh in range(NH):
                phase_B(l, hh)
                (MIX["even"] if l % 2 == 0 else MIX["odd"])(l, hh)
                phase_D(l, hh)
            if done:
                break
        if not done:
            phase_final()
        S.barrier()
        if "MOD" in dbg_out:
            S.dma("sp", dbg_out["MOD"], modv[:], reads=[bmodv])
        dump("HT", HT, bHT)
        dump("UF", UF, bUF)
        dump("VT", VT, [bVT])
        dump("YT", YT, bYT)
        dump("PT", PT, bPT)
        dump("RT", RT, bRTn)
        dump("XT", XT, bXT)
        dump("SD", SD, bSD)
        if "KA" in dbg_out:
            S.dma("sp", dbg_out["KA"], KAL, reads=[bK])
            S.dma("sp", dbg_out["KB"], KBL, reads=[bK])
            S.dma("sp", dbg_out["KAC"], KAC, reads=[bK])
        S.barrier()
        G.st.close()

    blk.sync(body)
    stack.close()
    return nc, S


def _evac(S, k, out, in_, reads, writes):
    if k % 2 == 0:
        S.op("act", lambda e: e.copy(out=out, in_=in_), reads=reads, writes=writes)
    else:
        S.op("dve", lambda e: e.tensor_copy(out=out, in_=in_), reads=reads, writes=writes)


def build_mixers(E):
    nc, S = E.nc, E.S
    ISQ = 1.0 / math.sqrt(128.0)

    def hyena(l, hh):
        j = l // 2
        need_ctx = l < 3
        hp = E.hy[(j, hh)]
        PH = Phase(nc, S)
        cw = PH.sb("cw", [128, 3, 8, 3]); cb = PH.sb("cb", [128, 3, 8]); skip = PH.sb("skip", [128, 8]); negd = PH.sb("negd", [128, 8])
        w1 = PH.sb("w1", [33, 64]); b1 = PH.sb("b1", [64, 1]); fr = PH.sb("fr", [64, 1]); w2 = PH.sb("w2", [64, 64]); b2 = PH.sb("b2", [64, 1])
        w3 = PH.sb("w3", [64, 2, 1024]); om2 = PH.sb("om2", [64, 1])
        bpar = Buf()
        for dst, src in ((cw, hp["cw"]), (cb, hp["cb"]), (skip, hp["skip"]), (negd, E.negdelta[hh]), (w1, hp["w1"]), (b1, hp["b1"]),
                         (fr, hp["fr"]), (w2, hp["w2"]), (b2, hp["b2"]), (w3, hp["w3"])):
            S.dma("sp", dst[:], src, writes=[bpar])
        S.op("dve", lambda e: e.tensor_scalar(out=om2[:], in0=fr[:], scalar1=1.0 / TWO_PI, scalar2=None, op0=ALU.mult), reads=[bpar], writes=[bpar])

        def transpose_rows(src_bf, bsrc, dstT, bdst, ntb, col0, pst):
            tb = 0
            while tb < ntb:
                n = min(4, ntb - tb)
                pT, bpT = pst.next()
                for q in range(n):
                    S.op("pe", lambda e: e.transpose(out=pT[:, q * 128:(q + 1) * 128], in_=src_bf[:, (tb + q) * 128:(tb + q + 1) * 128], identity=E.ident_f[:]),
                         reads=[bsrc, E.bconst], writes=[bpT])
                _evac(S, tb // 4, dstT[:, tb:tb + n, col0:col0 + 128], pT[:, 0:n * 128].rearrange("p (a c) -> p a c", c=128), [bpT], [bdst])
                tb += n

        def filt(ctx):
            Lq = LC if ctx else L
            CH = min(512, Lq)
            nchunk = Lq // CH
            nb = Lq // 128
            feats, tnorm = (E.featsC, E.tnormC) if ctx else (E.featsL, E.tnormL)
            Ctab, Stab, phi = (E.CtabC, E.StabC, E.phiC) if ctx else (E.CtabL, E.StabL, E.phiL)
            KA, KB = (E.KAC, E.KBC) if ctx else (E.KAL, E.KBL)
            FW = 256
            PF = Phase(nc, S)
            ft = PF.sb("ft", [33, Lq]); tn = PF.sb("tn", [128, Lq]); hdn2 = PF.sb("hdn2", [64, Lq]); phis = PF.sb("phis", [128, 3, nb])
            bft = Buf(); bh2 = Buf()
            S.dma("sp", ft[:], feats, writes=[bft]); S.dma("sp", tn[:], tnorm, writes=[bft]); S.dma("sp", phis[:], phi, writes=[bft])
            P1 = Phase(nc, S)
            hdn1 = P1.sb("hdn1", [64, CH]); u = P1.sb("u", [64, CH]); ki = P1.sb("ki", [64, CH], I32); kf = P1.sb("kf", [64, CH])
            bh1, bu, bki = Buf(), Buf(), Buf()
            pss = Rot([(P1.ps("pf"), PBuf()) for _ in range(2)])

            def sin_layer(ps, bp, bvec, out_ap, bout):
                S.op("dve", lambda e: e.tensor_scalar(out=u[:], in0=ps[0:64, 0:CH], scalar1=bvec[:, 0:1], scalar2=om2[:, 0:1], op0=ALU.add, op1=ALU.mult),
                     reads=[bp, bpar], writes=[bu])
                S.op("dve", lambda e: e.tensor_copy(out=ki[:], in_=u[:]), reads=[bu], writes=[bki])
                S.op("dve", lambda e: e.tensor_copy(out=kf[:], in_=ki[:]), reads=[bki], writes=[bki])
                S.op("dve", lambda e: e.tensor_tensor(out=u[:], in0=u[:], in1=kf[:], op=ALU.subtract), reads=[bu, bki], writes=[bu])
                S.op("act", lambda e: e.activation(out=out_ap, in_=u[:], func=ACT.Sin, scale=TWO_PI), reads=[bu], writes=[bout])

            for ch in range(nchunk):
                ps, bp = pss.next()
                S.op("pe", lambda e: e.matmul(ps[0:64, 0:CH], lhsT=w1[:, :], rhs=ft[:, ch * CH:(ch + 1) * CH], start=True, stop=True), reads=[bpar, bft], writes=[bp])
                sin_layer(ps, bp, b1, hdn1[:], bh1)
                ps, bp = pss.next()
                S.op("pe", lambda e: e.matmul(ps[0:64, 0:CH], lhsT=w2[:, :], rhs=hdn1[:], start=True, stop=True), reads=[bpar, bh1], writes=[bp])
                sin_layer(ps, bp, b2, hdn2[:, ch * CH:(ch + 1) * CH], bh2)
            P1.close()
            fstop = E.cfg.get("fstop", 99)
            for G in range(2 if fstop > 1 else 0):
                PG = Phase(nc, S)
                ksT = PG.sb("ksT", [128, nb, 512], BF16); kdT = PG.sb("kdT", [128, nb, 512], BF16)
                bks, bkd = Buf(), Buf()
                P2 = Phase(nc, S)
                hf = P2.sb("hf", [128, Lq]); hb = P2.sb("hb", [128, Lq]); dec = P2.sb("dec", [128, CH]); ksb = P2.sb("ksb", [128, Lq])
                kdb = P2.sb("kdb", [128, Lq]); nrm = P2.sb("nrm", [128, 4])
                bhf, bhb, bdec, bksb, bkdb, bnrm = Buf(), Buf(), Buf(), Buf(), Buf(), Buf()
                pss = Rot([(P2.ps("pg"), PBuf()) for _ in range(2)])
                pst = Rot([(P2.ps("pt", (128, 512), F32), PBuf()) for _ in range(2)])
                for cq in range(4):
                    cbk = G * 4 + cq
                    for ch in range(nchunk):
                        sl = slice(ch * CH, (ch + 1) * CH)
                        S.op("act", lambda e: e.activation(out=dec[:], in_=tn[:, sl], func=ACT.Exp, scale=negd[:, cbk:cbk + 1]), reads=[bft, bpar], writes=[bdec])
                        for fb, (dst, bd) in enumerate(((hf, bhf), (hb, bhb))):
                            ps, bp = pss.next()
                            S.op("pe", lambda e: e.matmul(ps[:, 0:CH], lhsT=w3[:, fb, cbk * 128:(cbk + 1) * 128], rhs=hdn2[:, sl], start=True, stop=True),
                                 reads=[bpar, bh2], writes=[bp])
                            S.op("dve", lambda e: e.tensor_tensor(out=dst[:, sl], in0=ps[:, 0:CH], in1=dec[:], op=ALU.mult), reads=[bp, bdec], writes=[bd])
                    p2stop = E.cfg.get("p2stop", 99)
                    if p2stop <= 1:
                        continue
                    S.op("dve", lambda e: e.tensor_reduce(out=nrm[:, 0:1], in_=hf[:, :], axis=AX.X, op=ALU.add, apply_absolute_value=True), reads=[bhf], writes=[bnrm])
                    S.op("dve", lambda e: e.tensor_reduce(out=nrm[:, 1:2], in_=hb[:, 1:Lq], axis=AX.X, op=ALU.add, apply_absolute_value=True), reads=[bhb], writes=[bnrm])
                    S.op("dve", lambda e: e.tensor_tensor(out=nrm[:, 2:3], in0=nrm[:, 0:1], in1=nrm[:, 1:2], op=ALU.add), reads=[bnrm], writes=[bnrm])
                    S.op("dve", lambda e: e.reciprocal(out=nrm[:, 3:4], in_=nrm[:, 2:3]), reads=[bnrm], writes=[bnrm])
                    S.op("act", lambda e: e.activation(out=hf[:], in_=hf[:], func=ACT.Identity, scale=nrm[:, 3:4]), reads=[bhf, bnrm], writes=[bhf])
                    S.op("act", lambda e: e.activation(out=hb[:], in_=hb[:], func=ACT.Identity, scale=nrm[:, 3:4]), reads=[bhb, bnrm], writes=[bhb])
                    if p2stop <= 2:
                        continue
                    S.op("dve", lambda e: e.tensor_tensor(out=ksb[:, 0:Lq - 1], in0=hf[:, 0:Lq - 1], in1=hb[:, 1:Lq], op=ALU.add), reads=[bhf, bhb], writes=[bksb])
                    S.op("dve", lambda e: e.tensor_copy(out=ksb[:, Lq - 1:Lq], in_=hf[:, Lq - 1:Lq]), reads=[bhf], writes=[bksb])
                    S.op("pool", lambda e: e.tensor_tensor(out=kdb[:, 0:Lq - 1], in0=hf[:, 0:Lq - 1], in1=hb[:, 1:Lq], op=ALU.subtract), reads=[bhf, bhb], writes=[bkdb])
                    S.op("pool", lambda e: e.tensor_copy(out=kdb[:, Lq - 1:Lq], in_=hf[:, Lq - 1:Lq]), reads=[bhf], writes=[bkdb])
                    if p2stop <= 3:
                        continue
                    transpose_rows(ksb, bksb, ksT, bks, nb, cq * 128, pst)
                    transpose_rows(kdb, bkdb, kdT, bkd, nb, cq * 128, pst)
                P2.close()
                if fstop <= 2:
                    PG.close()
                    continue
                P3 = Phase(nc, S)
                Cs = Rot([(P3.sb("Cc", [128, nb, FW], BF16), Buf()) for _ in range(2)])
                Ss = Rot([(P3.sb("Sc", [128, nb, FW], BF16), Buf()) for _ in range(2)])
                t1 = P3.sb("t1", [128, 512]); t2 = P3.sb("t2", [128, 512]); bt1, bt2 = Buf(), Buf()
                oAs = Rot([(P3.sb("oA", [128, 512]), Buf()) for _ in range(2)])
                oBs = Rot([(P3.sb("oB", [128, 512]), Buf()) for _ in range(2)])
                pA = Rot([(P3.ps("pA"), PBuf()) for _ in range(2)])
                pB = Rot([(P3.ps("pB"), PBuf()) for _ in range(2)])
                def ld3(fg):
                    Cc, bC = Cs.next(); Sc, bS = Ss.next()
                    S.dma("sp", Cc[:], Ctab[:, :, fg * FW:(fg + 1) * FW].rearrange("a p f -> p a f"), writes=[bC])
                    S.dma("sp", Sc[:], Stab[:, :, fg * FW:(fg + 1) * FW].rearrange("a p f -> p a f"), writes=[bS])
                    return Cc, bC, Sc, bS
                nxt3 = ld3(0)
                for fg in range(Lq // FW):
                    Cc, bC, Sc, bS = nxt3
                    if fg + 1 < Lq // FW:
                        nxt3 = ld3(fg + 1)
                    for fq in range(FW // 128):
                        fb = fg * (FW // 128) + fq
                        psA, bpA = pA.next(); psB, bpB = pB.next()
                        for tb in range(nb):
                            S.op("pe", lambda e: e.matmul(psA[:, :], lhsT=Cc[:, tb, fq * 128:(fq + 1) * 128], rhs=ksT[:, tb, :], start=(tb == 0), stop=(tb == nb - 1)),
                                 reads=[bC, bks], writes=[bpA])
                        for tb in range(nb):
                            S.op("pe", lambda e: e.matmul(psB[:, :], lhsT=Sc[:, tb, fq * 128:(fq + 1) * 128], rhs=kdT[:, tb, :], start=(tb == 0), stop=(tb == nb - 1)),
                                 reads=[bS, bkd], writes=[bpB])
                        oA, boA = oAs.next(); oB, boB = oBs.next()
                        S.op("act", lambda e: e.activation(out=t1[:], in_=psB[:, :], func=ACT.Identity, scale=phis[:, 1, fb:fb + 1]), reads=[bpB, bft], writes=[bt1])
                        S.op("dve", lambda e: e.scalar_tensor_tensor(out=oA[:], in0=psA[:, :], scalar=phis[:, 0, fb:fb + 1], in1=t1[:], op0=ALU.mult, op1=ALU.add),
                             reads=[bpA, bt1, bft], writes=[boA])
                        S.op("act", lambda e: e.activation(out=t2[:], in_=psA[:, :], func=ACT.Identity, scale=phis[:, 2, fb:fb + 1]), reads=[bpA, bft], writes=[bt2])
                        S.op("dve", lambda e: e.scalar_tensor_tensor(out=oB[:], in0=psB[:, :], scalar=phis[:, 0, fb:fb + 1], in1=t2[:], op0=ALU.mult, op1=ALU.add),
                             reads=[bpB, bt2, bft], writes=[boB])
                        S.dma("sp", KA[fb, :, G * 512:(G + 1) * 512], oA[:], reads=[boA], writes=[E.bK])
                        S.dma("sp", KB[fb, :, G * 512:(G + 1) * 512], oB[:], reads=[boB], writes=[E.bK])
                P3.close()
                PG.close()
            PF.close()

        filt(False)
        if need_ctx and E.cfg.get("fstop", 99) > 3:
            filt(True)

        def data(G):
            PB = Phase(nc, S)
            Ya = PB.sb("Ya", [128, 32, 512], BF16); Yb = PB.sb("Yb", [128, 32, 512], BF16)
            Yac = PB.sb("Yac", [128, 2, 512], BF16); Ybc = PB.sb("Ybc", [128, 2, 512], BF16)
            bYa, bYb, bYac, bYbc = Buf(), Buf(), Buf(), Buf()
            PD = Phase(nc, S)
            sT = PD.sb("sT", [128, 34, 512], BF16); bsT = Buf()
            PA = Phase(nc, S)
            us = Rot([(PA.sb("u", [128, NT]), Buf()) for _ in range(2)])
            accs = Rot([(PA.sb("acc", [128, NT]), Buf()) for _ in range(2)])
            pst = Rot([(PA.ps("pt", (128, 512), F32), PBuf()) for _ in range(2)])

            def conv(q, cbk):
                u, bu = us.next()
                S.dma("sp", u[:], E.UF[q * 8 + cbk, :, :], reads=[E.bUF[q * 8 + cbk]], writes=[bu])
                a, ba = accs.next()
                S.op("act", lambda e: e.activation(out=a[:], in_=u[:], func=ACT.Identity, scale=cw[:, q, cbk, 1:2], bias=cb[:, q, cbk:cbk + 1]),
                     reads=[bu, bpar], writes=[ba])
                for lo, hi in ((0, L), (L, NT)):
                    S.op("dve", lambda e: e.scalar_tensor_tensor(out=a[:, lo + 1:hi], in0=u[:, lo:hi - 1], scalar=cw[:, q, cbk, 0:1], in1=a[:, lo + 1:hi],
                                                                 op0=ALU.mult, op1=ALU.add), reads=[bu, ba, bpar], writes=[ba])
                    S.op("dve", lambda e: e.scalar_tensor_tensor(out=a[:, lo:hi - 1], in0=u[:, lo + 1:hi], scalar=cw[:, q, cbk, 2:3], in1=a[:, lo:hi - 1],
                                                                 op0=ALU.mult, op1=ALU.add), reads=[bu, ba, bpar], writes=[ba])
                return a, ba

            for cq in range(4):
                cbk = G * 4 + cq
                x1c, bx1 = conv(1, cbk)
                vc, bv = conv(2, cbk)
                S.op("pool", lambda e: e.tensor_tensor(out=x1c[:], in0=x1c[:], in1=vc[:], op=ALU.mult), reads=[bx1, bv], writes=[bx1])
                S.dma("sp", E.SD[cbk, :, :], x1c[:], reads=[bx1], writes=[E.bSD[cbk]])
                transpose_rows(x1c, bx1, sT, bsT, 34, cq * 128, pst)
                x0c, bx0 = conv(0, cbk)
                u, bu = us.next()
                S.dma("sp", u[:], E.UF[24 + cbk, :, :], reads=[E.bUF[24 + cbk]], writes=[bu])
                S.op("act", lambda e: e.activation(out=u[:], in_=u[:], func=ACT.Silu), reads=[bu], writes=[bu])
                S.op("dve", lambda e: e.tensor_tensor(out=x0c[:], in0=x0c[:], in1=u[:], op=ALU.mult), reads=[bx0, bu], writes=[bx0])
                S.dma("sp", E.X0G[cbk, :, :], x0c[:], reads=[bx0], writes=[E.bX0G[cbk]])
            PA.close()
            P1 = Phase(nc, S)
            FW = 256
            Cs = Rot([(P1.sb("Cc", [128, 32, FW], BF16), Buf()) for _ in range(2)])
            Ss = Rot([(P1.sb("Sc", [128, 32, FW], BF16), Buf()) for _ in range(2)])
            kas = Rot([(P1.sb("ka", [128, 512]), Buf()) for _ in range(2)])
            kbs = Rot([(P1.sb("kb", [128, 512]), Buf()) for _ in range(2)])
            tt = [(P1.sb(f"t{i}", [128, 512]), Buf()) for i in range(4)]
            pA = Rot([(P1.ps("pA"), PBuf()) for _ in range(2)])
            pB = Rot([(P1.ps("pB"), PBuf()) for _ in range(2)])

            def spectral(nb, tb0, Ctab, Stab, KA, KB, Yra, bYra, Yrb, bYrb, FWc):
                def ldS(fg):
                    Cc, bC = Cs.next(); Sc, bS = Ss.next()
                    S.dma("sp", Cc[:, 0:nb, 0:FWc], Ctab[:, :, fg * FWc:(fg + 1) * FWc].rearrange("a p f -> p a f"), writes=[bC])
                    S.dma("sp", Sc[:, 0:nb, 0:FWc], Stab[:, :, fg * FWc:(fg + 1) * FWc].rearrange("a p f -> p a f"), writes=[bS])
                    return Cc, bC, Sc, bS
                nfg = (nb * 128) // FWc
                nxtS = ldS(0)
                for fg in range(nfg):
                    Cc, bC, Sc, bS = nxtS
                    if fg + 1 < nfg:
                        nxtS = ldS(fg + 1)
                    for fq in range(FWc // 128):
                        fb = fg * (FWc // 128) + fq
                        psA, bpA = pA.next(); psB, bpB = pB.next()
                        for tb in range(nb):
                            S.op("pe", lambda e: e.matmul(psA[:, :], lhsT=Cc[:, tb, fq * 128:(fq + 1) * 128], rhs=sT[:, tb0 + tb, :], start=(tb == 0), stop=(tb == nb - 1)),
                                 reads=[bC, bsT], writes=[bpA])
                        for tb in range(nb):
                            S.op("pe", lambda e: e.matmul(psB[:, :], lhsT=Sc[:, tb, fq * 128:(fq + 1) * 128], rhs=sT[:, tb0 + tb, :], start=(tb == 0), stop=(tb == nb - 1)),
                                 reads=[bS, bsT], writes=[bpB])
                        ka, bka = kas.next(); kb, bkb = kbs.next()
                        S.dma("sp", ka[:], KA[fb, :, G * 512:(G + 1) * 512], reads=[E.bK], writes=[bka])
                        S.dma("sp", kb[:], KB[fb, :, G * 512:(G + 1) * 512], reads=[E.bK], writes=[bkb])
                        (t1, b1_), (t2, b2_), (t3, b3_), (t4, b4_) = tt
                        S.op("dve", lambda e: e.tensor_tensor(out=t1[:], in0=psA[:, :], in1=ka[:], op=ALU.mult), reads=[bpA, bka], writes=[b1_])
                        S.op("dve", lambda e: e.tensor_tensor(out=t2[:], in0=psB[:, :], in1=kb[:], op=ALU.mult), reads=[bpB, bkb], writes=[b2_])
                        S.op("pool", lambda e: e.tensor_tensor(out=Yra[:, fb, :], in0=t1[:], in1=t2[:], op=ALU.subtract), reads=[b1_, b2_], writes=[bYra])
                        S.op("dve", lambda e: e.tensor_tensor(out=t3[:], in0=psA[:, :], in1=kb[:], op=ALU.mult), reads=[bpA, bkb], writes=[b3_])
                        S.op("dve", lambda e: e.tensor_tensor(out=t4[:], in0=psB[:, :], in1=ka[:], op=ALU.mult), reads=[bpB, bka], writes=[b4_])
                        S.op("pool", lambda e: e.tensor_tensor(out=Yrb[:, fb, :], in0=t3[:], in1=t4[:], op=ALU.add), reads=[b3_, b4_], writes=[bYrb])

            spectral(32, 0, E.CtabL, E.StabL, E.KAL, E.KBL, Ya, bYa, Yb, bYb, FW)
            if need_ctx:
                spectral(2, 32, E.CtabC, E.StabC, E.KAC, E.KBC, Yac, bYac, Ybc, bYbc, 256)
            P1.close()
            PD.close()
            PC = Phase(nc, S)
            Cs = Rot([(PC.sb("Cr", [128, 16, 512], BF16), Buf()) for _ in range(2)])
            Ss = Rot([(PC.sb("Sr", [128, 16, 512], BF16), Buf()) for _ in range(2)])
            sts = Rot([(PC.sb("s_t", [128, 512]), Buf()) for _ in range(2)])
            xts = Rot([(PC.sb("x_t", [128, 512]), Buf()) for _ in range(2)])
            tms = Rot([(PC.sb("tmp", [128, 512]), Buf()) for _ in range(2)])
            yos = Rot([(PC.sb("yo", [128, 512], BF16), Buf()) for _ in range(2)])
            pys = Rot([(PC.ps("py"), PBuf()) for _ in range(8)])

            def epilogue(cq, ps, bp, t0, n):
                cbk = G * 4 + cq
                s_t, bs_ = sts.next(); x_t, bx_ = xts.next(); tmp, btm = tms.next(); yo, byo = yos.next()
                S.dma("sp", s_t[:, 0:n], E.SD[cbk, :, t0:t0 + n], reads=[E.bSD[cbk]], writes=[bs_])
                S.dma("sp", x_t[:, 0:n], E.X0G[cbk, :, t0:t0 + n], reads=[E.bX0G[cbk]], writes=[bx_])
                S.op("dve", lambda e: e.scalar_tensor_tensor(out=tmp[:, 0:n], in0=s_t[:, 0:n], scalar=skip[:, cbk:cbk + 1], in1=ps[:, 0:n], op0=ALU.mult, op1=ALU.add),
                     reads=[bs_, bp, bpar], writes=[btm])
                S.op("pool", lambda e: e.tensor_tensor(out=yo[:, 0:n], in0=tmp[:, 0:n], in1=x_t[:, 0:n], op=ALU.mult), reads=[btm, bx_], writes=[byo])
                S.dma("sp", E.YT[cbk, :, t0:t0 + n], yo[:, 0:n], reads=[byo], writes=[E.bYT[cbk]])

            def ldC(i):
                tc_, half_ = i // 2, i % 2
                Cr, bC = Cs.next(); Sr, bS = Ss.next()
                S.dma("sp", Cr[:], E.CtabL[half_ * 16:(half_ + 1) * 16, :, tc_ * 512:(tc_ + 1) * 512].rearrange("a p t -> p a t"), writes=[bC])
                S.dma("sp", Sr[:], E.StabL[half_ * 16:(half_ + 1) * 16, :, tc_ * 512:(tc_ + 1) * 512].rearrange("a p t -> p a t"), writes=[bS])
                return Cr, bC, Sr, bS
            nxtC = ldC(0)
            for tc in range(8):
                pl = [pys.next() for _ in range(4)]
                for half in range(2):
                    Cr, bC, Sr, bS = nxtC
                    if tc * 2 + half + 1 < 16:
                        nxtC = ldC(tc * 2 + half + 1)
                    for cq in range(4):
                        ps, bp = pl[cq]
                        for fbl in range(16):
                            fb = half * 16 + fbl
                            S.op("pe", lambda e: e.matmul(ps[:, :], lhsT=Ya[:, fb, cq * 128:(cq + 1) * 128], rhs=Cr[:, fbl, :], start=(fb == 0), stop=False),
                                 reads=[bYa, bC], writes=[bp])
                            S.op("pe", lambda e: e.matmul(ps[:, :], lhsT=Yb[:, fb, cq * 128:(cq + 1) * 128], rhs=Sr[:, fbl, :], start=False, stop=(fb == 31)),
                                 reads=[bYb, bS], writes=[bp])
                for cq in range(4):
                    epilogue(cq, pl[cq][0], pl[cq][1], tc * 512, 512)
            if need_ctx:
                Cr, bC = Cs.next(); Sr, bS = Ss.next()
                S.dma("sp", Cr[:, 0:2, 0:256], E.CtabC.rearrange("a p t -> p a t"), writes=[bC])
                S.dma("sp", Sr[:, 0:2, 0:256], E.StabC.rearrange("a p t -> p a t"), writes=[bS])
                for cq in range(4):
                    ps, bp = pys.next()
                    for fb in range(2):
                        S.op("pe", lambda e: e.matmul(ps[:, 0:256], lhsT=Yac[:, fb, cq * 128:(cq + 1) * 128], rhs=Cr[:, fb, 0:256], start=(fb == 0), stop=False),
                             reads=[bYac, bC], writes=[bp])
                        S.op("pe", lambda e: e.matmul(ps[:, 0:256], lhsT=Ybc[:, fb, cq * 128:(cq + 1) * 128], rhs=Sr[:, fb, 0:256], start=False, stop=(fb == 1)),
                             reads=[bYbc, bS], writes=[bp])
                    epilogue(cq, ps, bp, L, 256)
            PC.close()
            PB.close()

        if not E.cfg.get("hy_filter_only", False):
            data(0)
            data(1)
        PH.close()

    def attend(qT, bq, t0, n, kblocks, lhsK, bk, lhsV, bv, gate, bg, ycb, pS, pO, pZ, pTs, wk, emul=None):
        psO, bpO = pO.next(); psZ, bpZ = pZ.next()
        nk = len(kblocks)
        for i, kb in enumerate(kblocks):
            psS, bpS = pS.next()
            S.op("pe", lambda e: e.matmul(psS[:, 0:n], lhsT=lhsK(kb), rhs=qT[:, t0:t0 + n], start=True, stop=True), reads=[bk, bq], writes=[bpS])
            pT, bpT = pTs.next()
            em = emul(kb) if emul is not None else None
            if em is None:
                S.op("act", lambda e: e.activation(out=pT[:, 0:n], in_=psS[:, 0:n], func=ACT.Exp, scale=ISQ), reads=[bpS], writes=[bpT])
            else:
                pf, bpf = wk["pf"].next()
                S.op("act", lambda e: e.activation(out=pf[:, 0:n], in_=psS[:, 0:n], func=ACT.Exp, scale=ISQ), reads=[bpS], writes=[bpf])
                S.op("dve" if i % 2 == 0 else "pool", lambda e: e.tensor_tensor(out=pT[:, 0:n], in0=pf[:, 0:n], in1=em[0], op=ALU.mult), reads=[bpf, em[1]], writes=[bpT])
            S.op("pe", lambda e: e.matmul(psO[:, 0:n], lhsT=lhsV(kb), rhs=pT[:, 0:n], start=(i == 0), stop=(i == nk - 1)), reads=[bv, bpT], writes=[bpO])
            S.op("pe", lambda e: e.matmul(psZ[:, 0:n], lhsT=E.ones_bf[:], rhs=pT[:, 0:n], start=(i == 0), stop=(i == nk - 1)), reads=[E.bconst, bpT], writes=[bpZ])
        rz, brz = wk["rz"].next(); o1, bo1 = wk["o1"].next(); yo, byo = wk["yo"].next()
        S.op("dve", lambda e: e.reciprocal(out=rz[:, 0:n], in_=psZ[:, 0:n]), reads=[bpZ], writes=[brz])
        S.op("dve", lambda e: e.tensor_tensor(out=o1[:, 0:n], in0=psO[:, 0:n], in1=rz[:, 0:n], op=ALU.mult), reads=[bpO, brz], writes=[bo1])
        S.op("pool", lambda e: e.tensor_tensor(out=yo[:, 0:n], in0=o1[:, 0:n], in1=gate[:, t0:t0 + n], op=ALU.mult), reads=[bo1, bg], writes=[byo])
        S.dma("sp", E.YT[ycb, :, t0:t0 + n], yo[:, 0:n], reads=[byo], writes=[E.bYT[ycb]])

    def attn_work(P):
        return dict(rz=Rot([(P.sb("rz", [128, 512]), Buf()) for _ in range(2)]),
                    o1=Rot([(P.sb("o1", [128, 512]), Buf()) for _ in range(2)]),
                    yo=Rot([(P.sb("yo", [128, 512], BF16), Buf()) for _ in range(2)]),
                    pf=Rot([(P.sb("pf", [128, 512]), Buf()) for _ in range(2)]))

    def gqa(l, hh):
        j = l // 2
        need_ctx = l < 3
        hp = E.hy[(j, hh)]
        P = Phase(nc, S)
        kT = P.sb("kT", [128, 2, NT], BF16); vtok = P.sb("vtok", [128, 34, 256], BF16)
        cosT = P.sb("cosT", [128, L]); sinT = P.sb("sinT", [128, L]); prot = P.sb("prot", [128, 128]); qg = P.sb("qg", [128, 1]); kg = P.sb("kg", [128, 1])
        bkT, bvt, bcst = Buf(), Buf(), Buf()
        S.dma("sp", cosT[:], E.ropeC, writes=[bcst]); S.dma("sp", sinT[:], E.ropeS, writes=[bcst]); S.dma("sp", prot[:], E.protT, writes=[bcst])
        S.dma("sp", qg[:], hp["qg"], writes=[bcst]); S.dma("sp", kg[:], hp["kg"], writes=[bcst])
        S.dma("sp", vtok[:], E.VT[:, :, 0:256].rearrange("a p c -> p a c"), reads=[E.bVT], writes=[bvt])
        raws = Rot([(P.sb("raw", [128, NT]), Buf()) for _ in range(2)])
        gates = Rot([(P.sb("gate", [128, NT]), Buf()) for _ in range(2)])
        qTs = Rot([(P.sb("qT", [128, NT], BF16), Buf()) for _ in range(2)])
        sq = P.sb("sq", [128, 512], BF16); rs = P.sb("rs", [128, 512]); qn = P.sb("qn", [128, 512]); t1 = P.sb("t1", [128, 512]); t2 = P.sb("t2", [128, 512])
        bsq, brs, bqn, bt1, bt2 = Buf(), Buf(), Buf(), Buf(), Buf()
        pN = Rot([(P.ps("pN"), PBuf()) for _ in range(1)])
        pS = Rot([(P.ps("pS"), PBuf()) for _ in range(3)])
        pO = Rot([(P.ps("pO"), PBuf()) for _ in range(2)])
        pZ = Rot([(P.ps("pZ"), PBuf()) for _ in range(2)])
        pTs = Rot([(P.sb("pT", [128, 512], BF16), Buf()) for _ in range(3)])
        wk = attn_work(P)

        def normrope(src, bsrc, g, dst_ap_of, bdst):
            for ch in range(9):
                n = 512 if ch < 8 else 256
                t0 = ch * 512
                S.op("act", lambda e: e.activation(out=sq[:, 0:n], in_=src[:, t0:t0 + n], func=ACT.Square), reads=[bsrc], writes=[bsq])
                ps, bp = pN.next()
                S.op("pe", lambda e: e.matmul(ps[:, 0:n], lhsT=E.ones_bf[:], rhs=sq[:, 0:n], start=True, stop=True), reads=[bsq, E.bconst], writes=[bp])
                S.op("act", lambda e: e.activation(out=rs[:, 0:n], in_=ps[:, 0:n], func=ACT.Sqrt, scale=1.0 / 128.0, bias=EPS), reads=[bp], writes=[brs])
                S.op("dve", lambda e: e.reciprocal(out=rs[:, 0:n], in_=rs[:, 0:n]), reads=[brs], writes=[brs])
                S.op("dve", lambda e: e.scalar_tensor_tensor(out=qn[:, 0:n], in0=src[:, t0:t0 + n], scalar=g[:, 0:1], in1=rs[:, 0:n], op0=ALU.mult, op1=ALU.mult),
                     reads=[bsrc, brs, bcst], writes=[bqn])
                if ch < 8:
                    ps, bp = pN.next()
                    S.op("pe", lambda e: e.matmul(ps[:, 0:n], lhsT=prot[:], rhs=qn[:, 0:n], start=True, stop=True), reads=[bqn, bcst], writes=[bp])
                    S.op("dve", lambda e: e.tensor_tensor(out=t1[:, 0:n], in0=qn[:, 0:n], in1=cosT[:, t0:t0 + n], op=ALU.mult), reads=[bqn, bcst], writes=[bt1])
                    S.op("dve", lambda e: e.tensor_tensor(out=t2[:, 0:n], in0=ps[:, 0:n], in1=sinT[:, t0:t0 + n], op=ALU.mult), reads=[bp, bcst], writes=[bt2])
                    S.op("pool", lambda e: e.tensor_tensor(out=dst_ap_of(t0, n), in0=t1[:, 0:n], in1=t2[:, 0:n], op=ALU.add), reads=[bt1, bt2], writes=[bdst])
                else:
                    S.op("act", lambda e: e.copy(out=dst_ap_of(t0, n), in_=qn[:, 0:n]), reads=[bqn], writes=[bdst])

        for kv in range(2):
            raw, braw = raws.next()
            S.dma("sp", raw[:], E.UF[40 + kv, :, :], reads=[E.bUF[40 + kv]], writes=[braw])
            normrope(raw, braw, kg, lambda t0, n: kT[:, kv, t0:t0 + n], bkT)
        for hd in range(8):
            kv = hd // 4
            raw, braw = raws.next()
            S.dma("sp", raw[:], E.UF[32 + hd, :, :], reads=[E.bUF[32 + hd]], writes=[braw])
            gate, bg = gates.next()
            S.dma("sp", gate[:], E.UF[42 + hd, :, :], reads=[E.bUF[42 + hd]], writes=[bg])
            S.op("act", lambda e: e.activation(out=gate[:], in_=gate[:], func=ACT.Silu), reads=[bg], writes=[bg])
            qT, bq = qTs.next()
            normrope(raw, braw, qg, lambda t0, n: qT[:, t0:t0 + n], bq)
            for ch in range(9 if need_ctx else 8):
                n = 512 if ch < 8 else 256
                kbl = list(range(34)) if ch < 8 else [32, 33]
                attend(qT, bq, ch * 512, n, kbl, lambda kb: kT[:, kv, kb * 128:(kb + 1) * 128], bkT,
                       lambda kb: vtok[:, kb, kv * 128:(kv + 1) * 128], bvt, gate, bg, 8 + hd, pS, pO, pZ, pTs, wk)
        P.close()

    def lru(l, hh):
        j = l // 2
        need_ctx = l < 3
        lp = E.lru[(j, hh)]
        P = Phase(nc, S)
        cw = P.sb("cw", [128, 8, 4]); cb = P.sb("cb", [128, 8]); wa = P.sb("wa", [128, 2, 8, 128]); wx = P.sb("wx", [128, 2, 8, 128])
        ba = P.sb("ba", [128, 2, 8]); bx = P.sb("bx", [128, 2, 8]); lam = P.sb("lam", [128, 2, 8]); cl = P.sb("cl", [128, 2, 8]); cl2 = P.sb("cl2", [128, 2, 8])
        bpar = Buf()
        for dst, src in ((cw, lp["cw"]), (cb, lp["cb"]), (wa, lp["wa"]), (wx, lp["wx"]), (ba, lp["ba"]), (bx, lp["bx"]), (lam, lp["lam"])):
            S.dma("sp", dst[:], src, writes=[bpar])
        S.op("act", lambda e: e.activation(out=lam[:], in_=lam[:], func=ACT.Exp, scale=-1.0), reads=[bpar], writes=[bpar])
        S.op("act", lambda e: e.activation(out=lam[:], in_=lam[:], func=ACT.Ln, bias=1.0, scale=1.0), reads=[bpar], writes=[bpar])
        S.op("dve", lambda e: e.tensor_scalar(out=cl[:], in0=lam[:], scalar1=-8.0, scalar2=None, op0=ALU.mult), reads=[bpar], writes=[bpar])
        S.op("dve", lambda e: e.tensor_scalar(out=cl2[:], in0=lam[:], scalar1=-16.0, scalar2=None, op0=ALU.mult), reads=[bpar], writes=[bpar])
        us = Rot([(P.sb("u", [128, NT]), Buf()) for _ in range(2)])
        xr = P.sb("xr", [128, NT]); r = P.sb("r", [128, NT]); ig = P.sb("ig", [128, NT]); a = P.sb("a", [128, NT]); bb = P.sb("bb", [128, NT])
        hA = P.sb("hA", [128, NT]); hB = P.sb("hB", [128, NT]); yo = P.sb("yo", [128, NT], BF16)
        bxr, br, bi, ba_, bbb, bhA, bhB, byo = Buf(), Buf(), Buf(), Buf(), Buf(), Buf(), Buf(), Buf()
        pss = Rot([(P.ps("pl"), PBuf()) for _ in range(4)])
        for cbk in range(8):
            u, bu = us.next()
            S.dma("sp", u[:], E.UF[cbk, :, :], reads=[E.bUF[cbk]], writes=[bu])
            S.op("act", lambda e: e.activation(out=xr[:], in_=u[:], func=ACT.Identity, scale=cw[:, cbk, 2:3], bias=cb[:, cbk:cbk + 1]), reads=[bu, bpar], writes=[bxr])
            for lo, hi in ((0, L), (L, NT)):
                S.op("dve", lambda e: e.scalar_tensor_tensor(out=xr[:, lo + 1:hi], in0=u[:, lo:hi - 1], scalar=cw[:, cbk, 1:2], in1=xr[:, lo + 1:hi], op0=ALU.mult, op1=ALU.add),
                     reads=[bu, bxr, bpar], writes=[bxr])
                S.op("dve", lambda e: e.scalar_tensor_tensor(out=xr[:, lo + 2:hi], in0=u[:, lo:hi - 2], scalar=cw[:, cbk, 0:1], in1=xr[:, lo + 2:hi], op0=ALU.mult, op1=ALU.add),
                     reads=[bu, bxr, bpar], writes=[bxr])
                S.op("dve", lambda e: e.scalar_tensor_tensor(out=xr[:, lo:hi - 1], in0=u[:, lo + 1:hi], scalar=cw[:, cbk, 3:4], in1=xr[:, lo:hi - 1], op0=ALU.mult, op1=ALU.add),
                     reads=[bu, bxr, bpar], writes=[bxr])
            for d in range(2):
                for ch in range(9):
                    n = 512 if ch < 8 else 256
                    t0 = ch * 512
                    ps, bp = pss.next()
                    S.op("pe", lambda e: e.matmul(ps[:, 0:n], lhsT=wa[:, d, cbk, :], rhs=xr[:, t0:t0 + n], start=True, stop=True), reads=[bpar, bxr], writes=[bp])
                    S.op("act", lambda e: e.activation(out=r[:, t0:t0 + n], in_=ps[:, 0:n], func=ACT.Sigmoid, bias=ba[:, d, cbk:cbk + 1], scale=1.0), reads=[bp, bpar], writes=[br])
                    ps, bp = pss.next()
                    S.op("pe", lambda e: e.matmul(ps[:, 0:n], lhsT=wx[:, d, cbk, :], rhs=xr[:, t0:t0 + n], start=True, stop=True), reads=[bpar, bxr], writes=[bp])
                    S.op("act", lambda e: e.activation(out=ig[:, t0:t0 + n], in_=ps[:, 0:n], func=ACT.Sigmoid, bias=bx[:, d, cbk:cbk + 1], scale=1.0), reads=[bp, bpar], writes=[bi])
                S.op("act", lambda e: e.activation(out=a[:], in_=r[:], func=ACT.Exp, scale=cl[:, d, cbk:cbk + 1]), reads=[br, bpar], writes=[ba_])
                S.op("act", lambda e: e.activation(out=r[:], in_=r[:], func=ACT.Exp, scale=cl2[:, d, cbk:cbk + 1]), reads=[br, bpar], writes=[br])
                S.op("act", lambda e: e.activation(out=r[:], in_=r[:], func=ACT.Sqrt, scale=-1.0, bias=1.0), reads=[br], writes=[br])
                S.op("pool", lambda e: e.tensor_tensor(out=ig[:], in0=ig[:], in1=xr[:], op=ALU.mult), reads=[bi, bxr], writes=[bi])
                S.op("pool", lambda e: e.tensor_tensor(out=bb[:], in0=ig[:], in1=r[:], op=ALU.mult), reads=[bi, br], writes=[bbb])
                if d == 0:
                    S.op("dve", lambda e: e.tensor_tensor_scan(out=hA[:, L:NT], data0=a[:, L:NT], data1=bb[:, L:NT], initial=0.0, op0=ALU.mult, op1=ALU.add),
                         reads=[ba_, bbb], writes=[bhA])
                    S.op("dve", lambda e: e.tensor_tensor_scan(out=hA[:, 0:L], data0=a[:, 0:L], data1=bb[:, 0:L], initial=hA[:, NT - 1:NT], op0=ALU.mult, op1=ALU.add),
                         reads=[ba_, bbb, bhA], writes=[bhA])
                else:
                    S.op("dve", lambda e: e.tensor_tensor_scan(out=hB[:, L:NT][:, ::-1], data0=a[:, L:NT][:, ::-1], data1=bb[:, L:NT][:, ::-1], initial=0.0,
                                                               op0=ALU.mult, op1=ALU.add), reads=[ba_, bbb], writes=[bhB])
                    S.op("dve", lambda e: e.tensor_tensor_scan(out=hB[:, 0:L][:, ::-1], data0=a[:, 0:L][:, ::-1], data1=bb[:, 0:L][:, ::-1], initial=hB[:, L:L + 1],
                                                               op0=ALU.mult, op1=ALU.add), reads=[ba_, bbb, bhB], writes=[bhB])
            g, bg = us.next()
            S.dma("sp", g[:], E.UF[8 + cbk, :, :], reads=[E.bUF[8 + cbk]], writes=[bg])
            S.op("act", lambda e: e.activation(out=g[:], in_=g[:], func=ACT.Silu), reads=[bg], writes=[bg])
            S.op("pool", lambda e: e.tensor_tensor(out=hA[:], in0=hA[:], in1=hB[:], op=ALU.add), reads=[bhA, bhB], writes=[bhA])
            S.op("dve", lambda e: e.tensor_tensor(out=yo[:], in0=hA[:], in1=g[:], op=ALU.mult), reads=[bhA, bg], writes=[byo])
            S.dma("sp", E.YT[cbk, :, :], yo[:], reads=[byo], writes=[E.bYT[cbk]])
        P.close()

    def na(l, hh):
        j = l // 2
        need_ctx = l < 3
        lp = E.lru[(j, hh)]
        P = Phase(nc, S)
        raws = Rot([(P.sb("raw", [128, NT]), Buf()) for _ in range(2)])
        gates = Rot([(P.sb("gate", [128, NT]), Buf()) for _ in range(2)])
        qTs = Rot([(P.sb("qT", [128, NT], BF16), Buf()) for _ in range(2)])
        kTs = Rot([(P.sb("kT", [128, NT], BF16), Buf()) for _ in range(2)])
        vts = Rot([(P.sb("vt", [128, 34, 128], BF16), Buf()) for _ in range(2)])
        Ebs = Rot([(P.sb("Eb", [128, 20, 512], BF16), Buf()) for _ in range(2)])
        stg = Rot([(P.sb("bst", [128, 512]), Buf()) for _ in range(2)])
        pS = Rot([(P.ps("pS"), PBuf()) for _ in range(3)])
        pO = Rot([(P.ps("pO"), PBuf()) for _ in range(2)])
        pZ = Rot([(P.ps("pZ"), PBuf()) for _ in range(2)])
        pTs = Rot([(P.sb("pT", [128, 512], BF16), Buf()) for _ in range(3)])
        wk = attn_work(P)
        for hd in range(8):
            raw, braw = raws.next()
            S.dma("sp", raw[:], E.UF[16 + hd, :, :], reads=[E.bUF[16 + hd]], writes=[braw])
            qT, bq = qTs.next()
            S.op("act", lambda e: e.copy(out=qT[:], in_=raw[:]), reads=[braw], writes=[bq])
            raw, braw = raws.next()
            S.dma("sp", raw[:], E.UF[24 + hd, :, :], reads=[E.bUF[24 + hd]], writes=[braw])
            kT, bk = kTs.next()
            S.op("dve", lambda e: e.tensor_copy(out=kT[:], in_=raw[:]), reads=[braw], writes=[bk])
            gate, bg = gates.next()
            S.dma("sp", gate[:], E.UF[32 + hd, :, :], reads=[E.bUF[32 + hd]], writes=[bg])
            S.op("act", lambda e: e.activation(out=gate[:], in_=gate[:], func=ACT.Silu), reads=[bg], writes=[bg])
            vt, bv = vts.next()
            S.dma("sp", vt[:], E.VT[:, :, hd * 128:(hd + 1) * 128].rearrange("a p c -> p a c"), reads=[E.bVT], writes=[bv])
            Eb, bE = Ebs.next()
            for ti in range(20):
                st_, bst = stg.next()
                S.dma("sp", st_[:], lp["nab"][hd, ti, :, :], writes=[bst])
                S.op("act", lambda e: e.activation(out=Eb[:, ti, :], in_=st_[:], func=ACT.Exp), reads=[bst], writes=[bE])
            for i in range(8):
                if i == 0:
                    wkb = [(kb, kb) for kb in range(6)]
                elif i == 7:
                    wkb = [(26 + jj, 14 + jj) for jj in range(6)]
                else:
                    wkb = [(4 * i - 2 + jj, 6 + jj) for jj in range(8)]
                tmap = dict(wkb)
                kbl = [kb for kb, _ in wkb] + [32, 33]
                attend(qT, bq, i * 512, 512, kbl, lambda kb: kT[:, kb * 128:(kb + 1) * 128], bk, lambda kb: vt[:, kb, :], bv, gate, bg, 8 + hd,
                       pS, pO, pZ, pTs, wk, emul=lambda kb: ((Eb[:, tmap[kb], :], bE) if kb in tmap else None))
            if need_ctx:
                attend(qT, bq, L, 256, [32, 33], lambda kb: kT[:, kb * 128:(kb + 1) * 128], bk, lambda kb: vt[:, kb, :], bv, gate, bg, 8 + hd,
                       pS, pO, pZ, pTs, wk)
        P.close()

    def even(l, hh):
        if "hy" in E.cfg.get("mix", ("hy", "gqa")):
            hyena(l, hh)
        if "gqa" in E.cfg.get("mix", ("hy", "gqa")):
            gqa(l, hh)

    def odd(l, hh):
        if "lru" in E.cfg.get("mix", ("lru", "na")):
            lru(l, hh)
        if "na" in E.cfg.get("mix", ("lru", "na")):
            na(l, hh)

    return {"even": even, "odd": odd}


def _bf16(a):
    return np.ascontiguousarray(a.astype(ml_dtypes.bfloat16))


_CONST = {}


def host_constants():
    if _CONST:
        return _CONST
    f32 = np.float32
    c = {}
    for name, Lq in (("L", L), ("C", LC)):
        t = np.linspace(0.0, 1.0, Lq, dtype=f32)[:, None]
        bands = 16
        w = (2.0 * math.pi * np.arange(Lq, dtype=f32)[:, None] / Lq).astype(f32)
        fr = np.linspace(1e-4, bands - 1, bands, dtype=f32)[None, :]
        feats = np.concatenate([t, np.cos(fr * w), -np.sin(fr * w)], axis=-1).astype(f32)
        c["feats" + name] = np.ascontiguousarray(feats.T)
        c["tnorm" + name] = np.ascontiguousarray(np.broadcast_to(t[:, 0][None, :], (128, Lq))).astype(f32)
        N = 2 * Lq
        idx = (2 * np.arange(Lq, dtype=np.int64) + 1)
        m = (idx[:, None] * idx[None, :]) % (4 * N)
        ang = (2.0 * math.pi / (4 * N)) * m.astype(np.float64)
        nb = Lq // 128
        c["Ctab" + name] = _bf16(np.cos(ang).astype(f32)).reshape(nb, 128, Lq)
        c["Stab" + name] = _bf16(np.sin(ang).astype(f32)).reshape(nb, 128, Lq)
        phi = math.pi * (np.arange(Lq, dtype=np.float64) + 0.5) / N
        cp = (2.0 / N) * np.cos(phi)
        sp_ = (2.0 / N) * np.sin(phi)
        ph = np.stack([cp, sp_, -sp_], axis=0).astype(f32)
        c["phi" + name] = np.ascontiguousarray(ph.reshape(3, nb, 128).transpose(2, 0, 1))
    pos = np.arange(L)
    row = (pos // 64).astype(f32)
    col = (pos % 64).astype(f32)
    n = 32
    inv = (10000.0 ** (-np.arange(n, dtype=f32) / n)).astype(f32)
    ang = np.concatenate([row[:, None] * inv, col[:, None] * inv], axis=-1).astype(f32)
    cosT = np.zeros((128, L), f32)
    sinT = np.zeros((128, L), f32)
    prot = np.zeros((128, 128), f32)
    for a in range(2):
        for pr in range(2):
            for i in range(n):
                dh = a * 64 + pr * 32 + i
                cosT[dh] = np.cos(ang[:, a * 32 + i])
                sinT[dh] = np.sin(ang[:, a * 32 + i])
        for i in range(n):
            prot[a * 64 + i, a * 64 + 32 + i] = -1.0
            prot[a * 64 + 32 + i, a * 64 + i] = 1.0
    c["ropeC"] = cosT
    c["ropeS"] = sinT
    c["protT"] = np.ascontiguousarray(prot.T)
    deltas = np.linspace(abs(math.log(1e-2) / 1.5), abs(math.log(1e-2) / 0.3), 2048, dtype=f32)
    c["deltas"] = deltas
    _CONST.update(c)
    return c


def na_bias_tiles(rpb_heads):
    key_p = np.arange(128)
    krl = key_p // 64
    kc = key_p % 64
    qf = np.arange(512)
    qrl = qf // 64
    qc = qf % 64
    out = np.full((8, 20, 128, 512), -30000.0, np.float32)
    tiles = [(0, j, j) for j in range(6)] + [(3, j, 4 * 3 - 2 + j) for j in range(8)] + [(7, j, 26 + j) for j in range(6)]
    for ti, (i, j, kb) in enumerate(tiles):
        kr = 2 * kb + krl
        qr = 8 * i + qrl
        r0 = np.clip(qr - 4, 0, 56)
        c0 = np.clip(qc - 8, 0, 48)
        valid = ((kr[:, None] >= r0[None, :]) & (kr[:, None] < r0[None, :] + 8) &
                 (kc[:, None] >= c0[None, :]) & (kc[:, None] < c0[None, :] + 16))
        dr = np.clip(kr[:, None] - qr[None, :] + 7, 0, 14)
        dc = np.clip(kc[:, None] - qc[None, :] + 15, 0, 30)
        g = rpb_heads[:, dr, dc]
        out[:, ti] = np.where(valid[None], g, np.float32(-30000.0))
    return out


def prep_params(inp, core, nl=4):
    C = host_constants()
    f32 = np.float32
    m = {}
    m["gn"] = np.ascontiguousarray(inp["norm_g"].reshape(4, DB, 128).transpose(2, 0, 1))
    m["gfin"] = np.ascontiguousarray(inp["final_g"].reshape(DB, 128).T)
    for j in range((nl + 1) // 2):
        cw = inp["hy_conv_w"][j]
        cb = inp["hy_conv_b"][j]
        m[f"hyw1{j}"] = np.ascontiguousarray(inp["hy_w1"][j])
        m[f"hyb1{j}"] = np.ascontiguousarray(inp["hy_b1"][j][:, None])
        m[f"hyfr{j}"] = np.ascontiguousarray(inp["hy_freq"][j][:, None])
        m[f"hyw2{j}"] = np.ascontiguousarray(inp["hy_w2"][j])
        m[f"hyb2{j}"] = np.ascontiguousarray(inp["hy_b2"][j][:, None])
        m[f"attqg{j}"] = np.ascontiguousarray(inp["att_q_g"][j][:, None])
        m[f"attkg{j}"] = np.ascontiguousarray(inp["att_k_g"][j][:, None])
        w3 = inp["hy_w3"][j]
        for h in range(2):
            cwc = np.zeros((128, 3, 8, 3), f32)
            cbc = np.zeros((128, 3, 8), f32)
            for q in range(3):
                s0 = q * 2048 + h * 1024
                cwc[:, q] = cw[:, s0:s0 + 1024].reshape(3, 8, 128).transpose(2, 1, 0)
                cbc[:, q] = cb[s0:s0 + 1024].reshape(8, 128).T
            m[f"hycw{j}_{h}"] = cwc
            m[f"hycb{j}_{h}"] = cbc
            m[f"hyskip{j}_{h}"] = np.ascontiguousarray(inp["hy_skip"][j][h * 1024:(h + 1) * 1024].reshape(8, 128).T)
            m[f"hyw3{j}_{h}"] = np.ascontiguousarray(np.stack([w3[:, h * 1024:(h + 1) * 1024], w3[:, 2048 + h * 1024:2048 + (h + 1) * 1024]], axis=1))
    for j in range(nl // 2):
        for h in range(2):
            sl = slice(h * 1024, (h + 1) * 1024)
            m[f"lrucw{j}_{h}"] = np.ascontiguousarray(inp["lru_conv_w"][j][:, sl].reshape(4, 8, 128).transpose(2, 1, 0))
            m[f"lrucb{j}_{h}"] = np.ascontiguousarray(inp["lru_conv_b"][j][sl].reshape(8, 128).T)
            for nm, key in (("wa", "lru_wa"), ("wx", "lru_wx")):
                wgt = inp[key][j][:, h * 8:(h + 1) * 8]
                m[f"lru{nm}{j}_{h}"] = np.ascontiguousarray(wgt.transpose(2, 0, 1, 3))
            for nm, key in (("ba", "lru_ba"), ("bx", "lru_bx"), ("lam", "lru_lambda")):
                v = inp[key][j][:, sl]
                m[f"lru{nm}{j}_{h}"] = np.ascontiguousarray(v.reshape(2, 8, 128).transpose(2, 0, 1))
            m[f"nab{j}_{h}"] = na_bias_tiles(inp["na_rpb"][j][h * 8:(h + 1) * 8])
    for k in ("featsL", "featsC", "tnormL", "tnormC", "CtabL", "StabL", "CtabC", "StabC", "phiL", "phiC", "ropeC", "ropeS", "protT"):
        m[k] = C[k]
    for h in range(2):
        m[f"negdelta_{h}"] = np.ascontiguousarray(-C["deltas"][h * 1024:(h + 1) * 1024].reshape(8, 128).T)
    return m


_SHARED = {}


def prep_core(inp, core, nl=4):
    b, hme = core // NRANK, core % NRANK
    f32 = np.float32
    m = {}
    xt = np.concatenate([inp["x"][b].T, inp["ctx"][b].T], axis=1)
    m["xT"] = np.ascontiguousarray(xt.reshape(DB, 128, NT))
    sc = np.concatenate([inp["c"], inp["c_ctx"][None, :]], axis=0)
    m["scin"] = np.ascontiguousarray(sc.reshape(5, DB, 128).transpose(2, 1, 0))
    oh = np.zeros((128, 5), f32)
    oh[:, b] = 1.0
    m["onehot"] = oh
    ncol = NMB * 128
    cols = slice(hme * ncol, (hme + 1) * ncol)
    m["wmod"] = np.ascontiguousarray(np.stack([inp["w_mod"][l][:, cols].reshape(DB, 128, ncol).transpose(1, 0, 2) for l in range(4)]))
    m["bmod"] = np.ascontiguousarray(np.stack([inp["b_mod"][l][cols].reshape(NMB, 128).T for l in range(4)], axis=1))
    if "w" not in _SHARED:
        w = {}
        for l in range(nl):
            j = l // 2
            for h in range(2):
                if l % 2 == 0:
                    w_in, w_out = inp["ev_w_in"][j], inp["ev_w_out"][j]
                    blocks = even_fm_cols(h)
                    v0, vn = even_tm_cols(h)
                else:
                    w_in, w_out = inp["od_w_in"][j], inp["od_w_out"][j]
                    blocks = odd_fm_cols(h)
                    v0, vn = odd_tm_cols(h)
                ncb = len(blocks)
                colidx = np.concatenate([np.arange(s0, s0 + 128) for s0 in blocks])
                wsel = w_in[:, colidx]
                w[f"wfm{l}_{h}"] = np.ascontiguousarray(wsel.reshape(DB, 128, ncb, 128).transpose(2, 1, 0, 3))
                w[f"wtm{l}_{h}"] = np.ascontiguousarray(w_in[:, v0:v0 + vn].reshape(DB, 128, vn).transpose(1, 0, 2))
                rows = np.concatenate([np.arange(h * 1024, h * 1024 + 1024), np.arange(2048 + h * 1024, 2048 + h * 1024 + 1024)])
                wo = w_out[rows]
                w[f"wout{l}_{h}"] = np.ascontiguousarray(wo.reshape(16, 128, DB, 128).transpose(2, 1, 0, 3))
        w.update(prep_params(inp, core, nl))
        _SHARED["w"] = w
    for k, v in _SHARED["w"].items():
        if k.endswith("_0") or k.endswith("_1"):
            if int(k[-1]) == hme:
                m[k[:-2] + "_0"] = v
        else:
            m[k] = v
    return m


def kernel(**inputs):
    inp = {k: np.asarray(v) for k, v in inputs.items()}
    nc, S = build({})
    in_maps = [prep_core(inp, c) for c in range(NCORES)]
    res = run_bass_kernel_spmd(nc, in_maps, core_ids=list(range(NCORES)))
    out = np.empty((4, L, D), np.float32)
    for b in range(4):
        o = res.results[NRANK * b]["outT"]
        out[b] = o.reshape(D, L).T
    _SHARED.clear()
    return out
```
